# Optimizing a Trainium2 kernel written in Bass

```python
import jax, jax.numpy as jnp
from jax import lax
import numpy as np

D_MODEL = 1024
BATCH = 4
SEQ = 8192
DEPTH = 2
DEC_BATCH = 32
DEC_SEQ = 2048
PAST_LEN = 128

GRID_W = 64
N_HEADS = 8
HEAD_DIM = 64
ATTN_WIDTH = N_HEADS * HEAD_DIM
CONV_WIDTH = D_MODEL - ATTN_WIDTH
WIN_H_MAX = 8
WIN_W = 16
Q_COL_BLOCK = 16
K_COL_BLOCK = Q_COL_BLOCK + WIN_W
D_FF = 2816
PROJ_WIDTH = 3 * ATTN_WIDTH + 3 * CONV_WIDTH + 2 * D_MODEL
DEEPNORM_ALPHA = (2 * DEPTH) ** 0.25
DEEPNORM_BETA = (8 * DEPTH) ** -0.25
LN_EPS = 1e-5

kernel_name = 'hybrid_natten_shortconv_deepnorm_encoder'


def layer_norm(x, g, b):
    xf = x.astype(jnp.float32)
    mu = jnp.mean(xf, axis=-1, keepdims=True)
    var = jnp.mean(jnp.square(xf - mu), axis=-1, keepdims=True)
    y = (xf - mu) * lax.rsqrt(var + LN_EPS)
    return (y * g.astype(jnp.float32) + b.astype(jnp.float32)).astype(x.dtype)


def dwconv3(x, w, b):
    xp = jnp.pad(x, ((0, 0), (1, 1), (0, 0)))
    return xp[:, :-2] * w[0] + xp[:, 1:-1] * w[1] + xp[:, 2:] * w[2] + b


def neighbourhood_attention(q, k, v, rpb):
    B, T, H, dh = q.shape
    rows = T // GRID_W
    kh = min(WIN_H_MAX, rows)
    scale = dh ** -0.5
    qg = q.reshape(B, rows, GRID_W, H, dh)
    kg = k.reshape(B, rows, GRID_W, H, dh)
    vg = v.reshape(B, rows, GRID_W, H, dh)
    row_starts = jnp.clip(jnp.arange(rows) - kh // 2, 0, rows - kh)

    def one_row(args):
        i, rs = args
        q_i = lax.dynamic_index_in_dim(qg, i, axis=1, keepdims=False)
        k_i = lax.dynamic_slice_in_dim(kg, rs, kh, axis=1)
        v_i = lax.dynamic_slice_in_dim(vg, rs, kh, axis=1)
        dr_idx = rs + jnp.arange(kh) - i + (WIN_H_MAX - 1)
        outs = []
        for c in range(GRID_W // Q_COL_BLOCK):
            q0 = c * Q_COL_BLOCK
            cs = min(max(q0 - WIN_W // 2, 0), GRID_W - K_COL_BLOCK)
            qj = jnp.arange(q0, q0 + Q_COL_BLOCK)
            kj = jnp.arange(cs, cs + K_COL_BLOCK)
            js = jnp.clip(qj - WIN_W // 2, 0, GRID_W - WIN_W)
            valid = (kj[None, :] >= js[:, None]) & (kj[None, :] < js[:, None] + WIN_W)
            dc_idx = jnp.clip(kj[None, :] - qj[:, None] + (WIN_W - 1), 0, 2 * WIN_W - 2)
            bias = rpb[:, dr_idx[None, :, None], dc_idx[:, None, :]]
            qc = q_i[:, q0:q0 + Q_COL_BLOCK]
            kc = k_i[:, :, cs:cs + K_COL_BLOCK]
            vc = v_i[:, :, cs:cs + K_COL_BLOCK]
            s = jnp.einsum('bqhd,brkhd->bhqrk', qc, kc).astype(jnp.float32) * scale
            s = s + bias.astype(jnp.float32)
            s = jnp.where(valid[:, None, :], s, -jnp.inf)
            p = jax.nn.softmax(s.reshape(B, H, Q_COL_BLOCK, kh * K_COL_BLOCK), axis=-1)
            p = p.astype(v.dtype).reshape(B, H, Q_COL_BLOCK, kh, K_COL_BLOCK)
            outs.append(jnp.einsum('bhqrk,brkhd->bqhd', p, vc))
        return jnp.concatenate(outs, axis=1)

    o = lax.map(one_row, (jnp.arange(rows), row_starts))
    return jnp.moveaxis(o, 0, 1).reshape(B, T, H * dh)


def encoder_layer(x, w_in, b_in, attn_rpb, sc_conv_w, sc_conv_b, w_br_attn, w_br_conv,
                  w_o, b_o, ln1_g, ln1_b, ffn_w_up, ffn_b_up, ffn_conv_w, ffn_conv_b,
                  ffn_w_down, ffn_b_down, ln2_g, ln2_b):
    B, T, _ = x.shape
    proj = x @ w_in + b_in
    splits = np.cumsum([ATTN_WIDTH, ATTN_WIDTH, ATTN_WIDTH, CONV_WIDTH, CONV_WIDTH,
                        CONV_WIDTH, D_MODEL]).tolist()
    q, k, v, u, gb_in, gc_in, gate_a, gate_c = jnp.split(proj, splits, axis=-1)
    ya = neighbourhood_attention(q.reshape(B, T, N_HEADS, HEAD_DIM),
                                 k.reshape(B, T, N_HEADS, HEAD_DIM),
                                 v.reshape(B, T, N_HEADS, HEAD_DIM), attn_rpb)
    yc = gb_in * dwconv3(gc_in * u, sc_conv_w, sc_conv_b)
    merged = jax.nn.sigmoid(gate_a) * (ya @ w_br_attn) + jax.nn.sigmoid(gate_c) * (yc @ w_br_conv)
    mix = merged @ w_o + b_o
    x = layer_norm(DEEPNORM_ALPHA * x + mix, ln1_g, ln1_b)
    h = dwconv3(x @ ffn_w_up + ffn_b_up, ffn_conv_w, ffn_conv_b)
    h_gate, h_val = jnp.split(h, 2, axis=-1)
    f = (jax.nn.gelu(h_gate) * h_val) @ ffn_w_down + ffn_b_down
    return layer_norm(DEEPNORM_ALPHA * x + f, ln2_g, ln2_b)


def trunk(x, w_in, b_in, attn_rpb, sc_conv_w, sc_conv_b, w_br_attn, w_br_conv, w_o, b_o,
          ln1_g, ln1_b, ffn_w_up, ffn_b_up, ffn_conv_w, ffn_conv_b, ffn_w_down, ffn_b_down,
          ln2_g, ln2_b):
    for l in range(DEPTH):
        x = encoder_layer(x, w_in[l], b_in[l], attn_rpb[l], sc_conv_w[l], sc_conv_b[l],
                          w_br_attn[l], w_br_conv[l], w_o[l], b_o[l], ln1_g[l], ln1_b[l],
                          ffn_w_up[l], ffn_b_up[l], ffn_conv_w[l], ffn_conv_b[l],
                          ffn_w_down[l], ffn_b_down[l], ln2_g[l], ln2_b[l])
    return x


def setup_inputs(seed: int = 0) -> dict:
    key = jax.random.key(seed)
    ks = jax.random.split(key, 22)
    nrm = lambda k, s: jax.random.normal(k, s, dtype=jnp.float32)
    beta = DEEPNORM_BETA
    col_scale = jnp.concatenate([
        jnp.ones((2 * ATTN_WIDTH,), jnp.float32),
        jnp.full((ATTN_WIDTH,), beta, jnp.float32),
        jnp.ones((3 * CONV_WIDTH + 2 * D_MODEL,), jnp.float32)])
    return {
        'x_prompt': nrm(ks[0], (BATCH, SEQ, D_MODEL)),
        'x_sample': nrm(ks[1], (DEC_BATCH, DEC_SEQ, D_MODEL)),
        'w_in': nrm(ks[2], (DEPTH, D_MODEL, PROJ_WIDTH)) * (D_MODEL ** -0.5) * col_scale,
        'b_in': 0.02 * nrm(ks[3], (DEPTH, PROJ_WIDTH)),
        'attn_rpb': 0.02 * nrm(ks[4], (DEPTH, N_HEADS, 2 * WIN_H_MAX - 1, 2 * WIN_W - 1)),
        'sc_conv_w': nrm(ks[5], (DEPTH, 3, CONV_WIDTH)) * (3 ** -0.5),
        'sc_conv_b': 0.02 * nrm(ks[6], (DEPTH, CONV_WIDTH)),
        'w_br_attn': nrm(ks[7], (DEPTH, ATTN_WIDTH, D_MODEL)) * (ATTN_WIDTH ** -0.5) * beta,
        'w_br_conv': nrm(ks[8], (DEPTH, CONV_WIDTH, D_MODEL)) * (CONV_WIDTH ** -0.5) * beta,
        'w_o': nrm(ks[9], (DEPTH, D_MODEL, D_MODEL)) * (D_MODEL ** -0.5) * beta,
        'b_o': 0.02 * nrm(ks[10], (DEPTH, D_MODEL)),
        'ln1_g': 1.0 + 0.01 * nrm(ks[11], (DEPTH, D_MODEL)),
        'ln1_b': 0.01 * nrm(ks[12], (DEPTH, D_MODEL)),
        'ffn_w_up': nrm(ks[13], (DEPTH, D_MODEL, 2 * D_FF)) * (D_MODEL ** -0.5) * beta,
        'ffn_b_up': 0.02 * nrm(ks[14], (DEPTH, 2 * D_FF)),
        'ffn_conv_w': nrm(ks[15], (DEPTH, 3, 2 * D_FF)) * (3 ** -0.5),
        'ffn_conv_b': 0.02 * nrm(ks[16], (DEPTH, 2 * D_FF)),
        'ffn_w_down': nrm(ks[17], (DEPTH, D_FF, D_MODEL)) * (D_FF ** -0.5) * beta,
        'ffn_b_down': 0.02 * nrm(ks[18], (DEPTH, D_MODEL)),
        'ln2_g': 1.0 + 0.01 * nrm(ks[19], (DEPTH, D_MODEL)),
        'ln2_b': 0.01 * nrm(ks[20], (DEPTH, D_MODEL)),
    }


def reference(x_prompt, x_sample, w_in, b_in, attn_rpb, sc_conv_w, sc_conv_b, w_br_attn,
              w_br_conv, w_o, b_o, ln1_g, ln1_b, ffn_w_up, ffn_b_up, ffn_conv_w, ffn_conv_b,
              ffn_w_down, ffn_b_down, ln2_g, ln2_b):
    y_prompt = trunk(x_prompt, w_in, b_in, attn_rpb, sc_conv_w, sc_conv_b, w_br_attn,
                     w_br_conv, w_o, b_o, ln1_g, ln1_b, ffn_w_up, ffn_b_up, ffn_conv_w,
                     ffn_conv_b, ffn_w_down, ffn_b_down, ln2_g, ln2_b)
    y_sample = trunk(x_sample, w_in, b_in, attn_rpb, sc_conv_w, sc_conv_b, w_br_attn,
                     w_br_conv, w_o, b_o, ln1_g, ln1_b, ffn_w_up, ffn_b_up, ffn_conv_w,
                     ffn_conv_b, ffn_w_down, ffn_b_down, ln2_g, ln2_b)
    return (y_prompt, y_sample)
```

```python
import numpy as np
import ml_dtypes
from contextlib import ExitStack
import concourse.bass as bass
import concourse.mybir as mybir
from concourse.bass_utils import run_bass_kernel_spmd

F32 = mybir.dt.float32
BF16 = mybir.dt.bfloat16
AF = mybir.ActivationFunctionType
ALU = mybir.AluOpType

D = 1024
NH = 8
DH = 64
DFF = 2816
ALPHA = float(4 ** 0.25)
EPS = 1e-5
NEG = -30000.0
NPAR = 324
P_BIN, P_SCW, P_SCB, P_BO, P_G1, P_B1, P_BUP, P_FCW, P_FCB, P_BDN, P_G2, P_B2 = (
    0, 40, 52, 56, 64, 72, 80, 124, 256, 300, 308, 316)
NBM = 43

ENGS = ("pe", "act", "dve", "pool", "sp")


class Tl:
    __slots__ = ("name", "w", "rs", "rd", "sem", "cnt")

    def __init__(self, name):
        self.name = name
        self.w = None
        self.rs = {}
        self.rd = []
        self.sem = None
        self.cnt = 0


class Op:
    __slots__ = ("eng", "fn", "waits", "sig", "idx", "sigval", "dsem", "line")


class Sched:
    def __init__(self):
        self.q = {e: [] for e in ENGS}
        self.dma_tiles = []

    def add(self, eng, fn, reads=(), writes=(), dma=None, extra=()):
        op = Op()
        op.eng = eng
        op.fn = fn
        op.sig = False
        op.idx = len(self.q[eng])
        op.dsem = dma
        op.sigval = 0
        import sys as _s
        f = _s._getframe(1)
        ls = []
        while f is not None and len(ls) < 4:
            ls.append(f.f_lineno)
            f = f.f_back
        op.line = ls
        deps = list(extra)
        for t in reads:
            if t.w is not None:
                deps.append(t.w)
        for t in writes:
            if t.w is not None:
                deps.append(t.w)
            deps.extend(t.rs.values())
            deps.extend(t.rd)
        waits = []
        seen = set()
        for d in deps:
            if id(d) in seen:
                continue
            seen.add(id(d))
            if isinstance(d, Op):
                if d.eng == eng:
                    if eng in ("pe", "sp"):
                        continue
                    if op.idx - d.idx > 3:
                        continue
                d.sig = True
            waits.append(d)
        op.waits = waits
        if dma is not None:
            if dma.sem is None:
                dma.sem = True
                self.dma_tiles.append(dma)
            dma.cnt += 16
            ev = (dma, dma.cnt)
        else:
            ev = op
        for t in reads:
            if isinstance(ev, Op):
                t.rs[eng] = ev
            else:
                t.rd.append(ev)
        for t in writes:
            t.w = ev
            t.rs = {}
            t.rd = []
        self.q[eng].append(op)
        return ev

    def emit(self, nc, stack):
        esem = {}
        for e in ("pe", "act", "dve", "pool"):
            esem[e] = stack.enter_context(nc.semaphore("s_" + e))
        for t in self.dma_tiles:
            t.sem = stack.enter_context(nc.semaphore("d_" + t.name))
        for e in ENGS:
            c = 0
            for op in self.q[e]:
                if op.sig:
                    c += 1
                op.sigval = c
        q = self.q

        def run(name, eng):
            waited = {}
            for op in q[name]:
                for d in op.waits:
                    if isinstance(d, Op):
                        sem, val = esem[d.eng], d.sigval
                    else:
                        sem, val = d[0].sem, d[1]
                    k = id(sem)
                    if waited.get(k, 0) < val:
                        eng.wait_ge(sem, val)
                        waited[k] = val
                ins = op.fn(eng)
                if op.dsem is not None:
                    ins.then_inc(op.dsem.sem, 16)
                elif op.sig:
                    ins.then_inc(esem[name], 1)

        with nc.Block() as block:
            @block.tensor
            def _(e):
                run("pe", e)

            @block.scalar
            def _(e):
                run("act", e)

            @block.vector
            def _(e):
                run("dve", e)

            @block.gpsimd
            def _(e):
                run("pool", e)

            @block.sync
            def _(e):
                run("sp", e)


def split_windows(t0, t1, forced=()):
    cuts = sorted(set([t0, t1] + [f for f in forced if t0 < f < t1]))
    out = []
    for a, b in zip(cuts[:-1], cuts[1:]):
        n = b - a
        k = (n + 2) // 3
        base, rem = divmod(n, k)
        s = a
        for i in range(k):
            ln = base + (1 if i < rem else 0)
            out.append((s, s + ln))
            s += ln
    return out


def sample_kind():
    q = {}
    for t in range(16):
        if t == 0:
            q[t] = [(d, ("e", 5 + i)) for i, d in enumerate(range(0, 4))]
        elif t == 1:
            q[t] = [(d, ("e", 9 + i)) for i, d in enumerate(range(-1, 3))]
        elif t == 14:
            q[t] = [(d, ("e", 13 + i)) for i, d in enumerate(range(-2, 2))]
        elif t == 15:
            q[t] = [(d, ("e", 17 + i)) for i, d in enumerate(range(-3, 1))]
        else:
            q[t] = [(d, ("i", d + 2)) for d in range(-2, 3)]
    lay = dict(A=(0, 16), B=(0, 16), C=(0, 16), q=q, forced=(),
               bl={0: "zero"}, br={16: "zero"})
    return dict(NT=16, xoff=0, L=[lay, lay], out=(0, 16))


def prompt_kind(j):
    def qmap(b0, b1, a0, a1):
        q = {}
        for t in range(b0, b1):
            if j == 0 and t == 5:
                lst = [(d, ("e", 21 + i)) for i, d in enumerate(range(-2, 4))]
            elif j == 0 and t == 6:
                lst = [(d, ("e", 27 + i)) for i, d in enumerate(range(-2, 3))]
            elif j == 3 and t == 11:
                lst = [(d, ("e", 32 + i)) for i, d in enumerate(range(-2, 3))]
            elif j == 3 and t == 12:
                lst = [(d, ("e", 37 + i)) for i, d in enumerate(range(-3, 3))]
            else:
                lst = [(d, ("i", d + 2)) for d in range(-2, 3)]
            q[t] = [(d, s) for (d, s) in lst if a0 <= t + d < a1]
        return q
    bl = {5: "ftop"} if j == 0 else {}
    br = {13: "fbot"} if j == 3 else {}
    l0 = dict(A=(0, 17), B=(2, 16), C=(2, 15), q=qmap(2, 16, 0, 17), forced=(5, 13), bl=bl, br=br)
    l1 = dict(A=(2, 15), B=(4, 14), C=(5, 13), q=qmap(4, 14, 2, 15), forced=(5, 13), bl=bl, br=br)
    return dict(NT=17, xoff=2, L=[l0, l1], out=(5, 13))


class Builder:
    def __init__(self, NS, NP, dbg=False, stop=None):
        self.NS, self.NP = NS, NP
        self.stop = stop
        self.bstop = None
        if stop is not None and len(stop) == 3:
            self.bstop = int(stop[2])
            self.stop = stop[:2]
        self.nc = nc = bass.Bass("TRN2", target_bir_lowering=False)
        self.S = Sched()
        dt = nc.dram_tensor
        self.d = d = {}
        if NS:
            d["xs"] = dt("xs", [NS, 2048, D], F32, kind="ExternalInput").ap()
            d["ys"] = dt("ys", [NS, 2048, D], F32, kind="ExternalOutput").ap()
        if NP:
            d["xp"] = dt("xp", [NP, 17 * 128, D], F32, kind="ExternalInput").ap()
            d["yp"] = dt("yp", [NP, 1024, D], F32, kind="ExternalOutput").ap()
        self.wshapes = dict(win=(40, 1024), wba=(8, 512), wbc=(8, 512), wo=(8, 1024),
                            wup=(44, 1024), wdn=(8, 2816))
        for k, (n, f) in self.wshapes.items():
            d[k] = dt(k, [2, n, 128, f], F32, kind="ExternalInput").ap()
            d[k + "b"] = dt(k + "b", [2, n, 128, f], BF16, kind="Internal").ap()
        d["bm"] = dt("bm", [2, NBM, 128, 1024], F32, kind="ExternalInput").ap()
        d["eb"] = dt("eb", [2, NBM, 128, 1024], BF16, kind="Internal").ap()
        d["par"] = dt("par", [128, 2 * NPAR], F32, kind="ExternalInput").ap()
        d["bv"] = dt("bv", [128, 1024], F32, kind="ExternalInput").ap()
        d["flg"] = dt("flg", [128, 4], F32, kind="ExternalInput").ap()
        self.off = 0
        self.arena = nc.alloc_sbuf_tensor("arena", [128, 53100], F32)
        self.cap = 53100 * 4
        self.pbank = 0
        self.nbank = 5
        self.sbi = 0
        self.out_events = []

    def sb(self, shape, dtype, name=None):
        n = 1
        for s in shape:
            n *= s
        nbytes = n * (4 if dtype == F32 else 2)
        nbytes = (nbytes + 31) // 32 * 32
        assert self.off + nbytes <= self.cap, ("SBUF overflow", name, self.off + nbytes)
        w0 = self.off // 4
        ap = self.arena[:, w0:w0 + nbytes // 4]
        if dtype != F32:
            ap = ap.bitcast(BF16)
            ap = ap[:, 0:n]
        else:
            ap = ap[:, 0:n]
        self.off += nbytes
        if len(shape) == 2:
            ap = ap.rearrange("p (a b) -> p a b", a=shape[0])
        elif len(shape) == 3:
            ap = ap.rearrange("p (a b c) -> p a b c", a=shape[0], b=shape[1])
        return ap

    def ring(self, n, shape, dtype, name):
        return Ring([(self.sb(shape, dtype, name), Tl(f"{name}{i}")) for i in range(n)])

    def bank(self):
        self.pbank = (self.pbank + 1) % self.nbank
        b = self.pbank
        return self.ps[b], self.pst[b]

    def mm(self, out, pairs, reads, wt, flags=None):
        pairs = list(pairs)

        def fn(e):
            n = len(pairs)
            ins = None
            for i, (l, r) in enumerate(pairs):
                st, sp = (i == 0, i == n - 1) if flags is None else flags
                ins = e.matmul(out, lhsT=l, rhs=r, start=st, stop=sp, skip_group_check=True)
            return ins
        self.S.add("pe", fn, reads=reads, writes=[wt])

    def dma(self, out, in_, reads, writes, sem):
        return self.S.add("sp", lambda e: e.dma_start(out=out, in_=in_), reads=reads, writes=writes, dma=sem)

    def act(self, out, in_, func, reads, writes, bias=0.0, scale=1.0):
        self.S.add("act", lambda e: e.activation(out=out, in_=in_, func=func, bias=bias, scale=scale),
                   reads=reads, writes=writes)

    def ts(self, eng, out, in0, s1, s2, op0, op1, reads, writes):
        if s2 is None:
            self.S.add(eng, lambda e: e.tensor_scalar(out, in0, s1, None, op0), reads=reads, writes=writes)
        else:
            self.S.add(eng, lambda e: e.tensor_scalar(out, in0, s1, s2, op0, op1), reads=reads, writes=writes)

    def stt(self, eng, out, in0, sc, in1, op0, op1, reads, writes):
        self.S.add(eng, lambda e: e.scalar_tensor_tensor(out, in0, sc, in1, op0, op1), reads=reads, writes=writes)

    def fma(self, eng, out, in0, sc, in1, reads, writes):
        if eng == "dve":
            self.stt("dve", out, in0, sc, in1, ALU.mult, ALU.add, reads, writes)
        else:
            pa, pt = self.ptmp.next()
            shp = list(out.shape)
            pv = pa[:, :shp[-1]]
            self.ts(eng, pv, in0, sc, None, ALU.mult, None, reads, [pt])
            self.tt(eng, out, pv, in1, ALU.add, list(reads) + [pt], writes)

    def tt(self, eng, out, in0, in1, op, reads, writes):
        self.S.add(eng, lambda e: e.tensor_tensor(out, in0, in1, op), reads=reads, writes=writes)

    def cp(self, eng, out, in_, reads, writes):
        if eng == "act":
            self.S.add(eng, lambda e: e.copy(out, in_), reads=reads, writes=writes)
        else:
            self.S.add(eng, lambda e: e.tensor_copy(out, in_), reads=reads, writes=writes)

    def par(self, l, off, n=1):
        return self.PAR[:, l * NPAR + off: l * NPAR + off + n]

    def setup(self):
        nc = self.nc
        d = self.d
        self.ps = [nc.alloc_psum_tensor(f"ps{i}", [128, 512], F32) for i in range(7)]
        self.pst = [Tl(f"ps{i}") for i in range(7)]
        self.psb = nc.alloc_psum_tensor("psb", [128, 1024], BF16)
        self.psbt = [Tl("psb0"), Tl("psb1")]
        self.psbi = 0
        self.X = self.sb([8, 16 * 128], F32, "X")
        self.Xt = [Tl(f"X{t}") for t in range(16)]
        self.PAR = self.sb([2 * NPAR], F32, "PAR")
        self.BV = self.sb([2, 512], F32, "BV")
        self.FLG = self.sb([4], F32, "FLG")
        self.DER = self.sb([2, 5, 44], F32, "DER")
        self.ident = self.sb([128], F32, "ident")
        self.identb = self.sb([128], BF16, "identb")
        self.ones = self.sb([128], BF16, "ones")
        self.epsc = self.sb([1], F32, "epsc")
        self.barbuf = self.sb([1], F32, "barbuf")
        self.tC = Tl("const")
        self.xsave = self.sb([8, 1], F32, "xsave")
        self.xsavet = Tl("xsave")
        self.wring = self.ring(3, [8, 128], BF16, "w8")
        self.xw = self.ring(2, [8, 386], BF16, "xw")
        self.ptmp = self.ring(1, [386], F32, "ptmp")
        self.mark = self.off
        self.kT = self.sb([4, 17 * 128], BF16, "kT")
        self.kTt = [Tl(f"kT{t}") for t in range(17)]
        self.vA = self.sb([17, 8, 68], BF16, "vA")
        self.vAt = [Tl(f"vA{t}") for t in range(17)]
        self.Eint = self.sb([5, 1024], BF16, "Eint")
        self.Eintt = Tl("Eint")
        self.Eedge = self.sb([6, 1024], BF16, "Eedge")
        self.Eedget = Tl("Eedge")
        self.wv = Ring([(self.Eedge.rearrange("p a b -> p (a b)")[:, 0:4096].rearrange("p (k j) -> p k j", k=8), self.Eedget)])
        self.xstage = self.ring(1, [1024], F32, "xstage")
        self.qT = self.ring(1, [4, 2, 384], BF16, "qT")
        self.pexp = self.ring(4, [512], BF16, "pexp")
        self.Pm = self.ring(4, [512], BF16, "Pm")
        self.ya = self.ring(1, [512], BF16, "ya")
        self.rec = self.ring(2, [8], F32, "rec")
        self.yaT = self.ring(1, [4, 384], BF16, "yaT")
        self.cu = self.ring(1, [4, 386], F32, "cu")
        self.tu = self.ring(1, [386], F32, "tu")
        self.cv = self.ring(1, [384], F32, "cv")
        self.yc = self.ring(1, [4, 384], BF16, "yc")
        self.w4ring = self.ring(3, [4, 128], BF16, "w4")
        self.sg = self.ring(2, [384], F32, "sg")
        self.t12 = self.ring(2, [384], F32, "t12")
        self.mrg = self.ring(1, [8, 384], BF16, "mrg")
        self.ztmp = self.tu
        self.zb = self.ring(2, [384], BF16, "zb")
        self.zq = self.ring(2, [384], BF16, "zq")
        self.mean = Ring([self.t12.items[0]])
        self.rstd = Ring([self.t12.items[1]])
        self.msq = self.ring(1, [384], F32, "msq")
        endAB = self.off
        self.off = self.mark
        self.xwc = self.ring(2, [8, 386], BF16, "xwc")
        self.gT = self.ring(2, [22, 384], BF16, "gT")
        self.wdn = self.ring(2, [22, 128], BF16, "wdn")
        self.accg = self.ring(2, [384], F32, "accg")
        self.accv = self.ring(2, [384], F32, "accv")
        self.gl = self.ring(2, [384], F32, "gl")
        self.ostage = self.ring(2, [1024], F32, "ostage")
        self.ztmp2 = self.ring(1, [386], F32, "ztmp2")
        self.zb2 = self.ring(2, [384], BF16, "zb2")
        self.zq2 = self.ring(2, [384], BF16, "zq2")
        self.mean2 = self.ring(1, [384], F32, "mean2")
        self.rstd2 = self.ring(1, [384], F32, "rstd2")
        self.msq2 = self.ring(1, [384], F32, "msq2")
        self.stF = self.ring(4, [1024], F32, "stF")
        self.stB = self.ring(4, [1024], BF16, "stB")
        endC = self.off
        self.off = max(endAB, endC)
        print("SBUF bytes/partition: persistent", self.mark, "B", endAB - self.mark, "C", endC - self.mark, "cap", self.cap)
        self.ab_tiles = self.kTt + self.vAt + [self.Eintt, self.Eedget]
        self.c_tiles = []
        for r in (self.xstage, self.qT, self.pexp, self.Pm, self.ya, self.rec, self.yaT,
                  self.cu, self.tu, self.cv, self.yc, self.w4ring, self.sg, self.t12, self.mrg,
                  self.zb, self.zq, self.msq):
            self.ab_tiles += [t for _, t in r.items]
        for r in (self.xwc, self.gT, self.wdn, self.accg, self.accv, self.gl, self.ostage, self.ztmp2,
                  self.zb2, self.zq2, self.mean2, self.rstd2, self.msq2, self.stF, self.stB):
            self.c_tiles += [t for _, t in r.items]

        S = self.S
        tC = self.tC
        S.add("pool", lambda e: e.memset(self.ones, 1.0 / 1024.0), writes=[tC])
        S.add("pool", lambda e: e.memset(self.epsc, EPS), writes=[tC])
        self.dma(self.PAR, d["par"], [], [tC], tC)
        self.dma(self.BV.rearrange("p a b -> p (a b)"), d["bv"], [], [tC], tC)
        self.dma(self.FLG, d["flg"], [], [tC], tC)
        self.dma(self.ident, d["identin"], [], [tC], tC)
        self.cp("dve", self.identb, self.ident, [tC], [tC])
        for l in range(2):
            w0 = self.par(l, P_FCW, 44)
            w1 = self.par(l, P_FCW + 44, 44)
            w2 = self.par(l, P_FCW + 88, 44)
            bup = self.par(l, P_BUP, 44)
            fcb = self.par(l, P_FCB, 44)
            D_ = self.DER
            self.tt("dve", D_[:, l, 0, :], w0, w1, ALU.add, [tC], [tC])
            self.tt("dve", D_[:, l, 0, :], D_[:, l, 0, :], w2, ALU.add, [tC], [tC])
            self.tt("dve", D_[:, l, 0, :], D_[:, l, 0, :], bup, ALU.mult, [tC], [tC])
            self.tt("dve", D_[:, l, 0, :], D_[:, l, 0, :], fcb, ALU.add, [tC], [tC])
            self.tt("dve", D_[:, l, 1, :], bup, w0, ALU.mult, [tC], [tC])
            self.tt("dve", D_[:, l, 2, :], bup, w2, ALU.mult, [tC], [tC])
            self.ts("dve", D_[:, l, 3, :], D_[:, l, 1, :], self.FLG[:, 2:3], None, ALU.mult, None, [tC], [tC])
            self.ts("dve", D_[:, l, 4, :], D_[:, l, 2, :], self.FLG[:, 3:4], None, ALU.mult, None, [tC], [tC])

    def barrier(self, old_tiles, new_tiles):
        S = self.S
        S.add("pool", lambda e: e.memset(self.barbuf, 0.0), writes=list(old_tiles))
        last = old_tiles[0].w
        for t in new_tiles:
            t.w = last
            t.rs = {}
            t.rd = []

    def prepass(self):
        d = self.d
        jobs = []
        for k, (n, f) in self.wshapes.items():
            for l in range(2):
                for m in range(n):
                    for c0 in range(0, f, 1024):
                        c1 = min(c0 + 1024, f)
                        jobs.append((d[k][l, m][:, c0:c1], d[k + "b"][l, m][:, c0:c1], c1 - c0, "cast"))
        for l in range(2):
            for m in range(NBM):
                jobs.append((d["bm"][l, m], d["eb"][l, m], 1024, "exp"))
        evs = []
        ci = 0
        for i, (src, dst, f, kind) in enumerate(jobs):
            fa, ft = self.stF.next()
            ba, bt = self.stB.next()
            self.dma(fa[:, :f], src, [], [ft], ft)
            if kind == "exp":
                self.act(ba[:, :f], fa[:, :f], AF.Exp, [ft], [bt])
            else:
                eng = ("dve", "act")[ci % 2]
                ci += 1
                self.cp(eng, ba[:, :f], fa[:, :f], [ft], [bt])
            evs.append(self.dma(dst, ba[:, :f], [bt], [], bt))
        self.S.add("sp", lambda e: e.nop(nofuse=True), extra=evs)

    def load_w8(self, key, l, mc):
        ap, t = self.wring.next()
        self.dma(ap.rearrange("p k j -> p (k j)"), self.d[key + "b"][l, mc], [], [t], t)
        return ap, t

    def load_w4(self, key, l, mc):
        ap, t = self.w4ring.next()
        self.dma(ap.rearrange("p k j -> p (k j)"), self.d[key + "b"][l, mc], [], [t], t)
        return ap, t

    def layer_norm(self, l, c0, n, xt, goff, boff, zb, zq, mean, rstd, msq):
        X = self.X
        pm, pmt = self.bank()
        pq, pqt = self.bank()
        for c in range(8):
            za, zt = zb.next()
            qa, qt = zq.next()
            self.cp("act", za[:, :n], X[:, c, c0:c0 + n], xt, [zt])
            self.act(qa[:, :n], X[:, c, c0:c0 + n], AF.Square, xt, [qt])
            self.mm(pm[:, :n], [(self.ones, za[:, :n])], [zt, self.tC], pmt, flags=(c == 0, c == 7))
            self.mm(pq[:, :n], [(self.ones, qa[:, :n])], [qt, self.tC], pqt, flags=(c == 0, c == 7))
        ma, mt = mean.next()
        ra, rt = rstd.next()
        sa, st = msq.next()
        self.cp("act", ma[:, :n], pm[:, :n], [pmt], [mt])
        self.tt("dve", sa[:, :n], ma[:, :n], ma[:, :n], ALU.mult, [mt], [st])
        self.tt("dve", sa[:, :n], pq[:, :n], sa[:, :n], ALU.subtract, [pqt, st], [st])
        self.ts("dve", sa[:, :n], sa[:, :n], EPS, None, ALU.add, None, [st], [st])
        self.act(ra[:, :n], sa[:, :n], AF.Sqrt, [st], [rt])
        ma2, m2t = self.ptmp.next()
        self.S.add("dve", lambda e: e.reciprocal(ra[:, :n], ra[:, :n]), reads=[rt], writes=[rt])
        self.tt("dve", ma2[:, :n], ra[:, :n], ra[:, :n], ALU.mult, [rt], [m2t])
        self.tt("dve", ma2[:, :n], ma2[:, :n], sa[:, :n], ALU.mult, [m2t, st], [m2t])
        self.ts("dve", ma2[:, :n], ma2[:, :n], -0.5, 1.5, ALU.mult, ALU.add, [m2t], [m2t])
        self.tt("dve", ra[:, :n], ra[:, :n], ma2[:, :n], ALU.mult, [rt, m2t], [rt])
        xv = X[:, :, c0:c0 + n]
        self.tt("dve", xv, xv, ma[:, :n].unsqueeze(1).to_broadcast([128, 8, n]), ALU.subtract, xt + [mt], xt)
        self.tt("dve", xv, xv, ra[:, :n].unsqueeze(1).to_broadcast([128, 8, n]), ALU.mult, xt + [rt], xt)
        for c in range(8):
            eng = "act"
            if eng == "act":
                self.act(X[:, c, c0:c0 + n], X[:, c, c0:c0 + n], AF.Identity, xt + [self.tC], xt,
                         bias=self.par(l, boff + c), scale=self.par(l, goff + c))
            else:
                self.ts("pool", X[:, c, c0:c0 + n], X[:, c, c0:c0 + n], self.par(l, goff + c),
                        self.par(l, boff + c), ALU.mult, ALU.add, xt + [self.tC], xt)

    def chunk(self, kind, xin, yout):
        for l in range(2):
            self.chunk_layer(kind, l, xin, yout)
            if self.stop is not None and self.stop[0] == str(l):
                break

    def xtiles(self, kind, t0, t1):
        xo = kind["xoff"]
        return [self.Xt[t - xo] for t in range(t0, t1) if 0 <= t - xo < 16]

    def kv_from_xb(self, l, xb, xbt, tiles, wvap, wvt):
        n = len(tiles) * 128
        t0 = tiles[0]
        for m in range(4):
            wa, wt = self.load_w8("win", l, 4 + m)
            pb, pbt = self.bank()
            self.mm(pb[:, :n], [(wa[:, k, :], xb[:, k, 0:n]) for k in range(8)], [wt, xbt], pbt)
            self.act(self.kT[:, m, t0 * 128:t0 * 128 + n], pb[:, :n], AF.Identity, [pbt, self.tC],
                     [self.kTt[t] for t in tiles], bias=self.par(l, P_BIN + 4 + m))
        for i, t in enumerate(tiles):
            pb, pbt = self.bank()
            self.mm(pb[:, :], [(xb[:, k, i * 128:(i + 1) * 128], wvap[:, k, :]) for k in range(8)],
                    [wvt, xbt], pbt)
            self.tt("dve", self.vA[:, t, :, 0:64], pb[:, :].rearrange("p (h d) -> p h d", h=8),
                    self.BV[:, l, :].rearrange("p (h d) -> p h d", h=8), ALU.add,
                    [pbt, self.tC], [self.vAt[t]])

    def chunk_layer(self, kind, l, xin, yout):
        L = kind["L"][l]
        xo = kind["xoff"]
        X = self.X
        d = self.d
        a0, a1 = L["A"]
        b0, b1 = L["B"]
        c0_, c1_ = L["C"]
        wvap, wvt = self.wv.next()
        for m_ in range(4):
            self.dma(wvap[:, :, m_ * 128:(m_ + 1) * 128],
                     d["winb"][l, 8 + m_].rearrange("p (k j) -> p k j", k=8), [], [wvt], wvt)
        self.S.add("pool", lambda e: e.memset(self.vA[:, :, :, 64:65], 1.0), writes=self.vAt)
        qa0, qt0 = self.qT.items[0]
        self.S.add("pool", lambda e: e.memset(qa0[0:64, :, 1, :], 0.0), writes=[qt0])
        self.S.add("pool", lambda e: e.memset(qa0[64:128, :, 0, :], 0.0), writes=[qt0])
        self.dma(self.Eint, d["eb"][l, 0:5].rearrange("m p f -> p m f"), [], [self.Eintt], self.Eintt)
        t = a0
        while t < a1:
            tiles = list(range(t, min(t + 3, a1)))
            t += 3
            n = len(tiles) * 128
            xb, xbt = self.xw.next()
            if l == 0:
                for i, tt_ in enumerate(tiles):
                    sa, st = self.xstage.next()
                    self.dma(sa, xin[tt_ * 128:(tt_ + 1) * 128, :], [], [st], st)
                    for hb in range(2):
                        pb, pbt = self.bank()

                        def fn(e, pb=pb, sa=sa, hb=hb):
                            ins = None
                            for c in range(4):
                                ins = e.transpose(out=pb[:, c * 128:(c + 1) * 128],
                                                  in_=sa[:, (hb * 4 + c) * 128:(hb * 4 + c + 1) * 128],
                                                  identity=self.ident)
                            return ins
                        self.S.add("pe", fn, reads=[st, self.tC], writes=[pbt])
                        pv = pb[:, :].rearrange("p (c j) -> p c j", c=4)
                        if 0 <= tt_ - xo < 16:
                            xc = (tt_ - xo) * 128
                            self.cp("dve", X[:, hb * 4:hb * 4 + 4, xc:xc + 128], pv, [pbt], [self.Xt[tt_ - xo]])
                            self.cp("act", xb[:, hb * 4:hb * 4 + 4, i * 128:(i + 1) * 128],
                                    X[:, hb * 4:hb * 4 + 4, xc:xc + 128], [self.Xt[tt_ - xo]], [xbt])
                        else:
                            self.cp("act", xb[:, hb * 4:hb * 4 + 4, i * 128:(i + 1) * 128], pv, [pbt], [xbt])
            else:
                xc = (tiles[0] - xo) * 128
                self.cp("act", xb[:, :, 0:n], X[:, :, xc:xc + n], self.xtiles(kind, tiles[0], tiles[-1] + 1), [xbt])
            self.kv_from_xb(l, xb, xbt, tiles, wvap, wvt)
        if self.stop == f"{l}A":
            return
        wins = split_windows(b0, b1, L["forced"])
        ctx = self.b_pre(kind, l, L, wins[0][0], wins[0][1], True)
        self.b_q(l, ctx)
        for wi, (t0, t1) in enumerate(wins):
            ctx = self.phase_b_window(kind, l, L, ctx, wins[wi + 1] if wi + 1 < len(wins) else None)
        self.barrier(self.ab_tiles, self.c_tiles)
        if self.stop == f"{l}B":
            self.write_out(kind, yout)
            self.barrier(self.c_tiles, self.ab_tiles)
            return
        wins = split_windows(c0_, c1_, L["forced"])
        ctx = self.c_pre(kind, l, L, wins[0][0], wins[0][1], True)
        for wi, (t0, t1) in enumerate(wins):
            ctx = self.phase_c_window(kind, l, L, ctx, wins[wi + 1] if wi + 1 < len(wins) else None)
        if l == 1 or self.stop == f"{l}C":
            self.write_out(kind, yout)
        self.barrier(self.c_tiles, self.ab_tiles)

    def write_out(self, kind, yout):
        X = self.X
        xo = kind["xoff"]
        o0, o1 = kind["out"]
        for t in range(o0, o1):
            xc = (t - xo) * 128
            oa, ot = self.ostage.next()
            for hb in range(2):
                pb, pbt = self.bank()

                def fn(e, pb=pb, hb=hb, xc=xc):
                    ins = None
                    for c in range(4):
                        ins = e.transpose(out=pb[:, c * 128:(c + 1) * 128],
                                          in_=X[:, hb * 4 + c, xc:xc + 128], identity=self.ident)
                    return ins
                self.S.add("pe", fn, reads=[self.Xt[t - xo], self.tC], writes=[pbt])
                self.cp("act" if hb == 0 else "dve", oa[:, hb * 512:(hb + 1) * 512], pb[:, :], [pbt], [ot])
            ev = self.dma(yout[(t - o0) * 128:(t - o0 + 1) * 128, :], oa, [ot], [], ot)
            self.out_events.append(ev)

    def build_window(self, xwa, xwt, kind, t0, t1, first, L, leftx):
        X = self.X
        xo = kind["xoff"]
        n = (t1 - t0) * 128
        xc = (t0 - xo) * 128
        xt = self.xtiles(kind, t0, t1)
        right_ok = (t1 - xo) < 16 and t1 < kind["NT"]
        if right_ok:
            self.cp("act", xwa[:, :, 1:n + 2], X[:, :, xc:xc + n + 1], xt + self.xtiles(kind, t1, t1 + 1), [xwt])
        else:
            self.cp("act", xwa[:, :, 1:n + 1], X[:, :, xc:xc + n], xt, [xwt])
            self.S.add("pool", lambda e: e.memset(xwa[:, :, n + 1:n + 2], 0.0), writes=[xwt])
        if first and leftx and xc > 0:
            self.cp("pool", xwa[:, :, 0:1], X[:, :, xc - 1:xc], self.xtiles(kind, t0 - 1, t0), [xwt])
        elif first:
            self.S.add("pool", lambda e: e.memset(xwa[:, :, 0:1], 0.0), writes=[xwt])
        else:
            self.cp("pool", xwa[:, :, 0:1], self.xsave, [self.xsavet], [xwt])
        return n, xc, xt

    def edge_mode(self, L, t0, t1):
        return L["bl"].get(t0), L["br"].get(t1)

    def b_pre(self, kind, l, L, t0, t1, first):
        tC = self.tC
        xwa, xwt = self.xw.next()
        n, xc, xt = self.build_window(xwa, xwt, kind, t0, t1, first, L, False)
        return dict(xwa=xwa, xwt=xwt, n=n, xc=xc, xt=xt, t0=t0, t1=t1)

    def b_q(self, l, ctx):
        tC = self.tC
        xwa, xwt, n = ctx["xwa"], ctx["xwt"], ctx["n"]
        xm = xwa[:, :, 1:n + 1]
        qa, qt = self.qT.next()
        ctx["qa"], ctx["qt"] = qa, qt
        for m in range(4):
            wa, wt = self.load_w8("win", l, m)
            pb, pbt = self.bank()
            self.mm(pb[:, :n], [(wa[:, k, :], xm[:, k, :]) for k in range(8)], [wt, xwt], pbt)
            for pr in (0, 64):
                self.act(qa[pr:pr + 64, m, pr // 64, :n], pb[pr:pr + 64, :n], AF.Identity, [pbt, tC], [qt],
                         bias=self.PAR[pr:pr + 64, l * NPAR + P_BIN + m:l * NPAR + P_BIN + m + 1])

    def phase_b_window(self, kind, l, L, ctx, nwin):
        X = self.X
        d = self.d
        S = self.S
        tC = self.tC
        xwa, xwt, n, xc, xt, t0, t1 = (ctx[k] for k in ("xwa", "xwt", "n", "xc", "xt", "t0", "t1"))
        qa, qt = ctx["qa"], ctx["qt"]
        el, er = self.edge_mode(L, t0, t1)
        xm = xwa[:, :, 1:n + 1]
        for m in range(0):
            wa, wt = self.load_w8("win", l, m)
            pb, pbt = self.bank()
            self.mm(pb[:, :n], [(wa[:, k, :], xm[:, k, :]) for k in range(8)], [wt, xwt], pbt)
            for pr in (0, 64):
                self.act(qa[pr:pr + 64, m, pr // 64, :n], pb[pr:pr + 64, :n], AF.Identity, [pbt, tC], [qt],
                         bias=self.PAR[pr:pr + 64, l * NPAR + P_BIN + m:l * NPAR + P_BIN + m + 1])
        yTa, yTt = self.yaT.next()
        cua, cut = self.cu.next()
        yca, yct = self.yc.next()
        xh = xwa[:, :, 0:n + 2]
        fillers = []

        def f_ugc(m):
            wa, wt = self.load_w8("win", l, 12 + m)
            pu, put = self.bank()
            self.mm(pu[:, :n + 2], [(wa[:, k, :], xh[:, k, :]) for k in range(8)], [wt, xwt], put)
            wa2, wt2 = self.load_w8("win", l, 20 + m)
            pg, pgt = self.bank()
            self.mm(pg[:, :n + 2], [(wa2[:, k, :], xh[:, k, :]) for k in range(8)], [wt2, xwt], pgt)
            ta, tt_ = self.tu.next()
            self.act(ta[:, :n + 2], pu[:, :n + 2], AF.Identity, [put, tC], [tt_], bias=self.par(l, P_BIN + 12 + m))
            self.stt("dve", cua[:, m, :n + 2], pg[:, :n + 2], self.par(l, P_BIN + 20 + m), ta[:, :n + 2],
                     ALU.add, ALU.mult, [pgt, tt_, tC], [cut])

        def f_edge():
            for mode, col in ((el, 0), (er, n + 1)):
                if mode == "zero":
                    S.add("pool", lambda e, col=col: e.memset(cua[:, :, col:col + 1], 0.0), writes=[cut])
                elif mode in ("ftop", "fbot"):
                    f = self.FLG[:, 0:1] if mode == "ftop" else self.FLG[:, 1:2]
                    for m in range(4):
                        self.ts("pool", cua[:, m, col:col + 1], cua[:, m, col:col + 1], f, None, ALU.mult, None,
                                [cut, tC], [cut])

        def f_conv(m):
            if m == 0:
                f_edge()
            va, vt = self.cv.next()
            w0 = self.par(l, P_SCW + m)
            w1 = self.par(l, P_SCW + 4 + m)
            w2 = self.par(l, P_SCW + 8 + m)
            self.ts("pool", va[:, :n], cua[:, m, 1:n + 1], w1, self.par(l, P_SCB + m), ALU.mult, ALU.add, [cut, tC], [vt])
            self.fma("pool", va[:, :n], cua[:, m, 0:n], w0, va[:, :n], [cut, tC, vt], [vt])
            self.fma("pool", va[:, :n], cua[:, m, 2:n + 2], w2, va[:, :n], [cut, tC, vt], [vt])
            wa, wt = self.load_w8("win", l, 16 + m)
            pb, pbt = self.bank()
            self.mm(pb[:, :n], [(wa[:, k, :], xm[:, k, :]) for k in range(8)], [wt, xwt], pbt)
            self.stt("dve", yca[:, m, :n], pb[:, :n], self.par(l, P_BIN + 16 + m), va[:, :n], ALU.add, ALU.mult,
                     [pbt, vt, tC], [yct])
        for m in range(4):
            fillers.append(lambda m=m: f_ugc(m))
        for m in range(4):
            fillers.append(lambda m=m: f_conv(m))

        items = []
        for t in range(t0, t1):
            dl = L["q"][t]
            esrc = [s_ for (_, s_) in dl if s_[0] == "e"]
            for j, (dd, src) in enumerate(dl):
                for hb in range(2):
                    items.append((t, j, dd, src, len(dl), esrc, hb))
        po = [(self.ps[5], self.pst[5]), (self.ps[6], self.pst[6])]
        self.nbank = 3
        self.pbank = 0

        def emit_scores(it):
            t, j, dd, src, nj, esrc, hb = it
            kc = (t + dd) * 128
            qc = (t - t0) * 128
            self.sbi ^= 1
            pbank, pbt = self.ps[3 + self.sbi], self.pst[3 + self.sbi]

            def fn(e, hb=hb, kc=kc, qc=qc, pbank=pbank):
                ins = None
                for hh in range(4):
                    h = hb * 4 + hh
                    ins = e.matmul(pbank[:, hh * 128:(hh + 1) * 128],
                                   lhsT=self.kT[:, h // 2, kc:kc + 128],
                                   rhs=qa[:, h // 2, h % 2, qc:qc + 128],
                                   start=True, stop=True, skip_group_check=True)
                return ins
            S.add("pe", fn, reads=[self.kTt[t + dd], qt], writes=[pbt])
            return pbank, pbt

        def finish_tile(t):
            qc = (t - t0) * 128
            ra, rt = self.rec.next()
            ya, yt = self.ya.next()
            for hb in range(2):
                pv = po[hb][0][:, 0:260].rearrange("p (h d) -> p h d", h=4)
                S.add("dve", lambda e, ra=ra, pv=pv, hb=hb: e.reciprocal(ra[:, hb * 4:hb * 4 + 4].unsqueeze(2), pv[:, :, 64:65]),
                      reads=[po[hb][1]], writes=[rt])
                self.tt("dve", ya[:, hb * 256:(hb + 1) * 256].rearrange("p (h d) -> p h d", h=4), pv[:, :, 0:64],
                        ra[:, hb * 4:hb * 4 + 4].unsqueeze(2).to_broadcast([128, 4, 64]), ALU.mult,
                        [po[hb][1], rt], [yt])
            pbb = self.psb[:, 0:512]

            def fn(e, pbb=pbb, ya=ya):
                ins = None
                for c in range(4):
                    ins = e.transpose(out=pbb[:, c * 128:(c + 1) * 128], in_=ya[:, c * 128:(c + 1) * 128],
                                      identity=self.identb)
                return ins
            S.add("pe", fn, reads=[yt, tC], writes=[self.psbt[0]])
            self.cp("act", yTa[:, :, qc:qc + 128], pbb.rearrange("p (c j) -> p c j", c=4), [self.psbt[0]], [yTt])

        pending = emit_scores(items[0])
        for idx, it in enumerate(items):
            t, j, dd, src, nj, esrc, hb = it
            kt = t + dd
            nxt = emit_scores(items[idx + 1]) if idx + 1 < len(items) else None
            if j == 0 and hb == 0 and esrc:
                i0 = esrc[0][1]
                ne = len(esrc)
                self.dma(self.Eedge[:, 0:ne, :], d["eb"][l, i0:i0 + ne].rearrange("m p f -> p m f"),
                         [], [self.Eedget], self.Eedget)
            pa, pt = self.pexp.next()
            self.act(pa, pending[0][:, :], AF.Exp, [pending[1]], [pt], scale=0.125)
            ma, mt = self.Pm.next()
            if src[0] == "i":
                ea, et = self.Eint[:, src[1], hb * 512:(hb + 1) * 512], self.Eintt
            else:
                ea, et = self.Eedge[:, src[1] - esrc[0][1], hb * 512:(hb + 1) * 512], self.Eedget
            self.tt("dve", ma, pa, ea, ALU.mult, [pt, et], [mt])

            def fn(e, hb=hb, kt=kt, ma=ma, j=j, nj=nj, pbank=po[hb][0]):
                ins = None
                for hh in range(4):
                    h = hb * 4 + hh
                    ins = e.matmul(pbank[:, hh * 65:(hh + 1) * 65], lhsT=ma[:, hh * 128:(hh + 1) * 128],
                                   rhs=self.vA[:, kt, h, 0:65], start=(j == 0 and hh == 0), stop=(j == nj - 1),
                                   skip_group_check=True)
                return ins
            S.add("pe", fn, reads=[mt, self.vAt[kt]], writes=[po[hb][1]])
            if fillers and hb == 1:
                fillers.pop(0)()
            if j == nj - 1 and hb == 1:
                finish_tile(t)
            pending = nxt
        self.nbank = 5
        while fillers:
            fillers.pop(0)()
        mga, mgt = self.mrg.next()
        for mo in range(8):
            wa, wt = self.load_w4("wba", l, mo)
            pa_, pat = self.bank()
            self.mm(pa_[:, :n], [(wa[:, k, :], yTa[:, k, :n]) for k in range(4)], [wt, yTt], pat)
            wc, wct = self.load_w4("wbc", l, mo)
            pc_, pct = self.bank()
            self.mm(pc_[:, :n], [(wc[:, k, :], yca[:, k, :n]) for k in range(4)], [wct, yct], pct)
            wg, wgt = self.load_w8("win", l, 24 + mo)
            pga, pgat = self.bank()
            self.mm(pga[:, :n], [(wg[:, k, :], xm[:, k, :]) for k in range(8)], [wgt, xwt], pgat)
            wg2, wgt2 = self.load_w8("win", l, 32 + mo)
            pgc, pgct = self.bank()
            self.mm(pgc[:, :n], [(wg2[:, k, :], xm[:, k, :]) for k in range(8)], [wgt2, xwt], pgct)
            s1, s1t = self.sg.next()
            s2, s2t = self.sg.next()
            self.act(s1[:, :n], pga[:, :n], AF.Sigmoid, [pgat, tC], [s1t], bias=self.par(l, P_BIN + 24 + mo))
            self.act(s2[:, :n], pgc[:, :n], AF.Sigmoid, [pgct, tC], [s2t], bias=self.par(l, P_BIN + 32 + mo))
            u1, u1t = self.t12.next()
            u2, u2t = self.t12.next()
            self.tt("dve", u1[:, :n], s1[:, :n], pa_[:, :n], ALU.mult, [s1t, pat], [u1t])
            self.tt("dve", u2[:, :n], s2[:, :n], pc_[:, :n], ALU.mult, [s2t, pct], [u2t])
            self.tt("pool", mga[:, mo, :n], u1[:, :n], u2[:, :n], ALU.add, [u1t, u2t], [mgt])
        self.cp("pool", self.xsave, X[:, :, xc + n - 1:xc + n], xt, [self.xsavet])
        nctx = None
        if nwin is not None:
            nctx = self.b_pre(kind, l, L, nwin[0], nwin[1], False)
        for mo in range(8):
            wa, wt = self.load_w8("wo", l, mo)
            pb, pbt = self.bank()
            self.mm(pb[:, :n], [(wa[:, k, :], mga[:, k, :n]) for k in range(8)], [wt, mgt], pbt)
            za, zt = self.ztmp.next()
            self.act(za[:, :n], pb[:, :n], AF.Identity, [pbt, tC], [zt], bias=self.par(l, P_BO + mo))
            self.stt("dve", X[:, mo, xc:xc + n], X[:, mo, xc:xc + n], ALPHA, za[:, :n], ALU.mult, ALU.add,
                     xt + [zt], xt)
        if nctx is not None:
            self.b_q(l, nctx)
        self.layer_norm(l, xc, n, xt, P_G1, P_B1, self.zb, self.zq, self.mean, self.rstd, self.msq)
        return nctx

    def c_pre(self, kind, l, L, t0, t1, first):
        tC = self.tC
        xwa, xwt = self.xwc.next()
        n, xc, xt = self.build_window(xwa, xwt, kind, t0, t1, first, L, t0 > L["B"][0])
        el, er = self.edge_mode(L, t0, t1)
        for mode, col in ((el, 0), (er, n + 1)):
            if mode in ("ftop", "fbot"):
                f = self.FLG[:, 0:1] if mode == "ftop" else self.FLG[:, 1:2]
                self.ts("pool", xwa[:, :, col:col + 1], xwa[:, :, col:col + 1], f, None, ALU.mult, None,
                        [xwt, tC], [xwt])
        ga, gt = self.gT.next()
        return dict(xwa=xwa, xwt=xwt, n=n, xc=xc, xt=xt, el=el, er=er, ga=ga, gt=gt, m=0)

    def c_iter(self, l, ctx, count):
        tC = self.tC
        DER = self.DER
        xwa, xwt, n, el, er, ga, gt = (ctx[k] for k in ("xwa", "xwt", "n", "el", "er", "ga", "gt"))
        xh = xwa[:, :, 0:n + 2]
        for m in range(ctx["m"], min(22, ctx["m"] + count)):
            wg, wgt = self.load_w8("wup", l, m)
            wv_, wvt_ = self.load_w8("wup", l, 22 + m)
            pg, pgt = self.bank()
            self.mm(pg[:, :n + 2], [(wg[:, k, :], xh[:, k, :]) for k in range(8)], [wgt, xwt], pgt)
            pv, pvt = self.bank()
            self.mm(pv[:, :n + 2], [(wv_[:, k, :], xh[:, k, :]) for k in range(8)], [wvt_, xwt], pvt)
            accs = []
            for (pp, ppt, ring, mc) in ((pg, pgt, self.accg, m), (pv, pvt, self.accv, 22 + m)):
                aa, at = ring.next()
                w0 = self.par(l, P_FCW + mc)
                w1 = self.par(l, P_FCW + 44 + mc)
                w2 = self.par(l, P_FCW + 88 + mc)
                self.act(aa[:, :n], pp[:, 1:n + 1], AF.Identity, [ppt, tC], [at], bias=DER[:, l, 0, mc:mc + 1], scale=w1)
                self.stt("dve", aa[:, :n], pp[:, 0:n], w0, aa[:, :n], ALU.mult, ALU.add, [ppt, at, tC], [at])
                self.stt("dve", aa[:, :n], pp[:, 2:n + 2], w2, aa[:, :n], ALU.mult, ALU.add, [ppt, at, tC], [at])
                if el is not None:
                    j = 1 if el == "zero" else 3
                    self.tt("pool", aa[:, 0:1], aa[:, 0:1], DER[:, l, j, mc:mc + 1], ALU.subtract, [at, tC], [at])
                if er is not None:
                    j = 2 if er == "zero" else 4
                    self.tt("pool", aa[:, n - 1:n], aa[:, n - 1:n], DER[:, l, j, mc:mc + 1], ALU.subtract, [at, tC], [at])
                accs.append((aa, at))
            la, lt = self.gl.next()
            self.act(la[:, :n], accs[0][0][:, :n], AF.Gelu_apprx_tanh, [accs[0][1]], [lt])
            self.tt("pool", ga[:, m, :n], la[:, :n], accs[1][0][:, :n], ALU.mult, [lt, accs[1][1]], [gt])
        ctx["m"] = min(22, ctx["m"] + count)

    def phase_c_window(self, kind, l, L, ctx, nxt):
        X = self.X
        tC = self.tC
        n, xc, xt, ga, gt = (ctx[k] for k in ("n", "xc", "xt", "ga", "gt"))
        self.c_iter(l, ctx, 22)
        self.cp("pool", self.xsave, X[:, :, xc + n - 1:xc + n], xt, [self.xsavet])
        nctx = None
        if nxt is not None:
            nctx = self.c_pre(kind, l, L, nxt[0], nxt[1], False)
        for mo in range(8):
            wa, wt = self.wdn.next()
            self.dma(wa.rearrange("p k j -> p (k j)"), self.d["wdnb"][l, mo], [], [wt], wt)
            pb, pbt = self.bank()
            self.mm(pb[:, :n], [(wa[:, k, :], ga[:, k, :n]) for k in range(22)], [wt, gt], pbt)
            za, zt = self.ztmp2.next()
            self.act(za[:, :n], pb[:, :n], AF.Identity, [pbt, tC], [zt], bias=self.par(l, P_BDN + mo))
            self.stt("dve", X[:, mo, xc:xc + n], X[:, mo, xc:xc + n], ALPHA, za[:, :n], ALU.mult, ALU.add,
                     xt + [zt], xt)
        if nctx is not None:
            self.c_iter(l, nctx, 6)
        self.layer_norm(l, xc, n, xt, P_G2, P_B2, self.zb2, self.zq2, self.mean2, self.rstd2, self.msq2)
        return nctx

    def build(self):
        nc = self.nc
        self.d["identin"] = nc.dram_tensor("identin", [128, 128], F32, kind="ExternalInput").ap()
        self.setup()
        self.prepass()
        self.barrier(self.c_tiles, self.ab_tiles)
        sk = sample_kind()
        for i in range(self.NS if self.stop != 'P' else 0):
            self.chunk(sk, self.d["xs"][i], self.d["ys"][i])
        for j in range(self.NP):
            self.chunk(prompt_kind(j), self.d["xp"][j], self.d["yp"][j])
        self.S.add("sp", lambda e: e.nop(nofuse=True), extra=self.out_events)
        with ExitStack() as stack:
            self.S.emit(nc, stack)
        return nc


class Ring:
    def __init__(self, items):
        self.items = items
        self.i = 0

    def next(self):
        it = self.items[self.i]
        self.i = (self.i + 1) % len(self.items)
        return it


def wlayout(w):
    K, M = w.shape
    return np.ascontiguousarray(w.reshape(K // 128, 128, M // 128, 128).transpose(2, 1, 0, 3)).reshape(
        M // 128, 128, K)


def col(v):
    return np.ascontiguousarray(v.reshape(-1, 128).T)


def bm_tile(rpb_l, i0, r0, R):
    out = np.full((128, 8, 128), NEG, np.float32)
    kc = np.arange(64)[:, None]
    qc = np.arange(64)[None, :]
    js = np.clip(qc - 8, 0, 48)
    valid = (kc >= js) & (kc < js + 16)
    dc = np.clip(kc - qc + 15, 0, 30)
    for kr in range(2):
        for qr in range(2):
            r = r0 + kr
            i = i0 + qr
            if i < 0 or i >= R:
                continue
            rs = min(max(i - 4, 0), R - 8)
            if not (rs <= r < rs + 8):
                continue
            g = rpb_l[:, r - i + 7][:, dc]
            blk = np.where(valid[None], g, np.float32(NEG))
            out[kr * 64:(kr + 1) * 64, :, qr * 64:(qr + 1) * 64] = blk.transpose(1, 0, 2)
    return out.reshape(128, 1024)


def bm_set(rpb_l, half):
    tiles = []
    big = 1000
    def interior(d):
        return bm_tile(rpb_l, 500, 500 + 2 * d, big)
    for d in range(-2, 3):
        tiles.append(interior(d))
    for t, ds in ((0, range(0, 4)), (1, range(-1, 3)), (14, range(-2, 2)), (15, range(-3, 1))):
        for d in ds:
            tiles.append(bm_tile(rpb_l, 2 * t, 2 * (t + d), 32))
    for t, ds in ((5, range(-2, 4)), (6, range(-2, 3))):
        for d in ds:
            if half == 0:
                tiles.append(bm_tile(rpb_l, 2 * (t - 5), 2 * (t - 5 + d), 128))
            else:
                tiles.append(interior(d))
    for t, ds in ((11, range(-2, 3)), (12, range(-3, 3))):
        for d in ds:
            if half == 1:
                tiles.append(bm_tile(rpb_l, 112 + 2 * (t - 5), 112 + 2 * (t - 5 + d), 128))
            else:
                tiles.append(interior(d))
    assert len(tiles) == NBM
    return np.stack(tiles)


def host_params(inp):
    pars = []
    for l in range(2):
        cols = [col(inp["b_in"][l]),
                np.ascontiguousarray(inp["sc_conv_w"][l].reshape(3, 4, 128).transpose(2, 0, 1)).reshape(128, 12),
                col(inp["sc_conv_b"][l]), col(inp["b_o"][l]), col(inp["ln1_g"][l]), col(inp["ln1_b"][l]),
                col(inp["ffn_b_up"][l]),
                np.ascontiguousarray(inp["ffn_conv_w"][l].reshape(3, 44, 128).transpose(2, 0, 1)).reshape(128, 132),
                col(inp["ffn_conv_b"][l]), col(inp["ffn_b_down"][l]), col(inp["ln2_g"][l]), col(inp["ln2_b"][l])]
        p = np.concatenate(cols, axis=1)
        assert p.shape == (128, NPAR)
        pars.append(p)
    par = np.ascontiguousarray(np.concatenate(pars, axis=1), dtype=np.float32)
    bv = np.ascontiguousarray(np.broadcast_to(
        np.stack([inp["b_in"][l][1024:1536] for l in range(2)]).reshape(1, 1024), (128, 1024)), dtype=np.float32)
    return par, bv


def host_weights(inp):
    out = {}
    for key, name in (("win", "w_in"), ("wba", "w_br_attn"), ("wbc", "w_br_conv"), ("wo", "w_o"),
                      ("wup", "ffn_w_up"), ("wdn", "ffn_w_down")):
        out[key] = np.stack([wlayout(np.asarray(inp[name][l], np.float32)) for l in range(2)])
    return out


_CACHE = {}


def get_nc(NS, NP):
    k = (NS, NP)
    if k not in _CACHE:
        _CACHE[k] = Builder(NS, NP).build()
    return _CACHE[k]


def kernel(**inputs):
    inp = {k: np.asarray(v) for k, v in inputs.items()}
    xp_full = inp["x_prompt"]
    xs_full = inp["x_sample"]
    n = 8
    nc = get_nc(4, 4)
    par, bv = host_params(inp)
    W = host_weights(inp)
    ident = np.eye(128, dtype=np.float32)
    bms = [np.stack([bm_set(inp["attn_rpb"][l], h) for l in range(2)]) for h in range(2)]
    in_maps = []
    for c in range(n):
        p, half = c // 2, c % 2
        xpc = np.zeros((4, 17 * 128, D), np.float32)
        for j in range(4):
            r0 = 64 * half + 16 * j
            lo, hi = r0 - 10, r0 + 24
            slo, shi = max(lo, 0), min(hi, 128)
            xpc[j, (slo - lo) * 64:(shi - lo) * 64] = xp_full[p, slo * 64:shi * 64]
        ftop = 0.0 if half == 0 else 1.0
        fbot = 0.0 if half == 1 else 1.0
        flg = np.ascontiguousarray(np.broadcast_to(np.array([ftop, fbot, 1 - ftop, 1 - fbot], np.float32), (128, 4)))
        m = dict(xs=np.ascontiguousarray(xs_full[4 * c:4 * c + 4]), xp=xpc, bm=bms[half], par=par, bv=bv,
                 flg=flg, identin=ident)
        m.update(W)
        in_maps.append(m)
    res = run_bass_kernel_spmd(nc, in_maps, core_ids=list(range(n)))
    y_prompt = np.empty_like(xp_full)
    y_sample = np.empty_like(xs_full)
    for c in range(n):
        r = res.results[c]
        p, half = c // 2, c % 2
        y_sample[4 * c:4 * c + 4] = r["ys"]
        for j in range(4):
            r0 = 64 * half + 16 * j
            y_prompt[p, r0 * 64:(r0 + 16) * 64] = r["yp"][j]
    return (y_prompt, y_sample)
```

```python
import numpy as np
import ml_dtypes
from contextlib import ExitStack
import concourse.bass as bass
import concourse.mybir as mybir
from concourse.bass_utils import run_bass_kernel_spmd

F32 = mybir.dt.float32
BF16 = mybir.dt.bfloat16
AF = mybir.ActivationFunctionType
ALU = mybir.AluOpType

D = 1024
NH = 8
DH = 64
DFF = 2816
ALPHA = float(4 ** 0.25)
EPS = 1e-5
NEG = -30000.0
NPAR = 324
P_BIN, P_SCW, P_SCB, P_BO, P_G1, P_B1, P_BUP, P_FCW, P_FCB, P_BDN, P_G2, P_B2 = (
    0, 40, 52, 56, 64, 72, 80, 124, 256, 300, 308, 316)
NBM = 43

ENGS = ("pe", "act", "dve", "pool", "sp")


class Tl:
    __slots__ = ("name", "w", "rs", "rd", "sem", "cnt")

    def __init__(self, name):
        self.name = name
        self.w = None
        self.rs = {}
        self.rd = []
        self.sem = None
        self.cnt = 0


class Op:
    __slots__ = ("eng", "fn", "waits", "sig", "idx", "sigval", "dsem", "line")


class Sched:
    def __init__(self):
        self.q = {e: [] for e in ENGS}
        self.dma_tiles = []

    def add(self, eng, fn, reads=(), writes=(), dma=None, extra=()):
        op = Op()
        op.eng = eng
        op.fn = fn
        op.sig = False
        op.idx = len(self.q[eng])
        op.dsem = dma
        op.sigval = 0
        import sys as _s
        f = _s._getframe(1)
        ls = []
        while f is not None and len(ls) < 4:
            ls.append(f.f_lineno)
            f = f.f_back
        op.line = ls
        deps = list(extra)
        for t in reads:
            if t.w is not None:
                deps.append(t.w)
        for t in writes:
            if t.w is not None:
                deps.append(t.w)
            deps.extend(t.rs.values())
            deps.extend(t.rd)
        waits = []
        seen = set()
        for d in deps:
            if id(d) in seen:
                continue
            seen.add(id(d))
            if isinstance(d, Op):
                if d.eng == eng:
                    if eng in ("pe", "sp"):
                        continue
                    if op.idx - d.idx > 3:
                        continue
                d.sig = True
            waits.append(d)
        op.waits = waits
        if dma is not None:
            if dma.sem is None:
                dma.sem = True
                self.dma_tiles.append(dma)
            dma.cnt += 16
            ev = (dma, dma.cnt)
        else:
            ev = op
        for t in reads:
            if isinstance(ev, Op):
                t.rs[eng] = ev
            else:
                t.rd.append(ev)
        for t in writes:
            t.w = ev
            t.rs = {}
            t.rd = []
        self.q[eng].append(op)
        return ev

    def emit(self, nc, stack):
        esem = {}
        for e in ("pe", "act", "dve", "pool"):
            esem[e] = stack.enter_context(nc.semaphore("s_" + e))
        for t in self.dma_tiles:
            t.sem = stack.enter_context(nc.semaphore("d_" + t.name))
        for e in ENGS:
            c = 0
            for op in self.q[e]:
                if op.sig:
                    c += 1
                op.sigval = c
        q = self.q

        def run(name, eng):
            waited = {}
            for op in q[name]:
                for d in op.waits:
                    if isinstance(d, Op):
                        sem, val = esem[d.eng], d.sigval
                    else:
                        sem, val = d[0].sem, d[1]
                    k = id(sem)
                    if waited.get(k, 0) < val:
                        eng.wait_ge(sem, val)
                        waited[k] = val
                ins = op.fn(eng)
                if op.dsem is not None:
                    ins.then_inc(op.dsem.sem, 16)
                elif op.sig:
                    ins.then_inc(esem[name], 1)

        with nc.Block() as block:
            @block.tensor
            def _(e):
                run("pe", e)

            @block.scalar
            def _(e):
                run("act", e)

            @block.vector
            def _(e):
                run("dve", e)

            @block.gpsimd
            def _(e):
                run("pool", e)

            @block.sync
            def _(e):
                run("sp", e)


def split_windows(t0, t1, forced=()):
    cuts = sorted(set([t0, t1] + [f for f in forced if t0 < f < t1]))
    out = []
    for a, b in zip(cuts[:-1], cuts[1:]):
        n = b - a
        k = (n + 2) // 3
        base, rem = divmod(n, k)
        s = a
        for i in range(k):
            ln = base + (1 if i < rem else 0)
            out.append((s, s + ln))
            s += ln
    return out


def sample_kind():
    q = {}
    for t in range(16):
        if t == 0:
            q[t] = [(d, ("e", 5 + i)) for i, d in enumerate(range(0, 4))]
        elif t == 1:
            q[t] = [(d, ("e", 9 + i)) for i, d in enumerate(range(-1, 3))]
        elif t == 14:
            q[t] = [(d, ("e", 13 + i)) for i, d in enumerate(range(-2, 2))]
        elif t == 15:
            q[t] = [(d, ("e", 17 + i)) for i, d in enumerate(range(-3, 1))]
        else:
            q[t] = [(d, ("i", d + 2)) for d in range(-2, 3)]
    lay = dict(A=(0, 16), B=(0, 16), C=(0, 16), q=q, forced=(),
               bl={0: "zero"}, br={16: "zero"})
    return dict(NT=16, xoff=0, L=[lay, lay], out=(0, 16))


def prompt_kind(j):
    def qmap(b0, b1, a0, a1):
        q = {}
        for t in range(b0, b1):
            if j == 0 and t == 5:
                lst = [(d, ("e", 21 + i)) for i, d in enumerate(range(-2, 4))]
            elif j == 0 and t == 6:
                lst = [(d, ("e", 27 + i)) for i, d in enumerate(range(-2, 3))]
            elif j == 3 and t == 11:
                lst = [(d, ("e", 32 + i)) for i, d in enumerate(range(-2, 3))]
            elif j == 3 and t == 12:
                lst = [(d, ("e", 37 + i)) for i, d in enumerate(range(-3, 3))]
            else:
                lst = [(d, ("i", d + 2)) for d in range(-2, 3)]
            q[t] = [(d, s) for (d, s) in lst if a0 <= t + d < a1]
        return q
    bl = {5: "ftop"} if j == 0 else {}
    br = {13: "fbot"} if j == 3 else {}
    l0 = dict(A=(0, 17), B=(2, 16), C=(2, 15), q=qmap(2, 16, 0, 17), forced=(5, 13), bl=bl, br=br)
    l1 = dict(A=(2, 15), B=(4, 14), C=(5, 13), q=qmap(4, 14, 2, 15), forced=(5, 13), bl=bl, br=br)
    return dict(NT=17, xoff=2, L=[l0, l1], out=(5, 13))


class Builder:
    def __init__(self, NS, NP, dbg=False, stop=None):
        self.NS, self.NP = NS, NP
        self.stop = stop
        self.bstop = None
        if stop is not None and len(stop) == 3:
            self.bstop = int(stop[2])
            self.stop = stop[:2]
        self.nc = nc = bass.Bass("TRN2", target_bir_lowering=False)
        self.S = Sched()
        dt = nc.dram_tensor
        self.d = d = {}
        if NS:
            d["xs"] = dt("xs", [NS, 2048, D], F32, kind="ExternalInput").ap()
            d["ys"] = dt("ys", [NS, 2048, D], F32, kind="ExternalOutput").ap()
        if NP:
            d["xp"] = dt("xp", [NP, 17 * 128, D], F32, kind="ExternalInput").ap()
            d["yp"] = dt("yp", [NP, 1024, D], F32, kind="ExternalOutput").ap()
        self.wshapes = dict(win=(40, 1024), wba=(8, 512), wbc=(8, 512), wo=(8, 1024),
                            wup=(44, 1024), wdn=(8, 2816))
        for k, (n, f) in self.wshapes.items():
            d[k] = dt(k, [2, n, 128, f], F32, kind="ExternalInput").ap()
            d[k + "b"] = dt(k + "b", [2, n, 128, f], BF16, kind="Internal").ap()
        d["bm"] = dt("bm", [2, NBM, 128, 1024], F32, kind="ExternalInput").ap()
        d["eb"] = dt("eb", [2, NBM, 128, 1024], BF16, kind="Internal").ap()
        d["par"] = dt("par", [128, 2 * NPAR], F32, kind="ExternalInput").ap()
        d["bv"] = dt("bv", [128, 1024], F32, kind="ExternalInput").ap()
        d["flg"] = dt("flg", [128, 4], F32, kind="ExternalInput").ap()
        self.off = 0
        self.arena = nc.alloc_sbuf_tensor("arena", [128, 53100], F32)
        self.cap = 53100 * 4
        self.pbank = 0
        self.nbank = 5
        self.sbi = 0
        self.out_events = []

    def sb(self, shape, dtype, name=None):
        n = 1
        for s in shape:
            n *= s
        nbytes = n * (4 if dtype == F32 else 2)
        nbytes = (nbytes + 31) // 32 * 32
        assert self.off + nbytes <= self.cap, ("SBUF overflow", name, self.off + nbytes)
        w0 = self.off // 4
        ap = self.arena[:, w0:w0 + nbytes // 4]
        if dtype != F32:
            ap = ap.bitcast(BF16)
            ap = ap[:, 0:n]
        else:
            ap = ap[:, 0:n]
        self.off += nbytes
        if len(shape) == 2:
            ap = ap.rearrange("p (a b) -> p a b", a=shape[0])
        elif len(shape) == 3:
            ap = ap.rearrange("p (a b c) -> p a b c", a=shape[0], b=shape[1])
        return ap

    def ring(self, n, shape, dtype, name):
        return Ring([(self.sb(shape, dtype, name), Tl(f"{name}{i}")) for i in range(n)])

    def bank(self):
        self.pbank = (self.pbank + 1) % self.nbank
        b = self.pbank
        return self.ps[b], self.pst[b]

    def mm(self, out, pairs, reads, wt, flags=None):
        pairs = list(pairs)

        def fn(e):
            n = len(pairs)
            ins = None
            for i, (l, r) in enumerate(pairs):
                st, sp = (i == 0, i == n - 1) if flags is None else flags
                ins = e.matmul(out, lhsT=l, rhs=r, start=st, stop=sp, skip_group_check=True)
            return ins
        self.S.add("pe", fn, reads=reads, writes=[wt])

    def dma(self, out, in_, reads, writes, sem):
        return self.S.add("sp", lambda e: e.dma_start(out=out, in_=in_), reads=reads, writes=writes, dma=sem)

    def act(self, out, in_, func, reads, writes, bias=0.0, scale=1.0):
        self.S.add("act", lambda e: e.activation(out=out, in_=in_, func=func, bias=bias, scale=scale),
                   reads=reads, writes=writes)

    def ts(self, eng, out, in0, s1, s2, op0, op1, reads, writes):
        if s2 is None:
            self.S.add(eng, lambda e: e.tensor_scalar(out, in0, s1, None, op0), reads=reads, writes=writes)
        else:
            self.S.add(eng, lambda e: e.tensor_scalar(out, in0, s1, s2, op0, op1), reads=reads, writes=writes)

    def stt(self, eng, out, in0, sc, in1, op0, op1, reads, writes):
        self.S.add(eng, lambda e: e.scalar_tensor_tensor(out, in0, sc, in1, op0, op1), reads=reads, writes=writes)

    def fma(self, eng, out, in0, sc, in1, reads, writes):
        if eng == "dve":
            self.stt("dve", out, in0, sc, in1, ALU.mult, ALU.add, reads, writes)
        else:
            pa, pt = self.ptmp.next()
            shp = list(out.shape)
            pv = pa[:, :shp[-1]]
            self.ts(eng, pv, in0, sc, None, ALU.mult, None, reads, [pt])
            self.tt(eng, out, pv, in1, ALU.add, list(reads) + [pt], writes)

    def tt(self, eng, out, in0, in1, op, reads, writes):
        self.S.add(eng, lambda e: e.tensor_tensor(out, in0, in1, op), reads=reads, writes=writes)

    def cp(self, eng, out, in_, reads, writes):
        if eng == "act":
            self.S.add(eng, lambda e: e.copy(out, in_), reads=reads, writes=writes)
        else:
            self.S.add(eng, lambda e: e.tensor_copy(out, in_), reads=reads, writes=writes)

    def par(self, l, off, n=1):
        return self.PAR[:, l * NPAR + off: l * NPAR + off + n]

    def setup(self):
        nc = self.nc
        d = self.d
        self.ps = [nc.alloc_psum_tensor(f"ps{i}", [128, 512], F32) for i in range(7)]
        self.pst = [Tl(f"ps{i}") for i in range(7)]
        self.psb = nc.alloc_psum_tensor("psb", [128, 1024], BF16)
        self.psbt = [Tl("psb0"), Tl("psb1")]
        self.psbi = 0
        self.X = self.sb([8, 16 * 128], F32, "X")
        self.Xt = [Tl(f"X{t}") for t in range(16)]
        self.PAR = self.sb([2 * NPAR], F32, "PAR")
        self.BV = self.sb([2, 512], F32, "BV")
        self.FLG = self.sb([4], F32, "FLG")
        self.DER = self.sb([2, 5, 44], F32, "DER")
        self.ident = self.sb([128], F32, "ident")
        self.identb = self.sb([128], BF16, "identb")
        self.ones = self.sb([128], BF16, "ones")
        self.epsc = self.sb([1], F32, "epsc")
        self.barbuf = self.sb([1], F32, "barbuf")
        self.tC = Tl("const")
        self.xsave = self.sb([8, 1], F32, "xsave")
        self.xsavet = Tl("xsave")
        self.wring = self.ring(3, [8, 128], BF16, "w8")
        self.xw = self.ring(2, [8, 386], BF16, "xw")
        self.ptmp = self.ring(1, [386], F32, "ptmp")
        self.mark = self.off
        self.kT = self.sb([4, 17 * 128], BF16, "kT")
        self.kTt = [Tl(f"kT{t}") for t in range(17)]
        self.vA = self.sb([17, 8, 68], BF16, "vA")
        self.vAt = [Tl(f"vA{t}") for t in range(17)]
        self.Eint = self.sb([5, 1024], BF16, "Eint")
        self.Eintt = Tl("Eint")
        self.Eedge = self.sb([6, 1024], BF16, "Eedge")
        self.Eedget = Tl("Eedge")
        self.wv = Ring([(self.Eedge.rearrange("p a b -> p (a b)")[:, 0:4096].rearrange("p (k j) -> p k j", k=8), self.Eedget)])
        self.xstage = self.ring(1, [1024], F32, "xstage")
        self.qT = self.ring(1, [4, 2, 384], BF16, "qT")
        self.pexp = self.ring(4, [512], BF16, "pexp")
        self.Pm = self.ring(4, [512], BF16, "Pm")
        self.ya = self.ring(1, [512], BF16, "ya")
        self.rec = self.ring(2, [8], F32, "rec")
        self.yaT = self.ring(1, [4, 384], BF16, "yaT")
        self.cu = self.ring(1, [4, 386], F32, "cu")
        self.tu = self.ring(1, [386], F32, "tu")
        self.cv = self.ring(1, [384], F32, "cv")
        self.yc = self.ring(1, [4, 384], BF16, "yc")
        self.w4ring = self.ring(3, [4, 128], BF16, "w4")
        self.sg = self.ring(2, [384], F32, "sg")
        self.t12 = self.ring(2, [384], F32, "t12")
        self.mrg = self.ring(1, [8, 384], BF16, "mrg")
        self.ztmp = self.tu
        self.zb = self.ring(2, [384], BF16, "zb")
        self.zq = self.ring(2, [384], BF16, "zq")
        self.mean = Ring([self.t12.items[0]])
        self.rstd = Ring([self.t12.items[1]])
        self.msq = self.ring(1, [384], F32, "msq")
        endAB = self.off
        self.off = self.mark
        self.xwc = self.ring(2, [8, 386], BF16, "xwc")
        self.gT = self.ring(2, [22, 384], BF16, "gT")
        self.wdn = self.ring(2, [22, 128], BF16, "wdn")
        self.accg = self.ring(2, [384], F32, "accg")
        self.accv = self.ring(2, [384], F32, "accv")
        self.gl = self.ring(2, [384], F32, "gl")
        self.ostage = self.ring(2, [1024], F32, "ostage")
        self.ztmp2 = self.ring(1, [386], F32, "ztmp2")
        self.zb2 = self.ring(2, [384], BF16, "zb2")
        self.zq2 = self.ring(2, [384], BF16, "zq2")
        self.mean2 = self.ring(1, [384], F32, "mean2")
        self.rstd2 = self.ring(1, [384], F32, "rstd2")
        self.msq2 = self.ring(1, [384], F32, "msq2")
        self.stF = self.ring(4, [1024], F32, "stF")
        self.stB = self.ring(4, [1024], BF16, "stB")
        endC = self.off
        self.off = max(endAB, endC)
        print("SBUF bytes/partition: persistent", self.mark, "B", endAB - self.mark, "C", endC - self.mark, "cap", self.cap)
        self.ab_tiles = self.kTt + self.vAt + [self.Eintt, self.Eedget]
        self.c_tiles = []
        for r in (self.xstage, self.qT, self.pexp, self.Pm, self.ya, self.rec, self.yaT,
                  self.cu, self.tu, self.cv, self.yc, self.w4ring, self.sg, self.t12, self.mrg,
                  self.zb, self.zq, self.msq):
            self.ab_tiles += [t for _, t in r.items]
        for r in (self.xwc, self.gT, self.wdn, self.accg, self.accv, self.gl, self.ostage, self.ztmp2,
                  self.zb2, self.zq2, self.mean2, self.rstd2, self.msq2, self.stF, self.stB):
            self.c_tiles += [t for _, t in r.items]

        S = self.S
        tC = self.tC
        S.add("pool", lambda e: e.memset(self.ones, 1.0 / 1024.0), writes=[tC])
        S.add("pool", lambda e: e.memset(self.epsc, EPS), writes=[tC])
        self.dma(self.PAR, d["par"], [], [tC], tC)
        self.dma(self.BV.rearrange("p a b -> p (a b)"), d["bv"], [], [tC], tC)
        self.dma(self.FLG, d["flg"], [], [tC], tC)
        self.dma(self.ident, d["identin"], [], [tC], tC)
        self.cp("dve", self.identb, self.ident, [tC], [tC])
        for l in range(2):
            w0 = self.par(l, P_FCW, 44)
            w1 = self.par(l, P_FCW + 44, 44)
            w2 = self.par(l, P_FCW + 88, 44)
            bup = self.par(l, P_BUP, 44)
            fcb = self.par(l, P_FCB, 44)
            D_ = self.DER
            self.tt("dve", D_[:, l, 0, :], w0, w1, ALU.add, [tC], [tC])
            self.tt("dve", D_[:, l, 0, :], D_[:, l, 0, :], w2, ALU.add, [tC], [tC])
            self.tt("dve", D_[:, l, 0, :], D_[:, l, 0, :], bup, ALU.mult, [tC], [tC])
            self.tt("dve", D_[:, l, 0, :], D_[:, l, 0, :], fcb, ALU.add, [tC], [tC])
            self.tt("dve", D_[:, l, 1, :], bup, w0, ALU.mult, [tC], [tC])
            self.tt("dve", D_[:, l, 2, :], bup, w2, ALU.mult, [tC], [tC])
            self.ts("dve", D_[:, l, 3, :], D_[:, l, 1, :], self.FLG[:, 2:3], None, ALU.mult, None, [tC], [tC])
            self.ts("dve", D_[:, l, 4, :], D_[:, l, 2, :], self.FLG[:, 3:4], None, ALU.mult, None, [tC], [tC])

    def barrier(self, old_tiles, new_tiles):
        S = self.S
        S.add("pool", lambda e: e.memset(self.barbuf, 0.0), writes=list(old_tiles))
        last = old_tiles[0].w
        for t in new_tiles:
            t.w = last
            t.rs = {}
            t.rd = []

    def prepass(self):
        d = self.d
        jobs = []
        for k, (n, f) in self.wshapes.items():
            for l in range(2):
                for m in range(n):
                    for c0 in range(0, f, 1024):
                        c1 = min(c0 + 1024, f)
                        jobs.append((d[k][l, m][:, c0:c1], d[k + "b"][l, m][:, c0:c1], c1 - c0, "cast"))
        for l in range(2):
            for m in range(NBM):
                jobs.append((d["bm"][l, m], d["eb"][l, m], 1024, "exp"))
        evs = []
        ci = 0
        for i, (src, dst, f, kind) in enumerate(jobs):
            fa, ft = self.stF.next()
            ba, bt = self.stB.next()
            self.dma(fa[:, :f], src, [], [ft], ft)
            if kind == "exp":
                self.act(ba[:, :f], fa[:, :f], AF.Identity, [ft], [bt], scale=8.0)
            else:
                eng = ("dve", "act")[ci % 2]
                ci += 1
                self.cp(eng, ba[:, :f], fa[:, :f], [ft], [bt])
            evs.append(self.dma(dst, ba[:, :f], [bt], [], bt))
        self.S.add("sp", lambda e: e.nop(nofuse=True), extra=evs)

    def load_w8(self, key, l, mc):
        ap, t = self.wring.next()
        self.dma(ap.rearrange("p k j -> p (k j)"), self.d[key + "b"][l, mc], [], [t], t)
        return ap, t

    def load_w4(self, key, l, mc):
        ap, t = self.w4ring.next()
        self.dma(ap.rearrange("p k j -> p (k j)"), self.d[key + "b"][l, mc], [], [t], t)
        return ap, t

    def layer_norm(self, l, c0, n, xt, goff, boff, zb, zq, mean, rstd, msq):
        X = self.X
        pm, pmt = self.bank()
        pq, pqt = self.bank()
        for c in range(8):
            za, zt = zb.next()
            qa, qt = zq.next()
            self.cp("act", za[:, :n], X[:, c, c0:c0 + n], xt, [zt])
            self.act(qa[:, :n], X[:, c, c0:c0 + n], AF.Square, xt, [qt])
            self.mm(pm[:, :n], [(self.ones, za[:, :n])], [zt, self.tC], pmt, flags=(c == 0, c == 7))
            self.mm(pq[:, :n], [(self.ones, qa[:, :n])], [qt, self.tC], pqt, flags=(c == 0, c == 7))
        ma, mt = mean.next()
        ra, rt = rstd.next()
        sa, st = msq.next()
        self.cp("act", ma[:, :n], pm[:, :n], [pmt], [mt])
        self.tt("dve", sa[:, :n], ma[:, :n], ma[:, :n], ALU.mult, [mt], [st])
        self.tt("dve", sa[:, :n], pq[:, :n], sa[:, :n], ALU.subtract, [pqt, st], [st])
        self.ts("dve", sa[:, :n], sa[:, :n], EPS, None, ALU.add, None, [st], [st])
        self.act(ra[:, :n], sa[:, :n], AF.Sqrt, [st], [rt])
        ma2, m2t = self.ptmp.next()
        self.S.add("dve", lambda e: e.reciprocal(ra[:, :n], ra[:, :n]), reads=[rt], writes=[rt])
        self.tt("dve", ma2[:, :n], ra[:, :n], ra[:, :n], ALU.mult, [rt], [m2t])
        self.tt("dve", ma2[:, :n], ma2[:, :n], sa[:, :n], ALU.mult, [m2t, st], [m2t])
        self.ts("dve", ma2[:, :n], ma2[:, :n], -0.5, 1.5, ALU.mult, ALU.add, [m2t], [m2t])
        self.tt("dve", ra[:, :n], ra[:, :n], ma2[:, :n], ALU.mult, [rt, m2t], [rt])
        xv = X[:, :, c0:c0 + n]
        self.tt("dve", xv, xv, ma[:, :n].unsqueeze(1).to_broadcast([128, 8, n]), ALU.subtract, xt + [mt], xt)
        self.tt("dve", xv, xv, ra[:, :n].unsqueeze(1).to_broadcast([128, 8, n]), ALU.mult, xt + [rt], xt)
        for c in range(8):
            eng = "act"
            if eng == "act":
                self.act(X[:, c, c0:c0 + n], X[:, c, c0:c0 + n], AF.Identity, xt + [self.tC], xt,
                         bias=self.par(l, boff + c), scale=self.par(l, goff + c))
            else:
                self.ts("pool", X[:, c, c0:c0 + n], X[:, c, c0:c0 + n], self.par(l, goff + c),
                        self.par(l, boff + c), ALU.mult, ALU.add, xt + [self.tC], xt)

    def chunk(self, kind, xin, yout):
        for l in range(2):
            self.chunk_layer(kind, l, xin, yout)
            if self.stop is not None and self.stop[0] == str(l):
                break

    def xtiles(self, kind, t0, t1):
        xo = kind["xoff"]
        return [self.Xt[t - xo] for t in range(t0, t1) if 0 <= t - xo < 16]

    def kv_from_xb(self, l, xb, xbt, tiles, wvap, wvt):
        n = len(tiles) * 128
        t0 = tiles[0]
        for m in range(4):
            wa, wt = self.load_w8("win", l, 4 + m)
            pb, pbt = self.bank()
            self.mm(pb[:, :n], [(wa[:, k, :], xb[:, k, 0:n]) for k in range(8)], [wt, xbt], pbt)
            self.act(self.kT[:, m, t0 * 128:t0 * 128 + n], pb[:, :n], AF.Identity, [pbt, self.tC],
                     [self.kTt[t] for t in tiles], bias=self.par(l, P_BIN + 4 + m))
        for i, t in enumerate(tiles):
            pb, pbt = self.bank()
            self.mm(pb[:, :], [(xb[:, k, i * 128:(i + 1) * 128], wvap[:, k, :]) for k in range(8)],
                    [wvt, xbt], pbt)
            self.tt("dve", self.vA[:, t, :, 0:64], pb[:, :].rearrange("p (h d) -> p h d", h=8),
                    self.BV[:, l, :].rearrange("p (h d) -> p h d", h=8), ALU.add,
                    [pbt, self.tC], [self.vAt[t]])

    def chunk_layer(self, kind, l, xin, yout):
        L = kind["L"][l]
        xo = kind["xoff"]
        X = self.X
        d = self.d
        a0, a1 = L["A"]
        b0, b1 = L["B"]
        c0_, c1_ = L["C"]
        wvap, wvt = self.wv.next()
        for m_ in range(4):
            self.dma(wvap[:, :, m_ * 128:(m_ + 1) * 128],
                     d["winb"][l, 8 + m_].rearrange("p (k j) -> p k j", k=8), [], [wvt], wvt)
        self.S.add("pool", lambda e: e.memset(self.vA[:, :, :, 64:65], 1.0), writes=self.vAt)
        qa0, qt0 = self.qT.items[0]
        self.S.add("pool", lambda e: e.memset(qa0[0:64, :, 1, :], 0.0), writes=[qt0])
        self.S.add("pool", lambda e: e.memset(qa0[64:128, :, 0, :], 0.0), writes=[qt0])
        self.dma(self.Eint, d["eb"][l, 0:5].rearrange("m p f -> p m f"), [], [self.Eintt], self.Eintt)
        t = a0
        while t < a1:
            tiles = list(range(t, min(t + 3, a1)))
            t += 3
            n = len(tiles) * 128
            xb, xbt = self.xw.next()
            if l == 0:
                for i, tt_ in enumerate(tiles):
                    sa, st = self.xstage.next()
                    self.dma(sa, xin[tt_ * 128:(tt_ + 1) * 128, :], [], [st], st)
                    for hb in range(2):
                        pb, pbt = self.bank()

                        def fn(e, pb=pb, sa=sa, hb=hb):
                            ins = None
                            for c in range(4):
                                ins = e.transpose(out=pb[:, c * 128:(c + 1) * 128],
                                                  in_=sa[:, (hb * 4 + c) * 128:(hb * 4 + c + 1) * 128],
                                                  identity=self.ident)
                            return ins
                        self.S.add("pe", fn, reads=[st, self.tC], writes=[pbt])
                        pv = pb[:, :].rearrange("p (c j) -> p c j", c=4)
                        if 0 <= tt_ - xo < 16:
                            xc = (tt_ - xo) * 128
                            self.cp("dve", X[:, hb * 4:hb * 4 + 4, xc:xc + 128], pv, [pbt], [self.Xt[tt_ - xo]])
                            self.cp("act", xb[:, hb * 4:hb * 4 + 4, i * 128:(i + 1) * 128],
                                    X[:, hb * 4:hb * 4 + 4, xc:xc + 128], [self.Xt[tt_ - xo]], [xbt])
                        else:
                            self.cp("act", xb[:, hb * 4:hb * 4 + 4, i * 128:(i + 1) * 128], pv, [pbt], [xbt])
            else:
                xc = (tiles[0] - xo) * 128
                self.cp("act", xb[:, :, 0:n], X[:, :, xc:xc + n], self.xtiles(kind, tiles[0], tiles[-1] + 1), [xbt])
            self.kv_from_xb(l, xb, xbt, tiles, wvap, wvt)
        if self.stop == f"{l}A":
            return
        wins = split_windows(b0, b1, L["forced"])
        ctx = self.b_pre(kind, l, L, wins[0][0], wins[0][1], True)
        self.b_q(l, ctx)
        for wi, (t0, t1) in enumerate(wins):
            ctx = self.phase_b_window(kind, l, L, ctx, wins[wi + 1] if wi + 1 < len(wins) else None)
        self.barrier(self.ab_tiles, self.c_tiles)
        if self.stop == f"{l}B":
            self.write_out(kind, yout)
            self.barrier(self.c_tiles, self.ab_tiles)
            return
        wins = split_windows(c0_, c1_, L["forced"])
        ctx = self.c_pre(kind, l, L, wins[0][0], wins[0][1], True)
        for wi, (t0, t1) in enumerate(wins):
            ctx = self.phase_c_window(kind, l, L, ctx, wins[wi + 1] if wi + 1 < len(wins) else None)
        if l == 1 or self.stop == f"{l}C":
            self.write_out(kind, yout)
        self.barrier(self.c_tiles, self.ab_tiles)

    def write_out(self, kind, yout):
        X = self.X
        xo = kind["xoff"]
        o0, o1 = kind["out"]
        for t in range(o0, o1):
            xc = (t - xo) * 128
            oa, ot = self.ostage.next()
            for hb in range(2):
                pb, pbt = self.bank()

                def fn(e, pb=pb, hb=hb, xc=xc):
                    ins = None
                    for c in range(4):
                        ins = e.transpose(out=pb[:, c * 128:(c + 1) * 128],
                                          in_=X[:, hb * 4 + c, xc:xc + 128], identity=self.ident)
                    return ins
                self.S.add("pe", fn, reads=[self.Xt[t - xo], self.tC], writes=[pbt])
                self.cp("act" if hb == 0 else "dve", oa[:, hb * 512:(hb + 1) * 512], pb[:, :], [pbt], [ot])
            ev = self.dma(yout[(t - o0) * 128:(t - o0 + 1) * 128, :], oa, [ot], [], ot)
            self.out_events.append(ev)

    def build_window(self, xwa, xwt, kind, t0, t1, first, L, leftx):
        X = self.X
        xo = kind["xoff"]
        n = (t1 - t0) * 128
        xc = (t0 - xo) * 128
        xt = self.xtiles(kind, t0, t1)
        right_ok = (t1 - xo) < 16 and t1 < kind["NT"]
        if right_ok:
            self.cp("act", xwa[:, :, 1:n + 2], X[:, :, xc:xc + n + 1], xt + self.xtiles(kind, t1, t1 + 1), [xwt])
        else:
            self.cp("act", xwa[:, :, 1:n + 1], X[:, :, xc:xc + n], xt, [xwt])
            self.S.add("pool", lambda e: e.memset(xwa[:, :, n + 1:n + 2], 0.0), writes=[xwt])
        if first and leftx and xc > 0:
            self.cp("pool", xwa[:, :, 0:1], X[:, :, xc - 1:xc], self.xtiles(kind, t0 - 1, t0), [xwt])
        elif first:
            self.S.add("pool", lambda e: e.memset(xwa[:, :, 0:1], 0.0), writes=[xwt])
        else:
            self.cp("pool", xwa[:, :, 0:1], self.xsave, [self.xsavet], [xwt])
        return n, xc, xt

    def edge_mode(self, L, t0, t1):
        return L["bl"].get(t0), L["br"].get(t1)

    def b_pre(self, kind, l, L, t0, t1, first):
        tC = self.tC
        xwa, xwt = self.xw.next()
        n, xc, xt = self.build_window(xwa, xwt, kind, t0, t1, first, L, False)
        return dict(xwa=xwa, xwt=xwt, n=n, xc=xc, xt=xt, t0=t0, t1=t1)

    def b_q(self, l, ctx):
        tC = self.tC
        xwa, xwt, n = ctx["xwa"], ctx["xwt"], ctx["n"]
        xm = xwa[:, :, 1:n + 1]
        qa, qt = self.qT.next()
        ctx["qa"], ctx["qt"] = qa, qt
        for m in range(4):
            wa, wt = self.load_w8("win", l, m)
            pb, pbt = self.bank()
            self.mm(pb[:, :n], [(wa[:, k, :], xm[:, k, :]) for k in range(8)], [wt, xwt], pbt)
            for pr in (0, 64):
                self.act(qa[pr:pr + 64, m, pr // 64, :n], pb[pr:pr + 64, :n], AF.Identity, [pbt, tC], [qt],
                         bias=self.PAR[pr:pr + 64, l * NPAR + P_BIN + m:l * NPAR + P_BIN + m + 1])

    def phase_b_window(self, kind, l, L, ctx, nwin):
        X = self.X
        d = self.d
        S = self.S
        tC = self.tC
        xwa, xwt, n, xc, xt, t0, t1 = (ctx[k] for k in ("xwa", "xwt", "n", "xc", "xt", "t0", "t1"))
        qa, qt = ctx["qa"], ctx["qt"]
        el, er = self.edge_mode(L, t0, t1)
        xm = xwa[:, :, 1:n + 1]
        for m in range(0):
            wa, wt = self.load_w8("win", l, m)
            pb, pbt = self.bank()
            self.mm(pb[:, :n], [(wa[:, k, :], xm[:, k, :]) for k in range(8)], [wt, xwt], pbt)
            for pr in (0, 64):
                self.act(qa[pr:pr + 64, m, pr // 64, :n], pb[pr:pr + 64, :n], AF.Identity, [pbt, tC], [qt],
                         bias=self.PAR[pr:pr + 64, l * NPAR + P_BIN + m:l * NPAR + P_BIN + m + 1])
        yTa, yTt = self.yaT.next()
        cua, cut = self.cu.next()
        yca, yct = self.yc.next()
        xh = xwa[:, :, 0:n + 2]
        fillers = []

        def f_ugc(m):
            wa, wt = self.load_w8("win", l, 12 + m)
            pu, put = self.bank()
            self.mm(pu[:, :n + 2], [(wa[:, k, :], xh[:, k, :]) for k in range(8)], [wt, xwt], put)
            wa2, wt2 = self.load_w8("win", l, 20 + m)
            pg, pgt = self.bank()
            self.mm(pg[:, :n + 2], [(wa2[:, k, :], xh[:, k, :]) for k in range(8)], [wt2, xwt], pgt)
            ta, tt_ = self.tu.next()
            self.act(ta[:, :n + 2], pu[:, :n + 2], AF.Identity, [put, tC], [tt_], bias=self.par(l, P_BIN + 12 + m))
            self.stt("dve", cua[:, m, :n + 2], pg[:, :n + 2], self.par(l, P_BIN + 20 + m), ta[:, :n + 2],
                     ALU.add, ALU.mult, [pgt, tt_, tC], [cut])

        def f_edge():
            for mode, col in ((el, 0), (er, n + 1)):
                if mode == "zero":
                    S.add("pool", lambda e, col=col: e.memset(cua[:, :, col:col + 1], 0.0), writes=[cut])
                elif mode in ("ftop", "fbot"):
                    f = self.FLG[:, 0:1] if mode == "ftop" else self.FLG[:, 1:2]
                    for m in range(4):
                        self.ts("dve", cua[:, m, col:col + 1], cua[:, m, col:col + 1], f, None, ALU.mult, None,
                                [cut, tC], [cut])

        def f_conv(m):
            if m == 0:
                f_edge()
            va, vt = self.cv.next()
            w0 = self.par(l, P_SCW + m)
            w1 = self.par(l, P_SCW + 4 + m)
            w2 = self.par(l, P_SCW + 8 + m)
            self.act(va[:, :n], cua[:, m, 1:n + 1], AF.Identity, [cut, tC], [vt], bias=self.par(l, P_SCB + m), scale=w1)
            self.fma("dve", va[:, :n], cua[:, m, 0:n], w0, va[:, :n], [cut, tC, vt], [vt])
            self.fma("dve", va[:, :n], cua[:, m, 2:n + 2], w2, va[:, :n], [cut, tC, vt], [vt])
            wa, wt = self.load_w8("win", l, 16 + m)
            pb, pbt = self.bank()
            self.mm(pb[:, :n], [(wa[:, k, :], xm[:, k, :]) for k in range(8)], [wt, xwt], pbt)
            self.stt("dve", yca[:, m, :n], pb[:, :n], self.par(l, P_BIN + 16 + m), va[:, :n], ALU.add, ALU.mult,
                     [pbt, vt, tC], [yct])
        for m in range(4):
            fillers.append(lambda m=m: f_ugc(m))
        for m in range(4):
            fillers.append(lambda m=m: f_conv(m))

        items = []
        for t in range(t0, t1):
            dl = L["q"][t]
            esrc = [s_ for (_, s_) in dl if s_[0] == "e"]
            for j, (dd, src) in enumerate(dl):
                for hb in range(2):
                    items.append((t, j, dd, src, len(dl), esrc, hb))
        po = [(self.ps[5], self.pst[5]), (self.ps[6], self.pst[6])]
        self.nbank = 3
        self.pbank = 0

        def emit_scores(it):
            t, j, dd, src, nj, esrc, hb = it
            kc = (t + dd) * 128
            qc = (t - t0) * 128
            self.sbi ^= 1
            pbank, pbt = self.ps[3 + self.sbi], self.pst[3 + self.sbi]

            if src[0] == "i":
                ea, et = self.Eint[:, src[1], hb * 512:(hb + 1) * 512], self.Eintt
            else:
                ea, et = self.Eedge[:, src[1] - esrc[0][1], hb * 512:(hb + 1) * 512], self.Eedget
                if j == 0 and hb == 0:
                    i0 = esrc[0][1]
                    ne = len(esrc)
                    self.dma(self.Eedge[:, 0:ne, :], d["eb"][l, i0:i0 + ne].rearrange("m p f -> p m f"),
                             [], [self.Eedget], self.Eedget)

            def fn(e, hb=hb, kc=kc, qc=qc, pbank=pbank, ea=ea):
                ins = e.matmul(pbank[:, :], lhsT=self.identb, rhs=ea, start=True, stop=False, skip_group_check=True)
                for hh in range(4):
                    h = hb * 4 + hh
                    ins = e.matmul(pbank[:, hh * 128:(hh + 1) * 128],
                                   lhsT=self.kT[:, h // 2, kc:kc + 128],
                                   rhs=qa[:, h // 2, h % 2, qc:qc + 128],
                                   start=False, stop=(hh == 3), skip_group_check=True)
                return ins
            S.add("pe", fn, reads=[self.kTt[t + dd], qt, et, tC], writes=[pbt])
            return pbank, pbt

        def finish_tile(t):
            qc = (t - t0) * 128
            ra, rt = self.rec.next()
            ya, yt = self.ya.next()
            for hb in range(2):
                pv = po[hb][0][:, 0:260].rearrange("p (h d) -> p h d", h=4)
                S.add("dve", lambda e, ra=ra, pv=pv, hb=hb: e.reciprocal(ra[:, hb * 4:hb * 4 + 4].unsqueeze(2), pv[:, :, 64:65]),
                      reads=[po[hb][1]], writes=[rt])
                self.tt("dve", ya[:, hb * 256:(hb + 1) * 256].rearrange("p (h d) -> p h d", h=4), pv[:, :, 0:64],
                        ra[:, hb * 4:hb * 4 + 4].unsqueeze(2).to_broadcast([128, 4, 64]), ALU.mult,
                        [po[hb][1], rt], [yt])
            pbb = self.psb[:, 0:512]

            def fn(e, pbb=pbb, ya=ya):
                ins = None
                for c in range(4):
                    ins = e.transpose(out=pbb[:, c * 128:(c + 1) * 128], in_=ya[:, c * 128:(c + 1) * 128],
                                      identity=self.identb)
                return ins
            S.add("pe", fn, reads=[yt, tC], writes=[self.psbt[0]])
            self.cp("act", yTa[:, :, qc:qc + 128], pbb.rearrange("p (c j) -> p c j", c=4), [self.psbt[0]], [yTt])

        pending = emit_scores(items[0])
        for idx, it in enumerate(items):
            t, j, dd, src, nj, esrc, hb = it
            kt = t + dd
            nxt = emit_scores(items[idx + 1]) if idx + 1 < len(items) else None
            ma, mt = self.Pm.next()
            self.act(ma, pending[0][:, :], AF.Exp, [pending[1]], [mt], scale=0.125)

            def fn(e, hb=hb, kt=kt, ma=ma, j=j, nj=nj, pbank=po[hb][0]):
                ins = None
                for hh in range(4):
                    h = hb * 4 + hh
                    ins = e.matmul(pbank[:, hh * 65:(hh + 1) * 65], lhsT=ma[:, hh * 128:(hh + 1) * 128],
                                   rhs=self.vA[:, kt, h, 0:65], start=(j == 0 and hh == 0), stop=(j == nj - 1),
                                   skip_group_check=True)
                return ins
            S.add("pe", fn, reads=[mt, self.vAt[kt]], writes=[po[hb][1]])
            if fillers and hb == 1:
                fillers.pop(0)()
            if j == nj - 1 and hb == 1:
                finish_tile(t)
            pending = nxt
        self.nbank = 5
        while fillers:
            fillers.pop(0)()
        mga, mgt = self.mrg.next()
        for mo in range(8):
            wa, wt = self.load_w4("wba", l, mo)
            pa_, pat = self.bank()
            self.mm(pa_[:, :n], [(wa[:, k, :], yTa[:, k, :n]) for k in range(4)], [wt, yTt], pat)
            wc, wct = self.load_w4("wbc", l, mo)
            pc_, pct = self.bank()
            self.mm(pc_[:, :n], [(wc[:, k, :], yca[:, k, :n]) for k in range(4)], [wct, yct], pct)
            wg, wgt = self.load_w8("win", l, 24 + mo)
            pga, pgat = self.bank()
            self.mm(pga[:, :n], [(wg[:, k, :], xm[:, k, :]) for k in range(8)], [wgt, xwt], pgat)
            wg2, wgt2 = self.load_w8("win", l, 32 + mo)
            pgc, pgct = self.bank()
            self.mm(pgc[:, :n], [(wg2[:, k, :], xm[:, k, :]) for k in range(8)], [wgt2, xwt], pgct)
            s1, s1t = self.sg.next()
            s2, s2t = self.sg.next()
            self.act(s1[:, :n], pga[:, :n], AF.Sigmoid, [pgat, tC], [s1t], bias=self.par(l, P_BIN + 24 + mo))
            self.act(s2[:, :n], pgc[:, :n], AF.Sigmoid, [pgct, tC], [s2t], bias=self.par(l, P_BIN + 32 + mo))
            u1, u1t = self.t12.next()
            u2, u2t = self.t12.next()
            self.tt("dve", u1[:, :n], s1[:, :n], pa_[:, :n], ALU.mult, [s1t, pat], [u1t])
            self.tt("dve", u2[:, :n], s2[:, :n], pc_[:, :n], ALU.mult, [s2t, pct], [u2t])
            self.tt("pool", mga[:, mo, :n], u1[:, :n], u2[:, :n], ALU.add, [u1t, u2t], [mgt])
        self.cp("pool", self.xsave, X[:, :, xc + n - 1:xc + n], xt, [self.xsavet])
        nctx = None
        if nwin is not None:
            nctx = self.b_pre(kind, l, L, nwin[0], nwin[1], False)
        for mo in range(8):
            wa, wt = self.load_w8("wo", l, mo)
            pb, pbt = self.bank()
            self.mm(pb[:, :n], [(wa[:, k, :], mga[:, k, :n]) for k in range(8)], [wt, mgt], pbt)
            za, zt = self.ztmp.next()
            self.act(za[:, :n], pb[:, :n], AF.Identity, [pbt, tC], [zt], bias=self.par(l, P_BO + mo))
            self.stt("dve", X[:, mo, xc:xc + n], X[:, mo, xc:xc + n], ALPHA, za[:, :n], ALU.mult, ALU.add,
                     xt + [zt], xt)
        if nctx is not None:
            self.b_q(l, nctx)
        self.layer_norm(l, xc, n, xt, P_G1, P_B1, self.zb, self.zq, self.mean, self.rstd, self.msq)
        return nctx

    def c_pre(self, kind, l, L, t0, t1, first):
        tC = self.tC
        xwa, xwt = self.xwc.next()
        n, xc, xt = self.build_window(xwa, xwt, kind, t0, t1, first, L, t0 > L["B"][0])
        el, er = self.edge_mode(L, t0, t1)
        for mode, col in ((el, 0), (er, n + 1)):
            if mode in ("ftop", "fbot"):
                f = self.FLG[:, 0:1] if mode == "ftop" else self.FLG[:, 1:2]
                self.ts("dve", xwa[:, :, col:col + 1], xwa[:, :, col:col + 1], f, None, ALU.mult, None,
                        [xwt, tC], [xwt])
        ga, gt = self.gT.next()
        return dict(xwa=xwa, xwt=xwt, n=n, xc=xc, xt=xt, el=el, er=er, ga=ga, gt=gt, m=0)

    def c_iter(self, l, ctx, count):
        tC = self.tC
        DER = self.DER
        xwa, xwt, n, el, er, ga, gt = (ctx[k] for k in ("xwa", "xwt", "n", "el", "er", "ga", "gt"))
        xh = xwa[:, :, 0:n + 2]
        for m in range(ctx["m"], min(22, ctx["m"] + count)):
            wg, wgt = self.load_w8("wup", l, m)
            wv_, wvt_ = self.load_w8("wup", l, 22 + m)
            pg, pgt = self.bank()
            self.mm(pg[:, :n + 2], [(wg[:, k, :], xh[:, k, :]) for k in range(8)], [wgt, xwt], pgt)
            pv, pvt = self.bank()
            self.mm(pv[:, :n + 2], [(wv_[:, k, :], xh[:, k, :]) for k in range(8)], [wvt_, xwt], pvt)
            accs = []
            for (pp, ppt, ring, mc) in ((pg, pgt, self.accg, m), (pv, pvt, self.accv, 22 + m)):
                aa, at = ring.next()
                w0 = self.par(l, P_FCW + mc)
                w1 = self.par(l, P_FCW + 44 + mc)
                w2 = self.par(l, P_FCW + 88 + mc)
                self.act(aa[:, :n], pp[:, 1:n + 1], AF.Identity, [ppt, tC], [at], bias=DER[:, l, 0, mc:mc + 1], scale=w1)
                self.stt("dve", aa[:, :n], pp[:, 0:n], w0, aa[:, :n], ALU.mult, ALU.add, [ppt, at, tC], [at])
                self.stt("dve", aa[:, :n], pp[:, 2:n + 2], w2, aa[:, :n], ALU.mult, ALU.add, [ppt, at, tC], [at])
                if el is not None:
                    j = 1 if el == "zero" else 3
                    self.tt("pool", aa[:, 0:1], aa[:, 0:1], DER[:, l, j, mc:mc + 1], ALU.subtract, [at, tC], [at])
                if er is not None:
                    j = 2 if er == "zero" else 4
                    self.tt("pool", aa[:, n - 1:n], aa[:, n - 1:n], DER[:, l, j, mc:mc + 1], ALU.subtract, [at, tC], [at])
                accs.append((aa, at))
            la, lt = self.gl.next()
            self.act(la[:, :n], accs[0][0][:, :n], AF.Gelu_apprx_tanh, [accs[0][1]], [lt])
            self.tt("pool", ga[:, m, :n], la[:, :n], accs[1][0][:, :n], ALU.mult, [lt, accs[1][1]], [gt])
        ctx["m"] = min(22, ctx["m"] + count)

    def phase_c_window(self, kind, l, L, ctx, nxt):
        X = self.X
        tC = self.tC
        n, xc, xt, ga, gt = (ctx[k] for k in ("n", "xc", "xt", "ga", "gt"))
        self.c_iter(l, ctx, 22)
        self.cp("pool", self.xsave, X[:, :, xc + n - 1:xc + n], xt, [self.xsavet])
        nctx = None
        if nxt is not None:
            nctx = self.c_pre(kind, l, L, nxt[0], nxt[1], False)
        for mo in range(8):
            wa, wt = self.wdn.next()
            self.dma(wa.rearrange("p k j -> p (k j)"), self.d["wdnb"][l, mo], [], [wt], wt)
            pb, pbt = self.bank()
            self.mm(pb[:, :n], [(wa[:, k, :], ga[:, k, :n]) for k in range(22)], [wt, gt], pbt)
            za, zt = self.ztmp2.next()
            self.act(za[:, :n], pb[:, :n], AF.Identity, [pbt, tC], [zt], bias=self.par(l, P_BDN + mo))
            self.stt("dve", X[:, mo, xc:xc + n], X[:, mo, xc:xc + n], ALPHA, za[:, :n], ALU.mult, ALU.add,
                     xt + [zt], xt)
        if nctx is not None:
            self.c_iter(l, nctx, 6)
        self.layer_norm(l, xc, n, xt, P_G2, P_B2, self.zb2, self.zq2, self.mean2, self.rstd2, self.msq2)
        return nctx

    def build(self):
        nc = self.nc
        self.d["identin"] = nc.dram_tensor("identin", [128, 128], F32, kind="ExternalInput").ap()
        self.setup()
        self.prepass()
        self.barrier(self.c_tiles, self.ab_tiles)
        sk = sample_kind()
        for i in range(self.NS if self.stop != 'P' else 0):
            self.chunk(sk, self.d["xs"][i], self.d["ys"][i])
        for j in range(self.NP):
            self.chunk(prompt_kind(j), self.d["xp"][j], self.d["yp"][j])
        self.S.add("sp", lambda e: e.nop(nofuse=True), extra=self.out_events)
        with ExitStack() as stack:
            self.S.emit(nc, stack)
        return nc


class Ring:
    def __init__(self, items):
        self.items = items
        self.i = 0

    def next(self):
        it = self.items[self.i]
        self.i = (self.i + 1) % len(self.items)
        return it


def wlayout(w):
    K, M = w.shape
    return np.ascontiguousarray(w.reshape(K // 128, 128, M // 128, 128).transpose(2, 1, 0, 3)).reshape(
        M // 128, 128, K)


def col(v):
    return np.ascontiguousarray(v.reshape(-1, 128).T)


def bm_tile(rpb_l, i0, r0, R):
    out = np.full((128, 8, 128), NEG, np.float32)
    kc = np.arange(64)[:, None]
    qc = np.arange(64)[None, :]
    js = np.clip(qc - 8, 0, 48)
    valid = (kc >= js) & (kc < js + 16)
    dc = np.clip(kc - qc + 15, 0, 30)
    for kr in range(2):
        for qr in range(2):
            r = r0 + kr
            i = i0 + qr
            if i < 0 or i >= R:
                continue
            rs = min(max(i - 4, 0), R - 8)
            if not (rs <= r < rs + 8):
                continue
            g = rpb_l[:, r - i + 7][:, dc]
            blk = np.where(valid[None], g, np.float32(NEG))
            out[kr * 64:(kr + 1) * 64, :, qr * 64:(qr + 1) * 64] = blk.transpose(1, 0, 2)
    return out.reshape(128, 1024)


def bm_set(rpb_l, half):
    tiles = []
    big = 1000
    def interior(d):
        return bm_tile(rpb_l, 500, 500 + 2 * d, big)
    for d in range(-2, 3):
        tiles.append(interior(d))
    for t, ds in ((0, range(0, 4)), (1, range(-1, 3)), (14, range(-2, 2)), (15, range(-3, 1))):
        for d in ds:
            tiles.append(bm_tile(rpb_l, 2 * t, 2 * (t + d), 32))
    for t, ds in ((5, range(-2, 4)), (6, range(-2, 3))):
        for d in ds:
            if half == 0:
                tiles.append(bm_tile(rpb_l, 2 * (t - 5), 2 * (t - 5 + d), 128))
            else:
                tiles.append(interior(d))
    for t, ds in ((11, range(-2, 3)), (12, range(-3, 3))):
        for d in ds:
            if half == 1:
                tiles.append(bm_tile(rpb_l, 112 + 2 * (t - 5), 112 + 2 * (t - 5 + d), 128))
            else:
                tiles.append(interior(d))
    assert len(tiles) == NBM
    return np.stack(tiles)


def host_params(inp):
    pars = []
    for l in range(2):
        cols = [col(inp["b_in"][l]),
                np.ascontiguousarray(inp["sc_conv_w"][l].reshape(3, 4, 128).transpose(2, 0, 1)).reshape(128, 12),
                col(inp["sc_conv_b"][l]), col(inp["b_o"][l]), col(inp["ln1_g"][l]), col(inp["ln1_b"][l]),
                col(inp["ffn_b_up"][l]),
                np.ascontiguousarray(inp["ffn_conv_w"][l].reshape(3, 44, 128).transpose(2, 0, 1)).reshape(128, 132),
                col(inp["ffn_conv_b"][l]), col(inp["ffn_b_down"][l]), col(inp["ln2_g"][l]), col(inp["ln2_b"][l])]
        p = np.concatenate(cols, axis=1)
        assert p.shape == (128, NPAR)
        pars.append(p)
    par = np.ascontiguousarray(np.concatenate(pars, axis=1), dtype=np.float32)
    bv = np.ascontiguousarray(np.broadcast_to(
        np.stack([inp["b_in"][l][1024:1536] for l in range(2)]).reshape(1, 1024), (128, 1024)), dtype=np.float32)
    return par, bv


def host_weights(inp):
    out = {}
    for key, name in (("win", "w_in"), ("wba", "w_br_attn"), ("wbc", "w_br_conv"), ("wo", "w_o"),
                      ("wup", "ffn_w_up"), ("wdn", "ffn_w_down")):
        out[key] = np.stack([wlayout(np.asarray(inp[name][l], np.float32)) for l in range(2)])
    return out


_CACHE = {}


def get_nc(NS, NP):
    k = (NS, NP)
    if k not in _CACHE:
        _CACHE[k] = Builder(NS, NP).build()
    return _CACHE[k]


def kernel(**inputs):
    inp = {k: np.asarray(v) for k, v in inputs.items()}
    xp_full = inp["x_prompt"]
    xs_full = inp["x_sample"]
    n = 8
    nc = get_nc(4, 4)
    par, bv = host_params(inp)
    W = host_weights(inp)
    ident = np.eye(128, dtype=np.float32)
    bms = [np.stack([bm_set(inp["attn_rpb"][l], h) for l in range(2)]) for h in range(2)]
    in_maps = []
    for c in range(n):
        p, half = c // 2, c % 2
        xpc = np.zeros((4, 17 * 128, D), np.float32)
        for j in range(4):
            r0 = 64 * half + 16 * j
            lo, hi = r0 - 10, r0 + 24
            slo, shi = max(lo, 0), min(hi, 128)
            xpc[j, (slo - lo) * 64:(shi - lo) * 64] = xp_full[p, slo * 64:shi * 64]
        ftop = 0.0 if half == 0 else 1.0
        fbot = 0.0 if half == 1 else 1.0
        flg = np.ascontiguousarray(np.broadcast_to(np.array([ftop, fbot, 1 - ftop, 1 - fbot], np.float32), (128, 4)))
        m = dict(xs=np.ascontiguousarray(xs_full[4 * c:4 * c + 4]), xp=xpc, bm=bms[half], par=par, bv=bv,
                 flg=flg, identin=ident)
        m.update(W)
        in_maps.append(m)
    res = run_bass_kernel_spmd(nc, in_maps, core_ids=list(range(n)))
    y_prompt = np.empty_like(xp_full)
    y_sample = np.empty_like(xs_full)
    for c in range(n):
        r = res.results[c]
        p, half = c // 2, c % 2
        y_sample[4 * c:4 * c + 4] = r["ys"]
        for j in range(4):
            r0 = 64 * half + 16 * j
            y_prompt[p, r0 * 64:(r0 + 16) * 64] = r["yp"][j]
    return (y_prompt, y_sample)
```

```python
import numpy as np
import ml_dtypes
from contextlib import ExitStack
import concourse.bass as bass
import concourse.mybir as mybir
from concourse.bass_utils import run_bass_kernel_spmd

F32 = mybir.dt.float32
BF16 = mybir.dt.bfloat16
AF = mybir.ActivationFunctionType
ALU = mybir.AluOpType

D = 1024
NH = 8
DH = 64
DFF = 2816
ALPHA = float(4 ** 0.25)
EPS = 1e-5
NEG = -30000.0
NPAR = 324
P_BIN, P_SCW, P_SCB, P_BO, P_G1, P_B1, P_BUP, P_FCW, P_FCB, P_BDN, P_G2, P_B2 = (
    0, 40, 52, 56, 64, 72, 80, 124, 256, 300, 308, 316)
NBM = 43

ENGS = ("pe", "act", "dve", "pool", "sp")


class Tl:
    __slots__ = ("name", "w", "rs", "rd", "sem", "cnt")

    def __init__(self, name):
        self.name = name
        self.w = None
        self.rs = {}
        self.rd = []
        self.sem = None
        self.cnt = 0


class Op:
    __slots__ = ("eng", "fn", "waits", "sig", "idx", "sigval", "dsem", "line")


class Sched:
    def __init__(self):
        self.q = {e: [] for e in ENGS}
        self.dma_tiles = []

    def add(self, eng, fn, reads=(), writes=(), dma=None, extra=()):
        op = Op()
        op.eng = eng
        op.fn = fn
        op.sig = False
        op.idx = len(self.q[eng])
        op.dsem = dma
        op.sigval = 0
        import sys as _s
        f = _s._getframe(1)
        ls = []
        while f is not None and len(ls) < 4:
            ls.append(f.f_lineno)
            f = f.f_back
        op.line = ls
        deps = list(extra)
        for t in reads:
            if t.w is not None:
                deps.append(t.w)
        for t in writes:
            if t.w is not None:
                deps.append(t.w)
            deps.extend(t.rs.values())
            deps.extend(t.rd)
        waits = []
        seen = set()
        for d in deps:
            if id(d) in seen:
                continue
            seen.add(id(d))
            if isinstance(d, Op):
                if d.eng == eng:
                    if eng in ("pe", "sp"):
                        continue
                    if op.idx - d.idx > 3:
                        continue
                d.sig = True
            waits.append(d)
        op.waits = waits
        if dma is not None:
            if dma.sem is None:
                dma.sem = True
                self.dma_tiles.append(dma)
            dma.cnt += 16
            ev = (dma, dma.cnt)
        else:
            ev = op
        for t in reads:
            if isinstance(ev, Op):
                t.rs[eng] = ev
            else:
                t.rd.append(ev)
        for t in writes:
            t.w = ev
            t.rs = {}
            t.rd = []
        self.q[eng].append(op)
        return ev

    def emit(self, nc, stack):
        esem = {}
        for e in ("pe", "act", "dve", "pool"):
            esem[e] = stack.enter_context(nc.semaphore("s_" + e))
        for t in self.dma_tiles:
            t.sem = stack.enter_context(nc.semaphore("d_" + t.name))
        for e in ENGS:
            c = 0
            for op in self.q[e]:
                if op.sig:
                    c += 1
                op.sigval = c
        q = self.q

        def run(name, eng):
            waited = {}
            for op in q[name]:
                for d in op.waits:
                    if isinstance(d, Op):
                        sem, val = esem[d.eng], d.sigval
                    else:
                        sem, val = d[0].sem, d[1]
                    k = id(sem)
                    if waited.get(k, 0) < val:
                        eng.wait_ge(sem, val)
                        waited[k] = val
                ins = op.fn(eng)
                if op.dsem is not None:
                    ins.then_inc(op.dsem.sem, 16)
                elif op.sig:
                    ins.then_inc(esem[name], 1)

        with nc.Block() as block:
            @block.tensor
            def _(e):
                run("pe", e)

            @block.scalar
            def _(e):
                run("act", e)

            @block.vector
            def _(e):
                run("dve", e)

            @block.gpsimd
            def _(e):
                run("pool", e)

            @block.sync
            def _(e):
                run("sp", e)


def split_windows(t0, t1, forced=()):
    cuts = sorted(set([t0, t1] + [f for f in forced if t0 < f < t1]))
    out = []
    for a, b in zip(cuts[:-1], cuts[1:]):
        n = b - a
        k = (n + 2) // 3
        base, rem = divmod(n, k)
        s = a
        for i in range(k):
            ln = base + (1 if i < rem else 0)
            out.append((s, s + ln))
            s += ln
    return out


def sample_kind():
    q = {}
    for t in range(16):
        if t == 0:
            q[t] = [(d, ("e", 5 + i)) for i, d in enumerate(range(0, 4))]
        elif t == 1:
            q[t] = [(d, ("e", 9 + i)) for i, d in enumerate(range(-1, 3))]
        elif t == 14:
            q[t] = [(d, ("e", 13 + i)) for i, d in enumerate(range(-2, 2))]
        elif t == 15:
            q[t] = [(d, ("e", 17 + i)) for i, d in enumerate(range(-3, 1))]
        else:
            q[t] = [(d, ("i", d + 2)) for d in range(-2, 3)]
    lay = dict(A=(0, 16), B=(0, 16), C=(0, 16), q=q, forced=(),
               bl={0: "zero"}, br={16: "zero"})
    return dict(NT=16, xoff=0, L=[lay, lay], out=(0, 16))


def prompt_kind(j):
    def qmap(b0, b1, a0, a1):
        q = {}
        for t in range(b0, b1):
            if j == 0 and t == 5:
                lst = [(d, ("e", 21 + i)) for i, d in enumerate(range(-2, 4))]
            elif j == 0 and t == 6:
                lst = [(d, ("e", 27 + i)) for i, d in enumerate(range(-2, 3))]
            elif j == 3 and t == 11:
                lst = [(d, ("e", 32 + i)) for i, d in enumerate(range(-2, 3))]
            elif j == 3 and t == 12:
                lst = [(d, ("e", 37 + i)) for i, d in enumerate(range(-3, 3))]
            else:
                lst = [(d, ("i", d + 2)) for d in range(-2, 3)]
            q[t] = [(d, s) for (d, s) in lst if a0 <= t + d < a1]
        return q
    bl = {5: "ftop"} if j == 0 else {}
    br = {13: "fbot"} if j == 3 else {}
    l0 = dict(A=(0, 17), B=(2, 16), C=(2, 15), q=qmap(2, 16, 0, 17), forced=(5, 13), bl=bl, br=br)
    l1 = dict(A=(2, 15), B=(4, 14), C=(5, 13), q=qmap(4, 14, 2, 15), forced=(5, 13), bl=bl, br=br)
    return dict(NT=17, xoff=2, L=[l0, l1], out=(5, 13))


class Builder:
    def __init__(self, NS, NP, dbg=False, stop=None):
        self.NS, self.NP = NS, NP
        self.stop = stop
        self.bstop = None
        if stop is not None and len(stop) == 3:
            self.bstop = int(stop[2])
            self.stop = stop[:2]
        self.nc = nc = bass.Bass("TRN2", target_bir_lowering=False)
        self.S = Sched()
        dt = nc.dram_tensor
        self.d = d = {}
        if NS:
            d["xs"] = dt("xs", [NS, 2048, D], F32, kind="ExternalInput").ap()
            d["ys"] = dt("ys", [NS, 2048, D], F32, kind="ExternalOutput").ap()
        if NP:
            d["xp"] = dt("xp", [NP, 17 * 128, D], F32, kind="ExternalInput").ap()
            d["yp"] = dt("yp", [NP, 1024, D], F32, kind="ExternalOutput").ap()
        self.wshapes = dict(win=(40, 1024), wba=(8, 512), wbc=(8, 512), wo=(8, 1024),
                            wup=(44, 1024), wdn=(8, 2816))
        for k, (n, f) in self.wshapes.items():
            d[k] = dt(k, [2, n, 128, f], F32, kind="ExternalInput").ap()
            d[k + "b"] = dt(k + "b", [2, n, 128, f], BF16, kind="Internal").ap()
        d["bm"] = dt("bm", [2, NBM, 128, 1024], F32, kind="ExternalInput").ap()
        d["eb"] = dt("eb", [2, NBM, 128, 1024], BF16, kind="Internal").ap()
        d["par"] = dt("par", [128, 2 * NPAR], F32, kind="ExternalInput").ap()
        d["bv"] = dt("bv", [128, 1024], F32, kind="ExternalInput").ap()
        d["flg"] = dt("flg", [128, 4], F32, kind="ExternalInput").ap()
        self.off = 0
        self.arena = nc.alloc_sbuf_tensor("arena", [128, 53100], F32)
        self.cap = 53100 * 4
        self.pbank = 0
        self.nbank = 5
        self.sbi = 0
        self.out_events = []

    def sb(self, shape, dtype, name=None):
        n = 1
        for s in shape:
            n *= s
        nbytes = n * (4 if dtype == F32 else 2)
        nbytes = (nbytes + 31) // 32 * 32
        assert self.off + nbytes <= self.cap, ("SBUF overflow", name, self.off + nbytes)
        w0 = self.off // 4
        ap = self.arena[:, w0:w0 + nbytes // 4]
        if dtype != F32:
            ap = ap.bitcast(BF16)
            ap = ap[:, 0:n]
        else:
            ap = ap[:, 0:n]
        self.off += nbytes
        if len(shape) == 2:
            ap = ap.rearrange("p (a b) -> p a b", a=shape[0])
        elif len(shape) == 3:
            ap = ap.rearrange("p (a b c) -> p a b c", a=shape[0], b=shape[1])
        return ap

    def ring(self, n, shape, dtype, name):
        return Ring([(self.sb(shape, dtype, name), Tl(f"{name}{i}")) for i in range(n)])

    def bank(self):
        self.pbank = (self.pbank + 1) % self.nbank
        b = self.pbank
        return self.ps[b], self.pst[b]

    def mm(self, out, pairs, reads, wt, flags=None):
        pairs = list(pairs)

        def fn(e):
            n = len(pairs)
            ins = None
            for i, (l, r) in enumerate(pairs):
                st, sp = (i == 0, i == n - 1) if flags is None else flags
                ins = e.matmul(out, lhsT=l, rhs=r, start=st, stop=sp, skip_group_check=True)
            return ins
        self.S.add("pe", fn, reads=reads, writes=[wt])

    def dma(self, out, in_, reads, writes, sem):
        return self.S.add("sp", lambda e: e.dma_start(out=out, in_=in_), reads=reads, writes=writes, dma=sem)

    def act(self, out, in_, func, reads, writes, bias=0.0, scale=1.0):
        self.S.add("act", lambda e: e.activation(out=out, in_=in_, func=func, bias=bias, scale=scale),
                   reads=reads, writes=writes)

    def ts(self, eng, out, in0, s1, s2, op0, op1, reads, writes):
        if s2 is None:
            self.S.add(eng, lambda e: e.tensor_scalar(out, in0, s1, None, op0), reads=reads, writes=writes)
        else:
            self.S.add(eng, lambda e: e.tensor_scalar(out, in0, s1, s2, op0, op1), reads=reads, writes=writes)

    def stt(self, eng, out, in0, sc, in1, op0, op1, reads, writes):
        self.S.add(eng, lambda e: e.scalar_tensor_tensor(out, in0, sc, in1, op0, op1), reads=reads, writes=writes)

    def fma(self, eng, out, in0, sc, in1, reads, writes):
        if eng == "dve":
            self.stt("dve", out, in0, sc, in1, ALU.mult, ALU.add, reads, writes)
        else:
            pa, pt = self.ptmp.next()
            shp = list(out.shape)
            pv = pa[:, :shp[-1]]
            self.ts(eng, pv, in0, sc, None, ALU.mult, None, reads, [pt])
            self.tt(eng, out, pv, in1, ALU.add, list(reads) + [pt], writes)

    def tt(self, eng, out, in0, in1, op, reads, writes):
        self.S.add(eng, lambda e: e.tensor_tensor(out, in0, in1, op), reads=reads, writes=writes)

    def cp(self, eng, out, in_, reads, writes):
        if eng == "act":
            self.S.add(eng, lambda e: e.copy(out, in_), reads=reads, writes=writes)
        else:
            self.S.add(eng, lambda e: e.tensor_copy(out, in_), reads=reads, writes=writes)

    def par(self, l, off, n=1):
        return self.PAR[:, l * NPAR + off: l * NPAR + off + n]

    def setup(self):
        nc = self.nc
        d = self.d
        self.ps = [nc.alloc_psum_tensor(f"ps{i}", [128, 512], F32) for i in range(7)]
        self.pst = [Tl(f"ps{i}") for i in range(7)]
        self.psb = nc.alloc_psum_tensor("psb", [128, 1024], BF16)
        self.psbt = [Tl("psb0"), Tl("psb1")]
        self.psbi = 0
        self.X = self.sb([8, 16 * 128], F32, "X")
        self.Xt = [Tl(f"X{t}") for t in range(16)]
        self.PAR = self.sb([2 * NPAR], F32, "PAR")
        self.BV = self.sb([2, 512], F32, "BV")
        self.FLG = self.sb([4], F32, "FLG")
        self.DER = self.sb([2, 5, 44], F32, "DER")
        self.ident = self.sb([128], F32, "ident")
        self.identb = self.sb([128], BF16, "identb")
        self.ones = self.sb([128], BF16, "ones")
        self.epsc = self.sb([1], F32, "epsc")
        self.barbuf = self.sb([1], F32, "barbuf")
        self.tC = Tl("const")
        self.xsave = self.sb([8, 1], F32, "xsave")
        self.xsavet = Tl("xsave")
        self.wring = self.ring(3, [8, 128], BF16, "w8")
        self.xw = self.ring(2, [8, 386], BF16, "xw")
        self.ptmp = self.ring(1, [386], F32, "ptmp")
        self.mark = self.off
        self.kT = self.sb([4, 17 * 128], BF16, "kT")
        self.kTt = [Tl(f"kT{t}") for t in range(17)]
        self.vA = self.sb([17, 8, 68], BF16, "vA")
        self.vAt = [Tl(f"vA{t}") for t in range(17)]
        self.Eint = self.sb([5, 1024], BF16, "Eint")
        self.Eintt = Tl("Eint")
        self.Eedge = self.sb([6, 1024], BF16, "Eedge")
        self.Eedget = Tl("Eedge")
        self.wv = Ring([(self.Eedge.rearrange("p a b -> p (a b)")[:, 0:4096].rearrange("p (k j) -> p k j", k=8), self.Eedget)])
        self.xstage = self.ring(1, [1024], F32, "xstage")
        self.qT = self.ring(1, [4, 2, 384], BF16, "qT")
        self.pexp = self.ring(4, [512], BF16, "pexp")
        self.Pm = self.ring(4, [512], BF16, "Pm")
        self.ya = self.ring(1, [512], BF16, "ya")
        self.rec = self.ring(2, [8], F32, "rec")
        self.yaT = self.ring(1, [4, 384], BF16, "yaT")
        self.cu = self.ring(1, [4, 386], F32, "cu")
        self.tu = self.ring(1, [386], F32, "tu")
        self.cv = self.ring(1, [384], F32, "cv")
        self.yc = self.ring(1, [4, 384], BF16, "yc")
        self.w4ring = self.ring(3, [4, 128], BF16, "w4")
        self.sg = self.ring(2, [384], F32, "sg")
        self.t12 = self.ring(2, [384], F32, "t12")
        self.mrg = self.ring(1, [8, 384], BF16, "mrg")
        self.ztmp = self.tu
        self.zb = self.ring(2, [384], BF16, "zb")
        self.zq = self.ring(2, [384], BF16, "zq")
        self.mean = Ring([self.t12.items[0]])
        self.rstd = Ring([self.t12.items[1]])
        self.msq = self.ring(1, [384], F32, "msq")
        endAB = self.off
        self.off = self.mark
        self.xwc = self.ring(2, [8, 386], BF16, "xwc")
        self.gT = self.ring(2, [22, 384], BF16, "gT")
        self.wdn = self.ring(2, [22, 128], BF16, "wdn")
        self.accg = self.ring(2, [384], F32, "accg")
        self.accv = self.ring(2, [384], F32, "accv")
        self.gl = self.ring(2, [384], F32, "gl")
        self.ostage = self.ring(2, [1024], F32, "ostage")
        self.ztmp2 = self.ring(1, [386], F32, "ztmp2")
        self.zb2 = self.ring(2, [384], BF16, "zb2")
        self.zq2 = self.ring(2, [384], BF16, "zq2")
        self.mean2 = self.ring(1, [384], F32, "mean2")
        self.rstd2 = self.ring(1, [384], F32, "rstd2")
        self.msq2 = self.ring(1, [384], F32, "msq2")
        self.stF = self.ring(4, [1024], F32, "stF")
        self.stB = self.ring(4, [1024], BF16, "stB")
        endC = self.off
        self.off = max(endAB, endC)
        print("SBUF bytes/partition: persistent", self.mark, "B", endAB - self.mark, "C", endC - self.mark, "cap", self.cap)
        self.ab_tiles = self.kTt + self.vAt + [self.Eintt, self.Eedget]
        self.c_tiles = []
        for r in (self.xstage, self.qT, self.pexp, self.Pm, self.ya, self.rec, self.yaT,
                  self.cu, self.tu, self.cv, self.yc, self.w4ring, self.sg, self.t12, self.mrg,
                  self.zb, self.zq, self.msq):
            self.ab_tiles += [t for _, t in r.items]
        for r in (self.xwc, self.gT, self.wdn, self.accg, self.accv, self.gl, self.ostage, self.ztmp2,
                  self.zb2, self.zq2, self.mean2, self.rstd2, self.msq2, self.stF, self.stB):
            self.c_tiles += [t for _, t in r.items]

        S = self.S
        tC = self.tC
        S.add("pool", lambda e: e.memset(self.ones, 1.0 / 1024.0), writes=[tC])
        S.add("pool", lambda e: e.memset(self.epsc, EPS), writes=[tC])
        self.dma(self.PAR, d["par"], [], [tC], tC)
        self.dma(self.BV.rearrange("p a b -> p (a b)"), d["bv"], [], [tC], tC)
        self.dma(self.FLG, d["flg"], [], [tC], tC)
        self.dma(self.ident, d["identin"], [], [tC], tC)
        self.cp("dve", self.identb, self.ident, [tC], [tC])
        for l in range(2):
            w0 = self.par(l, P_FCW, 44)
            w1 = self.par(l, P_FCW + 44, 44)
            w2 = self.par(l, P_FCW + 88, 44)
            bup = self.par(l, P_BUP, 44)
            fcb = self.par(l, P_FCB, 44)
            D_ = self.DER
            self.tt("dve", D_[:, l, 0, :], w0, w1, ALU.add, [tC], [tC])
            self.tt("dve", D_[:, l, 0, :], D_[:, l, 0, :], w2, ALU.add, [tC], [tC])
            self.tt("dve", D_[:, l, 0, :], D_[:, l, 0, :], bup, ALU.mult, [tC], [tC])
            self.tt("dve", D_[:, l, 0, :], D_[:, l, 0, :], fcb, ALU.add, [tC], [tC])
            self.tt("dve", D_[:, l, 1, :], bup, w0, ALU.mult, [tC], [tC])
            self.tt("dve", D_[:, l, 2, :], bup, w2, ALU.mult, [tC], [tC])
            self.ts("dve", D_[:, l, 3, :], D_[:, l, 1, :], self.FLG[:, 2:3], None, ALU.mult, None, [tC], [tC])
            self.ts("dve", D_[:, l, 4, :], D_[:, l, 2, :], self.FLG[:, 3:4], None, ALU.mult, None, [tC], [tC])

    def barrier(self, old_tiles, new_tiles):
        S = self.S
        S.add("pool", lambda e: e.memset(self.barbuf, 0.0), writes=list(old_tiles))
        last = old_tiles[0].w
        for t in new_tiles:
            t.w = last
            t.rs = {}
            t.rd = []

    def prepass(self):
        d = self.d
        jobs = []
        for k, (n, f) in self.wshapes.items():
            for l in range(2):
                for m in range(n):
                    for c0 in range(0, f, 1024):
                        c1 = min(c0 + 1024, f)
                        jobs.append((d[k][l, m][:, c0:c1], d[k + "b"][l, m][:, c0:c1], c1 - c0, "cast"))
        for l in range(2):
            for m in range(NBM):
                jobs.append((d["bm"][l, m], d["eb"][l, m], 1024, "exp"))
        evs = []
        ci = 0
        LOOK = 3
        staged = {}

        def issue_in(i):
            src, dst, f, kind = jobs[i]
            fa, ft = self.stF.next()
            self.dma(fa[:, :f], src, [], [ft], ft)
            staged[i] = (fa, ft)
        for i in range(min(LOOK, len(jobs))):
            issue_in(i)
        for i, (src, dst, f, kind) in enumerate(jobs):
            if i + LOOK < len(jobs):
                issue_in(i + LOOK)
            fa, ft = staged.pop(i)
            ba, bt = self.stB.next()
            if kind == "exp":
                self.act(ba[:, :f], fa[:, :f], AF.Identity, [ft], [bt], scale=8.0)
            else:
                eng = ("dve", "act")[ci % 2]
                ci += 1
                self.cp(eng, ba[:, :f], fa[:, :f], [ft], [bt])
            evs.append(self.dma(dst, ba[:, :f], [bt], [], bt))
        self.S.add("sp", lambda e: e.nop(nofuse=True), extra=evs)

    def load_w8(self, key, l, mc):
        ap, t = self.wring.next()
        self.dma(ap.rearrange("p k j -> p (k j)"), self.d[key + "b"][l, mc], [], [t], t)
        return ap, t

    def load_w4(self, key, l, mc):
        ap, t = self.w4ring.next()
        self.dma(ap.rearrange("p k j -> p (k j)"), self.d[key + "b"][l, mc], [], [t], t)
        return ap, t

    def layer_norm(self, l, c0, n, xt, goff, boff, zb, zq, mean, rstd, msq):
        X = self.X
        pm, pmt = self.bank()
        pq, pqt = self.bank()
        for c in range(8):
            za, zt = zb.next()
            qa, qt = zq.next()
            self.cp("act", za[:, :n], X[:, c, c0:c0 + n], xt, [zt])
            self.act(qa[:, :n], X[:, c, c0:c0 + n], AF.Square, xt, [qt])
            self.mm(pm[:, :n], [(self.ones, za[:, :n])], [zt, self.tC], pmt, flags=(c == 0, c == 7))
            self.mm(pq[:, :n], [(self.ones, qa[:, :n])], [qt, self.tC], pqt, flags=(c == 0, c == 7))
        ma, mt = mean.next()
        ra, rt = rstd.next()
        sa, st = msq.next()
        self.cp("act", ma[:, :n], pm[:, :n], [pmt], [mt])
        self.tt("dve", sa[:, :n], ma[:, :n], ma[:, :n], ALU.mult, [mt], [st])
        self.tt("dve", sa[:, :n], pq[:, :n], sa[:, :n], ALU.subtract, [pqt, st], [st])
        self.ts("dve", sa[:, :n], sa[:, :n], EPS, None, ALU.add, None, [st], [st])
        self.act(ra[:, :n], sa[:, :n], AF.Sqrt, [st], [rt])
        ma2, m2t = self.ptmp.next()
        self.S.add("dve", lambda e: e.reciprocal(ra[:, :n], ra[:, :n]), reads=[rt], writes=[rt])
        self.tt("dve", ma2[:, :n], ra[:, :n], ra[:, :n], ALU.mult, [rt], [m2t])
        self.tt("dve", ma2[:, :n], ma2[:, :n], sa[:, :n], ALU.mult, [m2t, st], [m2t])
        self.ts("dve", ma2[:, :n], ma2[:, :n], -0.5, 1.5, ALU.mult, ALU.add, [m2t], [m2t])
        self.tt("dve", ra[:, :n], ra[:, :n], ma2[:, :n], ALU.mult, [rt, m2t], [rt])
        xv = X[:, :, c0:c0 + n]
        self.tt("dve", xv, xv, ma[:, :n].unsqueeze(1).to_broadcast([128, 8, n]), ALU.subtract, xt + [mt], xt)
        self.tt("dve", xv, xv, ra[:, :n].unsqueeze(1).to_broadcast([128, 8, n]), ALU.mult, xt + [rt], xt)
        for c in range(8):
            eng = "act"
            if eng == "act":
                self.act(X[:, c, c0:c0 + n], X[:, c, c0:c0 + n], AF.Identity, xt + [self.tC], xt,
                         bias=self.par(l, boff + c), scale=self.par(l, goff + c))
            else:
                self.ts("pool", X[:, c, c0:c0 + n], X[:, c, c0:c0 + n], self.par(l, goff + c),
                        self.par(l, boff + c), ALU.mult, ALU.add, xt + [self.tC], xt)

    def chunk(self, kind, xin, yout):
        for l in range(2):
            self.chunk_layer(kind, l, xin, yout)
            if self.stop is not None and self.stop[0] == str(l):
                break

    def xtiles(self, kind, t0, t1):
        xo = kind["xoff"]
        return [self.Xt[t - xo] for t in range(t0, t1) if 0 <= t - xo < 16]

    def kv_from_xb(self, l, xb, xbt, tiles, wvap, wvt):
        n = len(tiles) * 128
        t0 = tiles[0]
        for m in range(4):
            wa, wt = self.load_w8("win", l, 4 + m)
            pb, pbt = self.bank()
            self.mm(pb[:, :n], [(wa[:, k, :], xb[:, k, 0:n]) for k in range(8)], [wt, xbt], pbt)
            self.act(self.kT[:, m, t0 * 128:t0 * 128 + n], pb[:, :n], AF.Identity, [pbt, self.tC],
                     [self.kTt[t] for t in tiles], bias=self.par(l, P_BIN + 4 + m))
        for i, t in enumerate(tiles):
            pb, pbt = self.bank()
            self.mm(pb[:, :], [(xb[:, k, i * 128:(i + 1) * 128], wvap[:, k, :]) for k in range(8)],
                    [wvt, xbt], pbt)
            self.tt("dve", self.vA[:, t, :, 0:64], pb[:, :].rearrange("p (h d) -> p h d", h=8),
                    self.BV[:, l, :].rearrange("p (h d) -> p h d", h=8), ALU.add,
                    [pbt, self.tC], [self.vAt[t]])

    def chunk_layer(self, kind, l, xin, yout):
        L = kind["L"][l]
        xo = kind["xoff"]
        X = self.X
        d = self.d
        a0, a1 = L["A"]
        b0, b1 = L["B"]
        c0_, c1_ = L["C"]
        wvap, wvt = self.wv.next()
        for m_ in range(4):
            self.dma(wvap[:, :, m_ * 128:(m_ + 1) * 128],
                     d["winb"][l, 8 + m_].rearrange("p (k j) -> p k j", k=8), [], [wvt], wvt)
        self.S.add("pool", lambda e: e.memset(self.vA[:, :, :, 64:65], 1.0), writes=self.vAt)
        qa0, qt0 = self.qT.items[0]
        self.S.add("pool", lambda e: e.memset(qa0[0:64, :, 1, :], 0.0), writes=[qt0])
        self.S.add("pool", lambda e: e.memset(qa0[64:128, :, 0, :], 0.0), writes=[qt0])
        self.dma(self.Eint, d["eb"][l, 0:5].rearrange("m p f -> p m f"), [], [self.Eintt], self.Eintt)
        t = a0
        while t < a1:
            tiles = list(range(t, min(t + 3, a1)))
            t += 3
            n = len(tiles) * 128
            xb, xbt = self.xw.next()
            if l == 0:
                for i, tt_ in enumerate(tiles):
                    sa, st = self.xstage.next()
                    self.dma(sa, xin[tt_ * 128:(tt_ + 1) * 128, :], [], [st], st)
                    for hb in range(2):
                        pb, pbt = self.bank()

                        def fn(e, pb=pb, sa=sa, hb=hb):
                            ins = None
                            for c in range(4):
                                ins = e.transpose(out=pb[:, c * 128:(c + 1) * 128],
                                                  in_=sa[:, (hb * 4 + c) * 128:(hb * 4 + c + 1) * 128],
                                                  identity=self.ident)
                            return ins
                        self.S.add("pe", fn, reads=[st, self.tC], writes=[pbt])
                        pv = pb[:, :].rearrange("p (c j) -> p c j", c=4)
                        if 0 <= tt_ - xo < 16:
                            xc = (tt_ - xo) * 128
                            self.cp("dve", X[:, hb * 4:hb * 4 + 4, xc:xc + 128], pv, [pbt], [self.Xt[tt_ - xo]])
                            self.cp("act", xb[:, hb * 4:hb * 4 + 4, i * 128:(i + 1) * 128],
                                    X[:, hb * 4:hb * 4 + 4, xc:xc + 128], [self.Xt[tt_ - xo]], [xbt])
                        else:
                            self.cp("act", xb[:, hb * 4:hb * 4 + 4, i * 128:(i + 1) * 128], pv, [pbt], [xbt])
            else:
                xc = (tiles[0] - xo) * 128
                self.cp("act", xb[:, :, 0:n], X[:, :, xc:xc + n], self.xtiles(kind, tiles[0], tiles[-1] + 1), [xbt])
            self.kv_from_xb(l, xb, xbt, tiles, wvap, wvt)
        if self.stop == f"{l}A":
            return
        wins = split_windows(b0, b1, L["forced"])
        ctx = self.b_pre(kind, l, L, wins[0][0], wins[0][1], True)
        self.b_q(l, ctx)
        for wi, (t0, t1) in enumerate(wins):
            ctx = self.phase_b_window(kind, l, L, ctx, wins[wi + 1] if wi + 1 < len(wins) else None)
        self.barrier(self.ab_tiles, self.c_tiles)
        if self.stop == f"{l}B":
            self.write_out(kind, yout)
            self.barrier(self.c_tiles, self.ab_tiles)
            return
        wins = split_windows(c0_, c1_, L["forced"])
        ctx = self.c_pre(kind, l, L, wins[0][0], wins[0][1], True)
        for wi, (t0, t1) in enumerate(wins):
            ctx = self.phase_c_window(kind, l, L, ctx, wins[wi + 1] if wi + 1 < len(wins) else None)
        if l == 1 or self.stop == f"{l}C":
            self.write_out(kind, yout)
        self.barrier(self.c_tiles, self.ab_tiles)

    def write_out(self, kind, yout):
        X = self.X
        xo = kind["xoff"]
        o0, o1 = kind["out"]
        for t in range(o0, o1):
            xc = (t - xo) * 128
            oa, ot = self.ostage.next()
            for hb in range(2):
                pb, pbt = self.bank()

                def fn(e, pb=pb, hb=hb, xc=xc):
                    ins = None
                    for c in range(4):
                        ins = e.transpose(out=pb[:, c * 128:(c + 1) * 128],
                                          in_=X[:, hb * 4 + c, xc:xc + 128], identity=self.ident)
                    return ins
                self.S.add("pe", fn, reads=[self.Xt[t - xo], self.tC], writes=[pbt])
                self.cp("act" if hb == 0 else "dve", oa[:, hb * 512:(hb + 1) * 512], pb[:, :], [pbt], [ot])
            ev = self.dma(yout[(t - o0) * 128:(t - o0 + 1) * 128, :], oa, [ot], [], ot)
            self.out_events.append(ev)

    def build_window(self, xwa, xwt, kind, t0, t1, first, L, leftx):
        X = self.X
        xo = kind["xoff"]
        n = (t1 - t0) * 128
        xc = (t0 - xo) * 128
        xt = self.xtiles(kind, t0, t1)
        right_ok = (t1 - xo) < 16 and t1 < kind["NT"]
        if right_ok:
            self.cp("act", xwa[:, :, 1:n + 2], X[:, :, xc:xc + n + 1], xt + self.xtiles(kind, t1, t1 + 1), [xwt])
        else:
            self.cp("act", xwa[:, :, 1:n + 1], X[:, :, xc:xc + n], xt, [xwt])
            self.S.add("pool", lambda e: e.memset(xwa[:, :, n + 1:n + 2], 0.0), writes=[xwt])
        if first and leftx and xc > 0:
            self.cp("pool", xwa[:, :, 0:1], X[:, :, xc - 1:xc], self.xtiles(kind, t0 - 1, t0), [xwt])
        elif first:
            self.S.add("pool", lambda e: e.memset(xwa[:, :, 0:1], 0.0), writes=[xwt])
        else:
            self.cp("pool", xwa[:, :, 0:1], self.xsave, [self.xsavet], [xwt])
        return n, xc, xt

    def edge_mode(self, L, t0, t1):
        return L["bl"].get(t0), L["br"].get(t1)

    def b_pre(self, kind, l, L, t0, t1, first):
        tC = self.tC
        xwa, xwt = self.xw.next()
        n, xc, xt = self.build_window(xwa, xwt, kind, t0, t1, first, L, False)
        return dict(xwa=xwa, xwt=xwt, n=n, xc=xc, xt=xt, t0=t0, t1=t1)

    def b_q(self, l, ctx):
        tC = self.tC
        xwa, xwt, n = ctx["xwa"], ctx["xwt"], ctx["n"]
        xm = xwa[:, :, 1:n + 1]
        qa, qt = self.qT.next()
        ctx["qa"], ctx["qt"] = qa, qt
        for m in range(4):
            wa, wt = self.load_w8("win", l, m)
            pb, pbt = self.bank()
            self.mm(pb[:, :n], [(wa[:, k, :], xm[:, k, :]) for k in range(8)], [wt, xwt], pbt)
            for pr in (0, 64):
                self.act(qa[pr:pr + 64, m, pr // 64, :n], pb[pr:pr + 64, :n], AF.Identity, [pbt, tC], [qt],
                         bias=self.PAR[pr:pr + 64, l * NPAR + P_BIN + m:l * NPAR + P_BIN + m + 1])

    def phase_b_window(self, kind, l, L, ctx, nwin):
        X = self.X
        d = self.d
        S = self.S
        tC = self.tC
        xwa, xwt, n, xc, xt, t0, t1 = (ctx[k] for k in ("xwa", "xwt", "n", "xc", "xt", "t0", "t1"))
        qa, qt = ctx["qa"], ctx["qt"]
        el, er = self.edge_mode(L, t0, t1)
        xm = xwa[:, :, 1:n + 1]
        for m in range(0):
            wa, wt = self.load_w8("win", l, m)
            pb, pbt = self.bank()
            self.mm(pb[:, :n], [(wa[:, k, :], xm[:, k, :]) for k in range(8)], [wt, xwt], pbt)
            for pr in (0, 64):
                self.act(qa[pr:pr + 64, m, pr // 64, :n], pb[pr:pr + 64, :n], AF.Identity, [pbt, tC], [qt],
                         bias=self.PAR[pr:pr + 64, l * NPAR + P_BIN + m:l * NPAR + P_BIN + m + 1])
        yTa, yTt = self.yaT.next()
        cua, cut = self.cu.next()
        yca, yct = self.yc.next()
        xh = xwa[:, :, 0:n + 2]
        fillers = []

        def f_ugc(m):
            wa, wt = self.load_w8("win", l, 12 + m)
            pu, put = self.bank()
            self.mm(pu[:, :n + 2], [(wa[:, k, :], xh[:, k, :]) for k in range(8)], [wt, xwt], put)
            wa2, wt2 = self.load_w8("win", l, 20 + m)
            pg, pgt = self.bank()
            self.mm(pg[:, :n + 2], [(wa2[:, k, :], xh[:, k, :]) for k in range(8)], [wt2, xwt], pgt)
            ta, tt_ = self.tu.next()
            self.act(ta[:, :n + 2], pu[:, :n + 2], AF.Identity, [put, tC], [tt_], bias=self.par(l, P_BIN + 12 + m))
            self.stt("dve", cua[:, m, :n + 2], pg[:, :n + 2], self.par(l, P_BIN + 20 + m), ta[:, :n + 2],
                     ALU.add, ALU.mult, [pgt, tt_, tC], [cut])

        def f_edge():
            for mode, col in ((el, 0), (er, n + 1)):
                if mode == "zero":
                    S.add("pool", lambda e, col=col: e.memset(cua[:, :, col:col + 1], 0.0), writes=[cut])
                elif mode in ("ftop", "fbot"):
                    f = self.FLG[:, 0:1] if mode == "ftop" else self.FLG[:, 1:2]
                    for m in range(4):
                        self.ts("dve", cua[:, m, col:col + 1], cua[:, m, col:col + 1], f, None, ALU.mult, None,
                                [cut, tC], [cut])

        def f_conv(m):
            if m == 0:
                f_edge()
            va, vt = self.cv.next()
            w0 = self.par(l, P_SCW + m)
            w1 = self.par(l, P_SCW + 4 + m)
            w2 = self.par(l, P_SCW + 8 + m)
            self.act(va[:, :n], cua[:, m, 1:n + 1], AF.Identity, [cut, tC], [vt], bias=self.par(l, P_SCB + m), scale=w1)
            self.fma("dve", va[:, :n], cua[:, m, 0:n], w0, va[:, :n], [cut, tC, vt], [vt])
            self.fma("dve", va[:, :n], cua[:, m, 2:n + 2], w2, va[:, :n], [cut, tC, vt], [vt])
            wa, wt = self.load_w8("win", l, 16 + m)
            pb, pbt = self.bank()
            self.mm(pb[:, :n], [(wa[:, k, :], xm[:, k, :]) for k in range(8)], [wt, xwt], pbt)
            self.stt("dve", yca[:, m, :n], pb[:, :n], self.par(l, P_BIN + 16 + m), va[:, :n], ALU.add, ALU.mult,
                     [pbt, vt, tC], [yct])
        for m in range(4):
            fillers.append(lambda m=m: f_ugc(m))
        for m in range(4):
            fillers.append(lambda m=m: f_conv(m))

        items = []
        pre_edge = None
        for t in range(t0, t1):
            dl = L["q"][t]
            esrc = [s_ for (_, s_) in dl if s_[0] == "e"]
            if esrc and pre_edge is None:
                pre_edge = t
                self.dma(self.Eedge[:, 0:len(esrc), :],
                         d["eb"][l, esrc[0][1]:esrc[0][1] + len(esrc)].rearrange("m p f -> p m f"),
                         [], [self.Eedget], self.Eedget)
            for j, (dd, src) in enumerate(dl):
                for hb in range(2):
                    items.append((t, j, dd, src, len(dl), esrc, hb))
        po = [(self.ps[5], self.pst[5]), (self.ps[6], self.pst[6])]
        self.nbank = 3
        self.pbank = 0

        def emit_scores(it):
            t, j, dd, src, nj, esrc, hb = it
            kc = (t + dd) * 128
            qc = (t - t0) * 128
            self.sbi ^= 1
            pbank, pbt = self.ps[3 + self.sbi], self.pst[3 + self.sbi]

            if src[0] == "i":
                ea, et = self.Eint[:, src[1], hb * 512:(hb + 1) * 512], self.Eintt
            else:
                ea, et = self.Eedge[:, src[1] - esrc[0][1], hb * 512:(hb + 1) * 512], self.Eedget
                if j == 0 and hb == 0 and t != pre_edge:
                    i0 = esrc[0][1]
                    ne = len(esrc)
                    self.dma(self.Eedge[:, 0:ne, :], d["eb"][l, i0:i0 + ne].rearrange("m p f -> p m f"),
                             [], [self.Eedget], self.Eedget)

            def fn(e, hb=hb, kc=kc, qc=qc, pbank=pbank, ea=ea):
                ins = e.matmul(pbank[:, :], lhsT=self.identb, rhs=ea, start=True, stop=False, skip_group_check=True)
                for hh in range(4):
                    h = hb * 4 + hh
                    ins = e.matmul(pbank[:, hh * 128:(hh + 1) * 128],
                                   lhsT=self.kT[:, h // 2, kc:kc + 128],
                                   rhs=qa[:, h // 2, h % 2, qc:qc + 128],
                                   start=False, stop=(hh == 3), skip_group_check=True)
                return ins
            S.add("pe", fn, reads=[self.kTt[t + dd], qt, et, tC], writes=[pbt])
            return pbank, pbt

        def finish_tile(t):
            qc = (t - t0) * 128
            ra, rt = self.rec.next()
            ya, yt = self.ya.next()
            for hb in range(2):
                pv = po[hb][0][:, 0:260].rearrange("p (h d) -> p h d", h=4)
                S.add("dve", lambda e, ra=ra, pv=pv, hb=hb: e.reciprocal(ra[:, hb * 4:hb * 4 + 4].unsqueeze(2), pv[:, :, 64:65]),
                      reads=[po[hb][1]], writes=[rt])
                self.tt("dve", ya[:, hb * 256:(hb + 1) * 256].rearrange("p (h d) -> p h d", h=4), pv[:, :, 0:64],
                        ra[:, hb * 4:hb * 4 + 4].unsqueeze(2).to_broadcast([128, 4, 64]), ALU.mult,
                        [po[hb][1], rt], [yt])
            pbb = self.psb[:, 0:512]

            def fn(e, pbb=pbb, ya=ya):
                ins = None
                for c in range(4):
                    ins = e.transpose(out=pbb[:, c * 128:(c + 1) * 128], in_=ya[:, c * 128:(c + 1) * 128],
                                      identity=self.identb)
                return ins
            S.add("pe", fn, reads=[yt, tC], writes=[self.psbt[0]])
            self.cp("act", yTa[:, :, qc:qc + 128], pbb.rearrange("p (c j) -> p c j", c=4), [self.psbt[0]], [yTt])

        pending = emit_scores(items[0])
        for idx, it in enumerate(items):
            t, j, dd, src, nj, esrc, hb = it
            kt = t + dd
            nxt = emit_scores(items[idx + 1]) if idx + 1 < len(items) else None
            ma, mt = self.Pm.next()
            self.act(ma, pending[0][:, :], AF.Exp, [pending[1]], [mt], scale=0.125)

            def fn(e, hb=hb, kt=kt, ma=ma, j=j, nj=nj, pbank=po[hb][0]):
                ins = None
                for hh in range(4):
                    h = hb * 4 + hh
                    ins = e.matmul(pbank[:, hh * 65:(hh + 1) * 65], lhsT=ma[:, hh * 128:(hh + 1) * 128],
                                   rhs=self.vA[:, kt, h, 0:65], start=(j == 0 and hh == 0), stop=(j == nj - 1),
                                   skip_group_check=True)
                return ins
            S.add("pe", fn, reads=[mt, self.vAt[kt]], writes=[po[hb][1]])
            if fillers and hb == 1:
                fillers.pop(0)()
            if j == nj - 1 and hb == 1:
                finish_tile(t)
            pending = nxt
        self.nbank = 5
        while fillers:
            fillers.pop(0)()
        mga, mgt = self.mrg.next()
        for mo in range(8):
            wa, wt = self.load_w4("wba", l, mo)
            pa_, pat = self.bank()
            self.mm(pa_[:, :n], [(wa[:, k, :], yTa[:, k, :n]) for k in range(4)], [wt, yTt], pat)
            wc, wct = self.load_w4("wbc", l, mo)
            pc_, pct = self.bank()
            self.mm(pc_[:, :n], [(wc[:, k, :], yca[:, k, :n]) for k in range(4)], [wct, yct], pct)
            wg, wgt = self.load_w8("win", l, 24 + mo)
            pga, pgat = self.bank()
            self.mm(pga[:, :n], [(wg[:, k, :], xm[:, k, :]) for k in range(8)], [wgt, xwt], pgat)
            wg2, wgt2 = self.load_w8("win", l, 32 + mo)
            pgc, pgct = self.bank()
            self.mm(pgc[:, :n], [(wg2[:, k, :], xm[:, k, :]) for k in range(8)], [wgt2, xwt], pgct)
            s1, s1t = self.sg.next()
            s2, s2t = self.sg.next()
            self.act(s1[:, :n], pga[:, :n], AF.Sigmoid, [pgat, tC], [s1t], bias=self.par(l, P_BIN + 24 + mo))
            self.act(s2[:, :n], pgc[:, :n], AF.Sigmoid, [pgct, tC], [s2t], bias=self.par(l, P_BIN + 32 + mo))
            u1, u1t = self.t12.next()
            u2, u2t = self.t12.next()
            self.tt("dve", u1[:, :n], s1[:, :n], pa_[:, :n], ALU.mult, [s1t, pat], [u1t])
            self.tt("dve", u2[:, :n], s2[:, :n], pc_[:, :n], ALU.mult, [s2t, pct], [u2t])
            self.tt("pool", mga[:, mo, :n], u1[:, :n], u2[:, :n], ALU.add, [u1t, u2t], [mgt])
        self.cp("pool", self.xsave, X[:, :, xc + n - 1:xc + n], xt, [self.xsavet])
        nctx = None
        if nwin is not None:
            nctx = self.b_pre(kind, l, L, nwin[0], nwin[1], False)
        for mo in range(8):
            wa, wt = self.load_w8("wo", l, mo)
            pb, pbt = self.bank()
            self.mm(pb[:, :n], [(wa[:, k, :], mga[:, k, :n]) for k in range(8)], [wt, mgt], pbt)
            za, zt = self.ztmp.next()
            self.act(za[:, :n], pb[:, :n], AF.Identity, [pbt, tC], [zt], bias=self.par(l, P_BO + mo))
            self.stt("dve", X[:, mo, xc:xc + n], X[:, mo, xc:xc + n], ALPHA, za[:, :n], ALU.mult, ALU.add,
                     xt + [zt], xt)
        if nctx is not None:
            self.b_q(l, nctx)
        self.layer_norm(l, xc, n, xt, P_G1, P_B1, self.zb, self.zq, self.mean, self.rstd, self.msq)
        return nctx

    def c_pre(self, kind, l, L, t0, t1, first):
        tC = self.tC
        xwa, xwt = self.xwc.next()
        n, xc, xt = self.build_window(xwa, xwt, kind, t0, t1, first, L, t0 > L["B"][0])
        el, er = self.edge_mode(L, t0, t1)
        for mode, col in ((el, 0), (er, n + 1)):
            if mode in ("ftop", "fbot"):
                f = self.FLG[:, 0:1] if mode == "ftop" else self.FLG[:, 1:2]
                self.ts("dve", xwa[:, :, col:col + 1], xwa[:, :, col:col + 1], f, None, ALU.mult, None,
                        [xwt, tC], [xwt])
        ga, gt = self.gT.next()
        return dict(xwa=xwa, xwt=xwt, n=n, xc=xc, xt=xt, el=el, er=er, ga=ga, gt=gt, m=0)

    def c_iter(self, l, ctx, count):
        tC = self.tC
        DER = self.DER
        xwa, xwt, n, el, er, ga, gt = (ctx[k] for k in ("xwa", "xwt", "n", "el", "er", "ga", "gt"))
        xh = xwa[:, :, 0:n + 2]
        for m in range(ctx["m"], min(22, ctx["m"] + count)):
            wg, wgt = self.load_w8("wup", l, m)
            wv_, wvt_ = self.load_w8("wup", l, 22 + m)
            pg, pgt = self.bank()
            self.mm(pg[:, :n + 2], [(wg[:, k, :], xh[:, k, :]) for k in range(8)], [wgt, xwt], pgt)
            pv, pvt = self.bank()
            self.mm(pv[:, :n + 2], [(wv_[:, k, :], xh[:, k, :]) for k in range(8)], [wvt_, xwt], pvt)
            accs = []
            for (pp, ppt, ring, mc) in ((pg, pgt, self.accg, m), (pv, pvt, self.accv, 22 + m)):
                aa, at = ring.next()
                w0 = self.par(l, P_FCW + mc)
                w1 = self.par(l, P_FCW + 44 + mc)
                w2 = self.par(l, P_FCW + 88 + mc)
                self.act(aa[:, :n], pp[:, 1:n + 1], AF.Identity, [ppt, tC], [at], bias=DER[:, l, 0, mc:mc + 1], scale=w1)
                self.stt("dve", aa[:, :n], pp[:, 0:n], w0, aa[:, :n], ALU.mult, ALU.add, [ppt, at, tC], [at])
                self.stt("dve", aa[:, :n], pp[:, 2:n + 2], w2, aa[:, :n], ALU.mult, ALU.add, [ppt, at, tC], [at])
                if el is not None:
                    j = 1 if el == "zero" else 3
                    self.tt("pool", aa[:, 0:1], aa[:, 0:1], DER[:, l, j, mc:mc + 1], ALU.subtract, [at, tC], [at])
                if er is not None:
                    j = 2 if er == "zero" else 4
                    self.tt("pool", aa[:, n - 1:n], aa[:, n - 1:n], DER[:, l, j, mc:mc + 1], ALU.subtract, [at, tC], [at])
                accs.append((aa, at))
            la, lt = self.gl.next()
            self.act(la[:, :n], accs[0][0][:, :n], AF.Gelu_apprx_tanh, [accs[0][1]], [lt])
            self.tt("pool", ga[:, m, :n], la[:, :n], accs[1][0][:, :n], ALU.mult, [lt, accs[1][1]], [gt])
        ctx["m"] = min(22, ctx["m"] + count)

    def phase_c_window(self, kind, l, L, ctx, nxt):
        X = self.X
        tC = self.tC
        n, xc, xt, ga, gt = (ctx[k] for k in ("n", "xc", "xt", "ga", "gt"))
        self.c_iter(l, ctx, 22)
        self.cp("pool", self.xsave, X[:, :, xc + n - 1:xc + n], xt, [self.xsavet])
        nctx = None
        if nxt is not None:
            nctx = self.c_pre(kind, l, L, nxt[0], nxt[1], False)
        for mo in range(8):
            wa, wt = self.wdn.next()
            self.dma(wa.rearrange("p k j -> p (k j)"), self.d["wdnb"][l, mo], [], [wt], wt)
            pb, pbt = self.bank()
            self.mm(pb[:, :n], [(wa[:, k, :], ga[:, k, :n]) for k in range(22)], [wt, gt], pbt)
            za, zt = self.ztmp2.next()
            self.act(za[:, :n], pb[:, :n], AF.Identity, [pbt, tC], [zt], bias=self.par(l, P_BDN + mo))
            self.stt("dve", X[:, mo, xc:xc + n], X[:, mo, xc:xc + n], ALPHA, za[:, :n], ALU.mult, ALU.add,
                     xt + [zt], xt)
        if nctx is not None:
            self.c_iter(l, nctx, 6)
        self.layer_norm(l, xc, n, xt, P_G2, P_B2, self.zb2, self.zq2, self.mean2, self.rstd2, self.msq2)
        return nctx

    def build(self):
        nc = self.nc
        self.d["identin"] = nc.dram_tensor("identin", [128, 128], F32, kind="ExternalInput").ap()
        self.setup()
        self.prepass()
        self.barrier(self.c_tiles, self.ab_tiles)
        sk = sample_kind()
        for i in range(self.NS if self.stop != 'P' else 0):
            self.chunk(sk, self.d["xs"][i], self.d["ys"][i])
        for j in range(self.NP):
            self.chunk(prompt_kind(j), self.d["xp"][j], self.d["yp"][j])
        self.S.add("sp", lambda e: e.nop(nofuse=True), extra=self.out_events)
        with ExitStack() as stack:
            self.S.emit(nc, stack)
        return nc


class Ring:
    def __init__(self, items):
        self.items = items
        self.i = 0

    def next(self):
        it = self.items[self.i]
        self.i = (self.i + 1) % len(self.items)
        return it


def wlayout(w):
    K, M = w.shape
    return np.ascontiguousarray(w.reshape(K // 128, 128, M // 128, 128).transpose(2, 1, 0, 3)).reshape(
        M // 128, 128, K)


def col(v):
    return np.ascontiguousarray(v.reshape(-1, 128).T)


def bm_tile(rpb_l, i0, r0, R):
    out = np.full((128, 8, 128), NEG, np.float32)
    kc = np.arange(64)[:, None]
    qc = np.arange(64)[None, :]
    js = np.clip(qc - 8, 0, 48)
    valid = (kc >= js) & (kc < js + 16)
    dc = np.clip(kc - qc + 15, 0, 30)
    for kr in range(2):
        for qr in range(2):
            r = r0 + kr
            i = i0 + qr
            if i < 0 or i >= R:
                continue
            rs = min(max(i - 4, 0), R - 8)
            if not (rs <= r < rs + 8):
                continue
            g = rpb_l[:, r - i + 7][:, dc]
            blk = np.where(valid[None], g, np.float32(NEG))
            out[kr * 64:(kr + 1) * 64, :, qr * 64:(qr + 1) * 64] = blk.transpose(1, 0, 2)
    return out.reshape(128, 1024)


def bm_set(rpb_l, half):
    tiles = []
    big = 1000
    def interior(d):
        return bm_tile(rpb_l, 500, 500 + 2 * d, big)
    for d in range(-2, 3):
        tiles.append(interior(d))
    for t, ds in ((0, range(0, 4)), (1, range(-1, 3)), (14, range(-2, 2)), (15, range(-3, 1))):
        for d in ds:
            tiles.append(bm_tile(rpb_l, 2 * t, 2 * (t + d), 32))
    for t, ds in ((5, range(-2, 4)), (6, range(-2, 3))):
        for d in ds:
            if half == 0:
                tiles.append(bm_tile(rpb_l, 2 * (t - 5), 2 * (t - 5 + d), 128))
            else:
                tiles.append(interior(d))
    for t, ds in ((11, range(-2, 3)), (12, range(-3, 3))):
        for d in ds:
            if half == 1:
                tiles.append(bm_tile(rpb_l, 112 + 2 * (t - 5), 112 + 2 * (t - 5 + d), 128))
            else:
                tiles.append(interior(d))
    assert len(tiles) == NBM
    return np.stack(tiles)


def host_params(inp):
    pars = []
    for l in range(2):
        cols = [col(inp["b_in"][l]),
                np.ascontiguousarray(inp["sc_conv_w"][l].reshape(3, 4, 128).transpose(2, 0, 1)).reshape(128, 12),
                col(inp["sc_conv_b"][l]), col(inp["b_o"][l]), col(inp["ln1_g"][l]), col(inp["ln1_b"][l]),
                col(inp["ffn_b_up"][l]),
                np.ascontiguousarray(inp["ffn_conv_w"][l].reshape(3, 44, 128).transpose(2, 0, 1)).reshape(128, 132),
                col(inp["ffn_conv_b"][l]), col(inp["ffn_b_down"][l]), col(inp["ln2_g"][l]), col(inp["ln2_b"][l])]
        p = np.concatenate(cols, axis=1)
        assert p.shape == (128, NPAR)
        pars.append(p)
    par = np.ascontiguousarray(np.concatenate(pars, axis=1), dtype=np.float32)
    bv = np.ascontiguousarray(np.broadcast_to(
        np.stack([inp["b_in"][l][1024:1536] for l in range(2)]).reshape(1, 1024), (128, 1024)), dtype=np.float32)
    return par, bv


def host_weights(inp):
    out = {}
    for key, name in (("win", "w_in"), ("wba", "w_br_attn"), ("wbc", "w_br_conv"), ("wo", "w_o"),
                      ("wup", "ffn_w_up"), ("wdn", "ffn_w_down")):
        out[key] = np.stack([wlayout(np.asarray(inp[name][l], np.float32)) for l in range(2)])
    return out


_CACHE = {}


def get_nc(NS, NP):
    k = (NS, NP)
    if k not in _CACHE:
        _CACHE[k] = Builder(NS, NP).build()
    return _CACHE[k]


def kernel(**inputs):
    inp = {k: np.asarray(v) for k, v in inputs.items()}
    xp_full = inp["x_prompt"]
    xs_full = inp["x_sample"]
    n = 8
    nc = get_nc(4, 4)
    par, bv = host_params(inp)
    W = host_weights(inp)
    ident = np.eye(128, dtype=np.float32)
    bms = [np.stack([bm_set(inp["attn_rpb"][l], h) for l in range(2)]) for h in range(2)]
    in_maps = []
    for c in range(n):
        p, half = c // 2, c % 2
        xpc = np.zeros((4, 17 * 128, D), np.float32)
        for j in range(4):
            r0 = 64 * half + 16 * j
            lo, hi = r0 - 10, r0 + 24
            slo, shi = max(lo, 0), min(hi, 128)
            xpc[j, (slo - lo) * 64:(shi - lo) * 64] = xp_full[p, slo * 64:shi * 64]
        ftop = 0.0 if half == 0 else 1.0
        fbot = 0.0 if half == 1 else 1.0
        flg = np.ascontiguousarray(np.broadcast_to(np.array([ftop, fbot, 1 - ftop, 1 - fbot], np.float32), (128, 4)))
        m = dict(xs=np.ascontiguousarray(xs_full[4 * c:4 * c + 4]), xp=xpc, bm=bms[half], par=par, bv=bv,
                 flg=flg, identin=ident)
        m.update(W)
        in_maps.append(m)
    res = run_bass_kernel_spmd(nc, in_maps, core_ids=list(range(n)))
    y_prompt = np.empty_like(xp_full)
    y_sample = np.empty_like(xs_full)
    for c in range(n):
        r = res.results[c]
        p, half = c // 2, c % 2
        y_sample[4 * c:4 * c + 4] = r["ys"]
        for j in range(4):
            r0 = 64 * half + 16 * j
            y_prompt[p, r0 * 64:(r0 + 16) * 64] = r["yp"][j]
    return (y_prompt, y_sample)
```

```python
import numpy as np
import ml_dtypes
from contextlib import ExitStack
import concourse.bass as bass
import concourse.mybir as mybir
from concourse.bass_utils import run_bass_kernel_spmd

F32 = mybir.dt.float32
BF16 = mybir.dt.bfloat16
AF = mybir.ActivationFunctionType
ALU = mybir.AluOpType

D = 1024
NH = 8
DH = 64
DFF = 2816
ALPHA = float(4 ** 0.25)
EPS = 1e-5
NEG = -30000.0
NPAR = 324
P_BIN, P_SCW, P_SCB, P_BO, P_G1, P_B1, P_BUP, P_FCW, P_FCB, P_BDN, P_G2, P_B2 = (
    0, 40, 52, 56, 64, 72, 80, 124, 256, 300, 308, 316)
NBM = 43

ENGS = ("pe", "act", "dve", "pool", "sp")


class Tl:
    __slots__ = ("name", "w", "rs", "rd", "sem", "cnt")

    def __init__(self, name):
        self.name = name
        self.w = None
        self.rs = {}
        self.rd = []
        self.sem = None
        self.cnt = 0


class Op:
    __slots__ = ("eng", "fn", "waits", "sig", "idx", "sigval", "dsem", "line")


class Sched:
    def __init__(self):
        self.q = {e: [] for e in ENGS}
        self.dma_tiles = []

    def add(self, eng, fn, reads=(), writes=(), dma=None, extra=()):
        op = Op()
        op.eng = eng
        op.fn = fn
        op.sig = False
        op.idx = len(self.q[eng])
        op.dsem = dma
        op.sigval = 0
        import sys as _s
        f = _s._getframe(1)
        ls = []
        while f is not None and len(ls) < 4:
            ls.append(f.f_lineno)
            f = f.f_back
        op.line = ls
        deps = list(extra)
        for t in reads:
            if t.w is not None:
                deps.append(t.w)
        for t in writes:
            if t.w is not None:
                deps.append(t.w)
            deps.extend(t.rs.values())
            deps.extend(t.rd)
        waits = []
        seen = set()
        for d in deps:
            if id(d) in seen:
                continue
            seen.add(id(d))
            if isinstance(d, Op):
                if d.eng == eng:
                    if eng in ("pe", "sp"):
                        continue
                    if op.idx - d.idx > 3:
                        continue
                d.sig = True
            waits.append(d)
        op.waits = waits
        if dma is not None:
            if dma.sem is None:
                dma.sem = True
                self.dma_tiles.append(dma)
            dma.cnt += 16
            ev = (dma, dma.cnt)
        else:
            ev = op
        for t in reads:
            if isinstance(ev, Op):
                t.rs[eng] = ev
            else:
                t.rd.append(ev)
        for t in writes:
            t.w = ev
            t.rs = {}
            t.rd = []
        self.q[eng].append(op)
        return ev

    def emit(self, nc, stack):
        esem = {}
        for e in ("pe", "act", "dve", "pool"):
            esem[e] = stack.enter_context(nc.semaphore("s_" + e))
        for t in self.dma_tiles:
            t.sem = stack.enter_context(nc.semaphore("d_" + t.name))
        for e in ENGS:
            c = 0
            for op in self.q[e]:
                if op.sig:
                    c += 1
                op.sigval = c
        q = self.q

        def run(name, eng):
            waited = {}
            for op in q[name]:
                for d in op.waits:
                    if isinstance(d, Op):
                        sem, val = esem[d.eng], d.sigval
                    else:
                        sem, val = d[0].sem, d[1]
                    k = id(sem)
                    if waited.get(k, 0) < val:
                        eng.wait_ge(sem, val)
                        waited[k] = val
                ins = op.fn(eng)
                if op.dsem is not None:
                    ins.then_inc(op.dsem.sem, 16)
                elif op.sig:
                    ins.then_inc(esem[name], 1)

        with nc.Block() as block:
            @block.tensor
            def _(e):
                run("pe", e)

            @block.scalar
            def _(e):
                run("act", e)

            @block.vector
            def _(e):
                run("dve", e)

            @block.gpsimd
            def _(e):
                run("pool", e)

            @block.sync
            def _(e):
                run("sp", e)


def split_windows(t0, t1, forced=()):
    cuts = sorted(set([t0, t1] + [f for f in forced if t0 < f < t1]))
    out = []
    for a, b in zip(cuts[:-1], cuts[1:]):
        n = b - a
        k = (n + 2) // 3
        base, rem = divmod(n, k)
        s = a
        for i in range(k):
            ln = base + (1 if i < rem else 0)
            out.append((s, s + ln))
            s += ln
    return out


def sample_kind():
    q = {}
    for t in range(16):
        if t == 0:
            q[t] = [(d, ("e", 5 + i)) for i, d in enumerate(range(0, 4))]
        elif t == 1:
            q[t] = [(d, ("e", 9 + i)) for i, d in enumerate(range(-1, 3))]
        elif t == 14:
            q[t] = [(d, ("e", 13 + i)) for i, d in enumerate(range(-2, 2))]
        elif t == 15:
            q[t] = [(d, ("e", 17 + i)) for i, d in enumerate(range(-3, 1))]
        else:
            q[t] = [(d, ("i", d + 2)) for d in range(-2, 3)]
    lay = dict(A=(0, 16), B=(0, 16), C=(0, 16), q=q, forced=(),
               bl={0: "zero"}, br={16: "zero"})
    return dict(NT=16, xoff=0, L=[lay, lay], out=(0, 16))


def prompt_kind(j):
    def qmap(b0, b1, a0, a1):
        q = {}
        for t in range(b0, b1):
            if j == 0 and t == 5:
                lst = [(d, ("e", 21 + i)) for i, d in enumerate(range(-2, 4))]
            elif j == 0 and t == 6:
                lst = [(d, ("e", 27 + i)) for i, d in enumerate(range(-2, 3))]
            elif j == 3 and t == 11:
                lst = [(d, ("e", 32 + i)) for i, d in enumerate(range(-2, 3))]
            elif j == 3 and t == 12:
                lst = [(d, ("e", 37 + i)) for i, d in enumerate(range(-3, 3))]
            else:
                lst = [(d, ("i", d + 2)) for d in range(-2, 3)]
            q[t] = [(d, s) for (d, s) in lst if a0 <= t + d < a1]
        return q
    bl = {5: "ftop"} if j == 0 else {}
    br = {13: "fbot"} if j == 3 else {}
    l0 = dict(A=(0, 17), B=(2, 16), C=(2, 15), q=qmap(2, 16, 0, 17), forced=(5, 13), bl=bl, br=br)
    l1 = dict(A=(2, 15), B=(4, 14), C=(5, 13), q=qmap(4, 14, 2, 15), forced=(5, 13), bl=bl, br=br)
    return dict(NT=17, xoff=2, L=[l0, l1], out=(5, 13))


class Builder:
    def __init__(self, NS, NP, dbg=False, stop=None):
        self.NS, self.NP = NS, NP
        self.stop = stop
        self.bstop = None
        if stop is not None and len(stop) == 3:
            self.bstop = int(stop[2])
            self.stop = stop[:2]
        self.nc = nc = bass.Bass("TRN2", target_bir_lowering=False)
        self.S = Sched()
        dt = nc.dram_tensor
        self.d = d = {}
        if NS:
            d["xs"] = dt("xs", [NS, 2048, D], F32, kind="ExternalInput").ap()
            d["ys"] = dt("ys", [NS, 2048, D], F32, kind="ExternalOutput").ap()
        if NP:
            d["xp"] = dt("xp", [NP, 17 * 128, D], F32, kind="ExternalInput").ap()
            d["yp"] = dt("yp", [NP, 1024, D], F32, kind="ExternalOutput").ap()
        self.wshapes = dict(win=(40, 1024), wba=(8, 512), wbc=(8, 512), wo=(8, 1024),
                            wup=(44, 1024), wdn=(8, 2816))
        for k, (n, f) in self.wshapes.items():
            d[k] = dt(k, [2, n, 128, f], F32, kind="ExternalInput").ap()
            d[k + "b"] = dt(k + "b", [2, n, 128, f], BF16, kind="Internal").ap()
        d["bm"] = dt("bm", [2, NBM, 128, 1024], F32, kind="ExternalInput").ap()
        d["eb"] = dt("eb", [2, NBM, 128, 1024], BF16, kind="Internal").ap()
        d["par"] = dt("par", [128, 2 * NPAR], F32, kind="ExternalInput").ap()
        d["bv"] = dt("bv", [128, 1024], F32, kind="ExternalInput").ap()
        d["flg"] = dt("flg", [128, 4], F32, kind="ExternalInput").ap()
        self.off = 0
        self.arena = nc.alloc_sbuf_tensor("arena", [128, 53100], F32)
        self.cap = 53100 * 4
        self.pbank = 0
        self.nbank = 5
        self.sbi = 0
        self.out_events = []

    def sb(self, shape, dtype, name=None):
        n = 1
        for s in shape:
            n *= s
        nbytes = n * (4 if dtype == F32 else 2)
        nbytes = (nbytes + 31) // 32 * 32
        assert self.off + nbytes <= self.cap, ("SBUF overflow", name, self.off + nbytes)
        w0 = self.off // 4
        ap = self.arena[:, w0:w0 + nbytes // 4]
        if dtype != F32:
            ap = ap.bitcast(BF16)
            ap = ap[:, 0:n]
        else:
            ap = ap[:, 0:n]
        self.off += nbytes
        if len(shape) == 2:
            ap = ap.rearrange("p (a b) -> p a b", a=shape[0])
        elif len(shape) == 3:
            ap = ap.rearrange("p (a b c) -> p a b c", a=shape[0], b=shape[1])
        return ap

    def ring(self, n, shape, dtype, name):
        return Ring([(self.sb(shape, dtype, name), Tl(f"{name}{i}")) for i in range(n)])

    def bank(self):
        self.pbank = (self.pbank + 1) % self.nbank
        b = self.pbank
        return self.ps[b], self.pst[b]

    def mm(self, out, pairs, reads, wt, flags=None):
        pairs = list(pairs)

        def fn(e):
            n = len(pairs)
            ins = None
            for i, (l, r) in enumerate(pairs):
                st, sp = (i == 0, i == n - 1) if flags is None else flags
                ins = e.matmul(out, lhsT=l, rhs=r, start=st, stop=sp, skip_group_check=True)
            return ins
        self.S.add("pe", fn, reads=reads, writes=[wt])

    def dma(self, out, in_, reads, writes, sem):
        return self.S.add("sp", lambda e: e.dma_start(out=out, in_=in_), reads=reads, writes=writes, dma=sem)

    def act(self, out, in_, func, reads, writes, bias=0.0, scale=1.0):
        self.S.add("act", lambda e: e.activation(out=out, in_=in_, func=func, bias=bias, scale=scale),
                   reads=reads, writes=writes)

    def ts(self, eng, out, in0, s1, s2, op0, op1, reads, writes):
        if s2 is None:
            self.S.add(eng, lambda e: e.tensor_scalar(out, in0, s1, None, op0), reads=reads, writes=writes)
        else:
            self.S.add(eng, lambda e: e.tensor_scalar(out, in0, s1, s2, op0, op1), reads=reads, writes=writes)

    def stt(self, eng, out, in0, sc, in1, op0, op1, reads, writes):
        self.S.add(eng, lambda e: e.scalar_tensor_tensor(out, in0, sc, in1, op0, op1), reads=reads, writes=writes)

    def fma(self, eng, out, in0, sc, in1, reads, writes):
        if eng == "dve":
            self.stt("dve", out, in0, sc, in1, ALU.mult, ALU.add, reads, writes)
        else:
            pa, pt = self.ptmp.next()
            shp = list(out.shape)
            pv = pa[:, :shp[-1]]
            self.ts(eng, pv, in0, sc, None, ALU.mult, None, reads, [pt])
            self.tt(eng, out, pv, in1, ALU.add, list(reads) + [pt], writes)

    def tt(self, eng, out, in0, in1, op, reads, writes):
        self.S.add(eng, lambda e: e.tensor_tensor(out, in0, in1, op), reads=reads, writes=writes)

    def cp(self, eng, out, in_, reads, writes):
        if eng == "act":
            self.S.add(eng, lambda e: e.copy(out, in_), reads=reads, writes=writes)
        else:
            self.S.add(eng, lambda e: e.tensor_copy(out, in_), reads=reads, writes=writes)

    def par(self, l, off, n=1):
        return self.PAR[:, l * NPAR + off: l * NPAR + off + n]

    def setup(self):
        nc = self.nc
        d = self.d
        self.ps = [nc.alloc_psum_tensor(f"ps{i}", [128, 512], F32) for i in range(7)]
        self.pst = [Tl(f"ps{i}") for i in range(7)]
        self.psb = nc.alloc_psum_tensor("psb", [128, 1024], BF16)
        self.psbt = [Tl("psb0"), Tl("psb1")]
        self.psbi = 0
        self.X = self.sb([8, 16 * 128], F32, "X")
        self.Xt = [Tl(f"X{t}") for t in range(16)]
        self.PAR = self.sb([2 * NPAR], F32, "PAR")
        self.BV = self.sb([2, 512], F32, "BV")
        self.FLG = self.sb([4], F32, "FLG")
        self.DER = self.sb([2, 5, 44], F32, "DER")
        self.ident = self.sb([128], F32, "ident")
        self.identb = self.sb([128], BF16, "identb")
        self.ones = self.sb([128], BF16, "ones")
        self.epsc = self.sb([1], F32, "epsc")
        self.barbuf = self.sb([1], F32, "barbuf")
        self.tC = Tl("const")
        self.xsave = self.sb([8, 1], F32, "xsave")
        self.xsavet = Tl("xsave")
        self.wring = self.ring(6, [8, 128], BF16, "w8")
        self.xw = self.ring(2, [8, 386], BF16, "xw")
        self.ptmp = self.ring(1, [386], F32, "ptmp")
        self.mark = self.off
        self.kT = self.sb([4, 17 * 128], BF16, "kT")
        self.kTt = [Tl(f"kT{t}") for t in range(17)]
        self.vA = self.sb([17, 8, 68], BF16, "vA")
        self.vAt = [Tl(f"vA{t}") for t in range(17)]
        self.Eint = self.sb([5, 1024], BF16, "Eint")
        self.Eintt = Tl("Eint")
        self.Eedge = self.sb([6, 1024], BF16, "Eedge")
        self.Eedget = Tl("Eedge")
        self.wv = Ring([(self.Eedge.rearrange("p a b -> p (a b)")[:, 0:4096].rearrange("p (k j) -> p k j", k=8), self.Eedget)])
        self.xstage = self.ring(1, [1024], F32, "xstage")
        self.qT = self.ring(1, [4, 2, 384], BF16, "qT")
        self.Pm = self.ring(4, [512], BF16, "Pm")
        self.ya = self.ring(1, [512], BF16, "ya")
        self.rec = self.ring(2, [8], F32, "rec")
        self.yaT = self.ring(1, [4, 384], BF16, "yaT")
        self.cu = self.ring(1, [4, 386], F32, "cu")
        self.tu = self.ring(1, [386], F32, "tu")
        self.cv = self.ring(1, [384], F32, "cv")
        self.yc = self.ring(1, [4, 384], BF16, "yc")
        self.w4ring = self.ring(3, [4, 128], BF16, "w4")
        self.sg = self.ring(2, [384], F32, "sg")
        self.t12 = self.ring(2, [384], F32, "t12")
        self.mrg = self.ring(1, [8, 384], BF16, "mrg")
        self.ztmp = self.tu
        self.zb = self.ring(2, [384], BF16, "zb")
        self.zq = self.ring(2, [384], BF16, "zq")
        self.mean = Ring([self.t12.items[0]])
        self.rstd = Ring([self.t12.items[1]])
        self.msq = self.ring(1, [384], F32, "msq")
        endAB = self.off
        self.off = self.mark
        self.xwc = self.ring(2, [8, 386], BF16, "xwc")
        self.gT = self.ring(2, [22, 384], BF16, "gT")
        self.wdn = self.ring(2, [22, 128], BF16, "wdn")
        self.accg = self.ring(2, [384], F32, "accg")
        self.accv = self.ring(2, [384], F32, "accv")
        self.gl = self.ring(2, [384], F32, "gl")
        self.ostage = self.ring(2, [1024], F32, "ostage")
        self.ztmp2 = self.ring(1, [386], F32, "ztmp2")
        self.zb2 = self.ring(2, [384], BF16, "zb2")
        self.zq2 = self.ring(2, [384], BF16, "zq2")
        self.mean2 = self.ring(1, [384], F32, "mean2")
        self.rstd2 = self.ring(1, [384], F32, "rstd2")
        self.msq2 = self.ring(1, [384], F32, "msq2")
        self.stF = self.ring(4, [1024], F32, "stF")
        self.stB = self.ring(4, [1024], BF16, "stB")
        endC = self.off
        self.off = max(endAB, endC)
        print("SBUF bytes/partition: persistent", self.mark, "B", endAB - self.mark, "C", endC - self.mark, "cap", self.cap)
        self.ab_tiles = self.kTt + self.vAt + [self.Eintt, self.Eedget]
        self.c_tiles = []
        for r in (self.xstage, self.qT, self.Pm, self.ya, self.rec, self.yaT,
                  self.cu, self.tu, self.cv, self.yc, self.w4ring, self.sg, self.t12, self.mrg,
                  self.zb, self.zq, self.msq):
            self.ab_tiles += [t for _, t in r.items]
        for r in (self.xwc, self.gT, self.wdn, self.accg, self.accv, self.gl, self.ostage, self.ztmp2,
                  self.zb2, self.zq2, self.mean2, self.rstd2, self.msq2, self.stF, self.stB):
            self.c_tiles += [t for _, t in r.items]

        S = self.S
        tC = self.tC
        S.add("pool", lambda e: e.memset(self.ones, 1.0 / 1024.0), writes=[tC])
        S.add("pool", lambda e: e.memset(self.epsc, EPS), writes=[tC])
        self.dma(self.PAR, d["par"], [], [tC], tC)
        self.dma(self.BV.rearrange("p a b -> p (a b)"), d["bv"], [], [tC], tC)
        self.dma(self.FLG, d["flg"], [], [tC], tC)
        self.dma(self.ident, d["identin"], [], [tC], tC)
        self.cp("dve", self.identb, self.ident, [tC], [tC])
        for l in range(2):
            w0 = self.par(l, P_FCW, 44)
            w1 = self.par(l, P_FCW + 44, 44)
            w2 = self.par(l, P_FCW + 88, 44)
            bup = self.par(l, P_BUP, 44)
            fcb = self.par(l, P_FCB, 44)
            D_ = self.DER
            self.tt("dve", D_[:, l, 0, :], w0, w1, ALU.add, [tC], [tC])
            self.tt("dve", D_[:, l, 0, :], D_[:, l, 0, :], w2, ALU.add, [tC], [tC])
            self.tt("dve", D_[:, l, 0, :], D_[:, l, 0, :], bup, ALU.mult, [tC], [tC])
            self.tt("dve", D_[:, l, 0, :], D_[:, l, 0, :], fcb, ALU.add, [tC], [tC])
            self.tt("dve", D_[:, l, 1, :], bup, w0, ALU.mult, [tC], [tC])
            self.tt("dve", D_[:, l, 2, :], bup, w2, ALU.mult, [tC], [tC])
            self.ts("dve", D_[:, l, 3, :], D_[:, l, 1, :], self.FLG[:, 2:3], None, ALU.mult, None, [tC], [tC])
            self.ts("dve", D_[:, l, 4, :], D_[:, l, 2, :], self.FLG[:, 3:4], None, ALU.mult, None, [tC], [tC])

    def barrier(self, old_tiles, new_tiles):
        S = self.S
        S.add("pool", lambda e: e.memset(self.barbuf, 0.0), writes=list(old_tiles))
        last = old_tiles[0].w
        for t in new_tiles:
            t.w = last
            t.rs = {}
            t.rd = []

    def prepass(self):
        d = self.d
        jobs = []
        for k, (n, f) in self.wshapes.items():
            for l in range(2):
                for m in range(n):
                    for c0 in range(0, f, 1024):
                        c1 = min(c0 + 1024, f)
                        jobs.append((d[k][l, m][:, c0:c1], d[k + "b"][l, m][:, c0:c1], c1 - c0, "cast"))
        for l in range(2):
            for m in range(NBM):
                jobs.append((d["bm"][l, m], d["eb"][l, m], 1024, "exp"))
        evs = []
        ci = 0
        LOOK = 3
        staged = {}

        def issue_in(i):
            src, dst, f, kind = jobs[i]
            fa, ft = self.stF.next()
            self.dma(fa[:, :f], src, [], [ft], ft)
            staged[i] = (fa, ft)
        for i in range(min(LOOK, len(jobs))):
            issue_in(i)
        for i, (src, dst, f, kind) in enumerate(jobs):
            if i + LOOK < len(jobs):
                issue_in(i + LOOK)
            fa, ft = staged.pop(i)
            ba, bt = self.stB.next()
            if kind == "exp":
                self.act(ba[:, :f], fa[:, :f], AF.Identity, [ft], [bt], scale=8.0)
            else:
                eng = ("dve", "act")[ci % 2]
                ci += 1
                self.cp(eng, ba[:, :f], fa[:, :f], [ft], [bt])
            evs.append(self.dma(dst, ba[:, :f], [bt], [], bt))
        self.S.add("sp", lambda e: e.nop(nofuse=True), extra=evs)

    def load_w8(self, key, l, mc):
        ap, t = self.wring.next()
        self.dma(ap.rearrange("p k j -> p (k j)"), self.d[key + "b"][l, mc], [], [t], t)
        return ap, t

    def load_w4(self, key, l, mc):
        ap, t = self.w4ring.next()
        self.dma(ap.rearrange("p k j -> p (k j)"), self.d[key + "b"][l, mc], [], [t], t)
        return ap, t

    def layer_norm(self, l, c0, n, xt, goff, boff, zb, zq, mean, rstd, msq):
        X = self.X
        pm, pmt = self.bank()
        pq, pqt = self.bank()
        for c in range(8):
            za, zt = zb.next()
            qa, qt = zq.next()
            self.cp("act", za[:, :n], X[:, c, c0:c0 + n], xt, [zt])
            self.act(qa[:, :n], X[:, c, c0:c0 + n], AF.Square, xt, [qt])
            self.mm(pm[:, :n], [(self.ones, za[:, :n])], [zt, self.tC], pmt, flags=(c == 0, c == 7))
            self.mm(pq[:, :n], [(self.ones, qa[:, :n])], [qt, self.tC], pqt, flags=(c == 0, c == 7))
        ma, mt = mean.next()
        ra, rt = rstd.next()
        sa, st = msq.next()
        self.cp("act", ma[:, :n], pm[:, :n], [pmt], [mt])
        self.tt("dve", sa[:, :n], ma[:, :n], ma[:, :n], ALU.mult, [mt], [st])
        self.tt("dve", sa[:, :n], pq[:, :n], sa[:, :n], ALU.subtract, [pqt, st], [st])
        self.ts("dve", sa[:, :n], sa[:, :n], EPS, None, ALU.add, None, [st], [st])
        self.act(ra[:, :n], sa[:, :n], AF.Sqrt, [st], [rt])
        ma2, m2t = self.ptmp.next()
        self.S.add("dve", lambda e: e.reciprocal(ra[:, :n], ra[:, :n]), reads=[rt], writes=[rt])
        self.tt("dve", ma2[:, :n], ra[:, :n], ra[:, :n], ALU.mult, [rt], [m2t])
        self.tt("dve", ma2[:, :n], ma2[:, :n], sa[:, :n], ALU.mult, [m2t, st], [m2t])
        self.ts("dve", ma2[:, :n], ma2[:, :n], -0.5, 1.5, ALU.mult, ALU.add, [m2t], [m2t])
        self.tt("dve", ra[:, :n], ra[:, :n], ma2[:, :n], ALU.mult, [rt, m2t], [rt])
        xv = X[:, :, c0:c0 + n]
        self.tt("dve", xv, xv, ma[:, :n].unsqueeze(1).to_broadcast([128, 8, n]), ALU.subtract, xt + [mt], xt)
        self.tt("dve", xv, xv, ra[:, :n].unsqueeze(1).to_broadcast([128, 8, n]), ALU.mult, xt + [rt], xt)
        for c in range(8):
            eng = "act"
            if eng == "act":
                self.act(X[:, c, c0:c0 + n], X[:, c, c0:c0 + n], AF.Identity, xt + [self.tC], xt,
                         bias=self.par(l, boff + c), scale=self.par(l, goff + c))
            else:
                self.ts("pool", X[:, c, c0:c0 + n], X[:, c, c0:c0 + n], self.par(l, goff + c),
                        self.par(l, boff + c), ALU.mult, ALU.add, xt + [self.tC], xt)

    def chunk(self, kind, xin, yout):
        for l in range(2):
            self.chunk_layer(kind, l, xin, yout)
            if self.stop is not None and self.stop[0] == str(l):
                break

    def xtiles(self, kind, t0, t1):
        xo = kind["xoff"]
        return [self.Xt[t - xo] for t in range(t0, t1) if 0 <= t - xo < 16]

    def kv_from_xb(self, l, xb, xbt, tiles, wvap, wvt):
        n = len(tiles) * 128
        t0 = tiles[0]
        for m in range(4):
            wa, wt = self.load_w8("win", l, 4 + m)
            pb, pbt = self.bank()
            self.mm(pb[:, :n], [(wa[:, k, :], xb[:, k, 0:n]) for k in range(8)], [wt, xbt], pbt)
            self.act(self.kT[:, m, t0 * 128:t0 * 128 + n], pb[:, :n], AF.Identity, [pbt, self.tC],
                     [self.kTt[t] for t in tiles], bias=self.par(l, P_BIN + 4 + m))
        for i, t in enumerate(tiles):
            pb, pbt = self.bank()
            self.mm(pb[:, :], [(xb[:, k, i * 128:(i + 1) * 128], wvap[:, k, :]) for k in range(8)],
                    [wvt, xbt], pbt)
            self.tt("dve", self.vA[:, t, :, 0:64], pb[:, :].rearrange("p (h d) -> p h d", h=8),
                    self.BV[:, l, :].rearrange("p (h d) -> p h d", h=8), ALU.add,
                    [pbt, self.tC], [self.vAt[t]])

    def chunk_layer(self, kind, l, xin, yout):
        L = kind["L"][l]
        xo = kind["xoff"]
        X = self.X
        d = self.d
        a0, a1 = L["A"]
        b0, b1 = L["B"]
        c0_, c1_ = L["C"]
        wvap, wvt = self.wv.next()
        for m_ in range(4):
            self.dma(wvap[:, :, m_ * 128:(m_ + 1) * 128],
                     d["winb"][l, 8 + m_].rearrange("p (k j) -> p k j", k=8), [], [wvt], wvt)
        self.S.add("pool", lambda e: e.memset(self.vA[:, :, :, 64:65], 1.0), writes=self.vAt)
        qa0, qt0 = self.qT.items[0]
        self.S.add("pool", lambda e: e.memset(qa0[0:64, :, 1, :], 0.0), writes=[qt0])
        self.S.add("pool", lambda e: e.memset(qa0[64:128, :, 0, :], 0.0), writes=[qt0])
        self.dma(self.Eint, d["eb"][l, 0:5].rearrange("m p f -> p m f"), [], [self.Eintt], self.Eintt)
        t = a0
        while t < a1:
            tiles = list(range(t, min(t + 3, a1)))
            t += 3
            n = len(tiles) * 128
            xb, xbt = self.xw.next()
            if l == 0:
                for i, tt_ in enumerate(tiles):
                    sa, st = self.xstage.next()
                    self.dma(sa, xin[tt_ * 128:(tt_ + 1) * 128, :], [], [st], st)
                    for hb in range(2):
                        pb, pbt = self.bank()

                        def fn(e, pb=pb, sa=sa, hb=hb):
                            ins = None
                            for c in range(4):
                                ins = e.transpose(out=pb[:, c * 128:(c + 1) * 128],
                                                  in_=sa[:, (hb * 4 + c) * 128:(hb * 4 + c + 1) * 128],
                                                  identity=self.ident)
                            return ins
                        self.S.add("pe", fn, reads=[st, self.tC], writes=[pbt])
                        pv = pb[:, :].rearrange("p (c j) -> p c j", c=4)
                        if 0 <= tt_ - xo < 16:
                            xc = (tt_ - xo) * 128
                            self.cp("dve", X[:, hb * 4:hb * 4 + 4, xc:xc + 128], pv, [pbt], [self.Xt[tt_ - xo]])
                            self.cp("act", xb[:, hb * 4:hb * 4 + 4, i * 128:(i + 1) * 128],
                                    X[:, hb * 4:hb * 4 + 4, xc:xc + 128], [self.Xt[tt_ - xo]], [xbt])
                        else:
                            self.cp("act", xb[:, hb * 4:hb * 4 + 4, i * 128:(i + 1) * 128], pv, [pbt], [xbt])
            else:
                xc = (tiles[0] - xo) * 128
                self.cp("act", xb[:, :, 0:n], X[:, :, xc:xc + n], self.xtiles(kind, tiles[0], tiles[-1] + 1), [xbt])
            self.kv_from_xb(l, xb, xbt, tiles, wvap, wvt)
        if self.stop == f"{l}A":
            return
        wins = split_windows(b0, b1, L["forced"])
        ctx = self.b_pre(kind, l, L, wins[0][0], wins[0][1], True)
        self.b_q(l, ctx)
        for wi, (t0, t1) in enumerate(wins):
            ctx = self.phase_b_window(kind, l, L, ctx, wins[wi + 1] if wi + 1 < len(wins) else None)
        self.barrier(self.ab_tiles, self.c_tiles)
        if self.stop == f"{l}B":
            self.write_out(kind, yout)
            self.barrier(self.c_tiles, self.ab_tiles)
            return
        wins = split_windows(c0_, c1_, L["forced"])
        ctx = self.c_pre(kind, l, L, wins[0][0], wins[0][1], True)
        for wi, (t0, t1) in enumerate(wins):
            ctx = self.phase_c_window(kind, l, L, ctx, wins[wi + 1] if wi + 1 < len(wins) else None)
        if l == 1 or self.stop == f"{l}C":
            self.write_out(kind, yout)
        self.barrier(self.c_tiles, self.ab_tiles)

    def write_out(self, kind, yout):
        X = self.X
        xo = kind["xoff"]
        o0, o1 = kind["out"]
        for t in range(o0, o1):
            xc = (t - xo) * 128
            oa, ot = self.ostage.next()
            for hb in range(2):
                pb, pbt = self.bank()

                def fn(e, pb=pb, hb=hb, xc=xc):
                    ins = None
                    for c in range(4):
                        ins = e.transpose(out=pb[:, c * 128:(c + 1) * 128],
                                          in_=X[:, hb * 4 + c, xc:xc + 128], identity=self.ident)
                    return ins
                self.S.add("pe", fn, reads=[self.Xt[t - xo], self.tC], writes=[pbt])
                self.cp("act" if hb == 0 else "dve", oa[:, hb * 512:(hb + 1) * 512], pb[:, :], [pbt], [ot])
            ev = self.dma(yout[(t - o0) * 128:(t - o0 + 1) * 128, :], oa, [ot], [], ot)
            self.out_events.append(ev)

    def build_window(self, xwa, xwt, kind, t0, t1, first, L, leftx):
        X = self.X
        xo = kind["xoff"]
        n = (t1 - t0) * 128
        xc = (t0 - xo) * 128
        xt = self.xtiles(kind, t0, t1)
        right_ok = (t1 - xo) < 16 and t1 < kind["NT"]
        if right_ok:
            self.cp("act", xwa[:, :, 1:n + 2], X[:, :, xc:xc + n + 1], xt + self.xtiles(kind, t1, t1 + 1), [xwt])
        else:
            self.cp("act", xwa[:, :, 1:n + 1], X[:, :, xc:xc + n], xt, [xwt])
            self.S.add("pool", lambda e: e.memset(xwa[:, :, n + 1:n + 2], 0.0), writes=[xwt])
        if first and leftx and xc > 0:
            self.cp("pool", xwa[:, :, 0:1], X[:, :, xc - 1:xc], self.xtiles(kind, t0 - 1, t0), [xwt])
        elif first:
            self.S.add("pool", lambda e: e.memset(xwa[:, :, 0:1], 0.0), writes=[xwt])
        else:
            self.cp("pool", xwa[:, :, 0:1], self.xsave, [self.xsavet], [xwt])
        return n, xc, xt

    def edge_mode(self, L, t0, t1):
        return L["bl"].get(t0), L["br"].get(t1)

    def b_pre(self, kind, l, L, t0, t1, first):
        tC = self.tC
        xwa, xwt = self.xw.next()
        n, xc, xt = self.build_window(xwa, xwt, kind, t0, t1, first, L, False)
        return dict(xwa=xwa, xwt=xwt, n=n, xc=xc, xt=xt, t0=t0, t1=t1)

    def b_q(self, l, ctx):
        tC = self.tC
        xwa, xwt, n = ctx["xwa"], ctx["xwt"], ctx["n"]
        xm = xwa[:, :, 1:n + 1]
        qa, qt = self.qT.next()
        ctx["qa"], ctx["qt"] = qa, qt
        for m in range(4):
            wa, wt = self.load_w8("win", l, m)
            pb, pbt = self.bank()
            self.mm(pb[:, :n], [(wa[:, k, :], xm[:, k, :]) for k in range(8)], [wt, xwt], pbt)
            for pr in (0, 64):
                self.act(qa[pr:pr + 64, m, pr // 64, :n], pb[pr:pr + 64, :n], AF.Identity, [pbt, tC], [qt],
                         bias=self.PAR[pr:pr + 64, l * NPAR + P_BIN + m:l * NPAR + P_BIN + m + 1])

    def phase_b_window(self, kind, l, L, ctx, nwin):
        X = self.X
        d = self.d
        S = self.S
        tC = self.tC
        xwa, xwt, n, xc, xt, t0, t1 = (ctx[k] for k in ("xwa", "xwt", "n", "xc", "xt", "t0", "t1"))
        qa, qt = ctx["qa"], ctx["qt"]
        el, er = self.edge_mode(L, t0, t1)
        xm = xwa[:, :, 1:n + 1]
        for m in range(0):
            wa, wt = self.load_w8("win", l, m)
            pb, pbt = self.bank()
            self.mm(pb[:, :n], [(wa[:, k, :], xm[:, k, :]) for k in range(8)], [wt, xwt], pbt)
            for pr in (0, 64):
                self.act(qa[pr:pr + 64, m, pr // 64, :n], pb[pr:pr + 64, :n], AF.Identity, [pbt, tC], [qt],
                         bias=self.PAR[pr:pr + 64, l * NPAR + P_BIN + m:l * NPAR + P_BIN + m + 1])
        yTa, yTt = self.yaT.next()
        cua, cut = self.cu.next()
        yca, yct = self.yc.next()
        xh = xwa[:, :, 0:n + 2]
        fillers = []

        def f_ugc(m):
            wa, wt = self.load_w8("win", l, 12 + m)
            pu, put = self.bank()
            self.mm(pu[:, :n + 2], [(wa[:, k, :], xh[:, k, :]) for k in range(8)], [wt, xwt], put)
            wa2, wt2 = self.load_w8("win", l, 20 + m)
            pg, pgt = self.bank()
            self.mm(pg[:, :n + 2], [(wa2[:, k, :], xh[:, k, :]) for k in range(8)], [wt2, xwt], pgt)
            ta, tt_ = self.tu.next()
            self.act(ta[:, :n + 2], pu[:, :n + 2], AF.Identity, [put, tC], [tt_], bias=self.par(l, P_BIN + 12 + m))
            self.stt("dve", cua[:, m, :n + 2], pg[:, :n + 2], self.par(l, P_BIN + 20 + m), ta[:, :n + 2],
                     ALU.add, ALU.mult, [pgt, tt_, tC], [cut])

        def f_edge():
            for mode, col in ((el, 0), (er, n + 1)):
                if mode == "zero":
                    S.add("pool", lambda e, col=col: e.memset(cua[:, :, col:col + 1], 0.0), writes=[cut])
                elif mode in ("ftop", "fbot"):
                    f = self.FLG[:, 0:1] if mode == "ftop" else self.FLG[:, 1:2]
                    for m in range(4):
                        self.ts("dve", cua[:, m, col:col + 1], cua[:, m, col:col + 1], f, None, ALU.mult, None,
                                [cut, tC], [cut])

        def f_conv(m):
            if m == 0:
                f_edge()
            va, vt = self.cv.next()
            w0 = self.par(l, P_SCW + m)
            w1 = self.par(l, P_SCW + 4 + m)
            w2 = self.par(l, P_SCW + 8 + m)
            self.act(va[:, :n], cua[:, m, 1:n + 1], AF.Identity, [cut, tC], [vt], bias=self.par(l, P_SCB + m), scale=w1)
            self.fma("dve", va[:, :n], cua[:, m, 0:n], w0, va[:, :n], [cut, tC, vt], [vt])
            self.fma("dve", va[:, :n], cua[:, m, 2:n + 2], w2, va[:, :n], [cut, tC, vt], [vt])
            wa, wt = self.load_w8("win", l, 16 + m)
            pb, pbt = self.bank()
            self.mm(pb[:, :n], [(wa[:, k, :], xm[:, k, :]) for k in range(8)], [wt, xwt], pbt)
            self.stt("dve", yca[:, m, :n], pb[:, :n], self.par(l, P_BIN + 16 + m), va[:, :n], ALU.add, ALU.mult,
                     [pbt, vt, tC], [yct])
        for m in range(4):
            fillers.append(lambda m=m: f_ugc(m))
        for m in range(4):
            fillers.append(lambda m=m: f_conv(m))

        items = []
        pre_edge = None
        for t in range(t0, t1):
            dl = L["q"][t]
            esrc = [s_ for (_, s_) in dl if s_[0] == "e"]
            if esrc and pre_edge is None:
                pre_edge = t
                self.dma(self.Eedge[:, 0:len(esrc), :],
                         d["eb"][l, esrc[0][1]:esrc[0][1] + len(esrc)].rearrange("m p f -> p m f"),
                         [], [self.Eedget], self.Eedget)
            for j, (dd, src) in enumerate(dl):
                for hb in range(2):
                    items.append((t, j, dd, src, len(dl), esrc, hb))
        po = [(self.ps[5], self.pst[5]), (self.ps[6], self.pst[6])]
        self.nbank = 3
        self.pbank = 0

        def emit_scores(it):
            t, j, dd, src, nj, esrc, hb = it
            kc = (t + dd) * 128
            qc = (t - t0) * 128
            self.sbi ^= 1
            pbank, pbt = self.ps[3 + self.sbi], self.pst[3 + self.sbi]

            if src[0] == "i":
                ea, et = self.Eint[:, src[1], hb * 512:(hb + 1) * 512], self.Eintt
            else:
                ea, et = self.Eedge[:, src[1] - esrc[0][1], hb * 512:(hb + 1) * 512], self.Eedget
                if j == 0 and hb == 0 and t != pre_edge:
                    i0 = esrc[0][1]
                    ne = len(esrc)
                    self.dma(self.Eedge[:, 0:ne, :], d["eb"][l, i0:i0 + ne].rearrange("m p f -> p m f"),
                             [], [self.Eedget], self.Eedget)

            def fn(e, hb=hb, kc=kc, qc=qc, pbank=pbank, ea=ea):
                ins = e.matmul(pbank[:, :], lhsT=self.identb, rhs=ea, start=True, stop=False, skip_group_check=True)
                for hh in range(4):
                    h = hb * 4 + hh
                    ins = e.matmul(pbank[:, hh * 128:(hh + 1) * 128],
                                   lhsT=self.kT[:, h // 2, kc:kc + 128],
                                   rhs=qa[:, h // 2, h % 2, qc:qc + 128],
                                   start=False, stop=(hh == 3), skip_group_check=True)
                return ins
            S.add("pe", fn, reads=[self.kTt[t + dd], qt, et, tC], writes=[pbt])
            return pbank, pbt

        def finish_tile(t):
            qc = (t - t0) * 128
            ra, rt = self.rec.next()
            ya, yt = self.ya.next()
            for hb in range(2):
                pv = po[hb][0][:, 0:260].rearrange("p (h d) -> p h d", h=4)
                S.add("dve", lambda e, ra=ra, pv=pv, hb=hb: e.reciprocal(ra[:, hb * 4:hb * 4 + 4].unsqueeze(2), pv[:, :, 64:65]),
                      reads=[po[hb][1]], writes=[rt])
                self.tt("dve", ya[:, hb * 256:(hb + 1) * 256].rearrange("p (h d) -> p h d", h=4), pv[:, :, 0:64],
                        ra[:, hb * 4:hb * 4 + 4].unsqueeze(2).to_broadcast([128, 4, 64]), ALU.mult,
                        [po[hb][1], rt], [yt])
            pbb = self.psb[:, 0:512]

            def fn(e, pbb=pbb, ya=ya):
                ins = None
                for c in range(4):
                    ins = e.transpose(out=pbb[:, c * 128:(c + 1) * 128], in_=ya[:, c * 128:(c + 1) * 128],
                                      identity=self.identb)
                return ins
            S.add("pe", fn, reads=[yt, tC], writes=[self.psbt[0]])
            self.cp("act", yTa[:, :, qc:qc + 128], pbb.rearrange("p (c j) -> p c j", c=4), [self.psbt[0]], [yTt])

        pending = emit_scores(items[0])
        for idx, it in enumerate(items):
            t, j, dd, src, nj, esrc, hb = it
            kt = t + dd
            nxt = emit_scores(items[idx + 1]) if idx + 1 < len(items) else None
            ma, mt = self.Pm.next()
            self.act(ma, pending[0][:, :], AF.Exp, [pending[1]], [mt], scale=0.125)

            def fn(e, hb=hb, kt=kt, ma=ma, j=j, nj=nj, pbank=po[hb][0]):
                ins = None
                for hh in range(4):
                    h = hb * 4 + hh
                    ins = e.matmul(pbank[:, hh * 65:(hh + 1) * 65], lhsT=ma[:, hh * 128:(hh + 1) * 128],
                                   rhs=self.vA[:, kt, h, 0:65], start=(j == 0 and hh == 0), stop=(j == nj - 1),
                                   skip_group_check=True)
                return ins
            S.add("pe", fn, reads=[mt, self.vAt[kt]], writes=[po[hb][1]])
            if fillers and hb == 1:
                fillers.pop(0)()
            if j == nj - 1 and hb == 1:
                finish_tile(t)
            pending = nxt
        self.nbank = 5
        while fillers:
            fillers.pop(0)()
        mga, mgt = self.mrg.next()
        for mo in range(8):
            wa, wt = self.load_w4("wba", l, mo)
            pa_, pat = self.bank()
            self.mm(pa_[:, :n], [(wa[:, k, :], yTa[:, k, :n]) for k in range(4)], [wt, yTt], pat)
            wc, wct = self.load_w4("wbc", l, mo)
            pc_, pct = self.bank()
            self.mm(pc_[:, :n], [(wc[:, k, :], yca[:, k, :n]) for k in range(4)], [wct, yct], pct)
            wg, wgt = self.load_w8("win", l, 24 + mo)
            pga, pgat = self.bank()
            self.mm(pga[:, :n], [(wg[:, k, :], xm[:, k, :]) for k in range(8)], [wgt, xwt], pgat)
            wg2, wgt2 = self.load_w8("win", l, 32 + mo)
            pgc, pgct = self.bank()
            self.mm(pgc[:, :n], [(wg2[:, k, :], xm[:, k, :]) for k in range(8)], [wgt2, xwt], pgct)
            s1, s1t = self.sg.next()
            s2, s2t = self.sg.next()
            self.act(s1[:, :n], pga[:, :n], AF.Sigmoid, [pgat, tC], [s1t], bias=self.par(l, P_BIN + 24 + mo))
            self.act(s2[:, :n], pgc[:, :n], AF.Sigmoid, [pgct, tC], [s2t], bias=self.par(l, P_BIN + 32 + mo))
            u1, u1t = self.t12.next()
            u2, u2t = self.t12.next()
            self.tt("dve", u1[:, :n], s1[:, :n], pa_[:, :n], ALU.mult, [s1t, pat], [u1t])
            self.tt("dve", u2[:, :n], s2[:, :n], pc_[:, :n], ALU.mult, [s2t, pct], [u2t])
            self.tt("pool", mga[:, mo, :n], u1[:, :n], u2[:, :n], ALU.add, [u1t, u2t], [mgt])
        self.cp("pool", self.xsave, X[:, :, xc + n - 1:xc + n], xt, [self.xsavet])
        nctx = None
        if nwin is not None:
            nctx = self.b_pre(kind, l, L, nwin[0], nwin[1], False)
        for mo in range(8):
            wa, wt = self.load_w8("wo", l, mo)
            pb, pbt = self.bank()
            self.mm(pb[:, :n], [(wa[:, k, :], mga[:, k, :n]) for k in range(8)], [wt, mgt], pbt)
            za, zt = self.ztmp.next()
            self.act(za[:, :n], pb[:, :n], AF.Identity, [pbt, tC], [zt], bias=self.par(l, P_BO + mo))
            self.stt("dve", X[:, mo, xc:xc + n], X[:, mo, xc:xc + n], ALPHA, za[:, :n], ALU.mult, ALU.add,
                     xt + [zt], xt)
        if nctx is not None:
            self.b_q(l, nctx)
        self.layer_norm(l, xc, n, xt, P_G1, P_B1, self.zb, self.zq, self.mean, self.rstd, self.msq)
        return nctx

    def c_pre(self, kind, l, L, t0, t1, first):
        tC = self.tC
        xwa, xwt = self.xwc.next()
        n, xc, xt = self.build_window(xwa, xwt, kind, t0, t1, first, L, t0 > L["B"][0])
        el, er = self.edge_mode(L, t0, t1)
        for mode, col in ((el, 0), (er, n + 1)):
            if mode in ("ftop", "fbot"):
                f = self.FLG[:, 0:1] if mode == "ftop" else self.FLG[:, 1:2]
                self.ts("dve", xwa[:, :, col:col + 1], xwa[:, :, col:col + 1], f, None, ALU.mult, None,
                        [xwt, tC], [xwt])
        ga, gt = self.gT.next()
        return dict(xwa=xwa, xwt=xwt, n=n, xc=xc, xt=xt, el=el, er=er, ga=ga, gt=gt, m=0)

    def c_iter(self, l, ctx, count):
        tC = self.tC
        DER = self.DER
        xwa, xwt, n, el, er, ga, gt = (ctx[k] for k in ("xwa", "xwt", "n", "el", "er", "ga", "gt"))
        xh = xwa[:, :, 0:n + 2]
        for m in range(ctx["m"], min(22, ctx["m"] + count)):
            wg, wgt = self.load_w8("wup", l, m)
            wv_, wvt_ = self.load_w8("wup", l, 22 + m)
            pg, pgt = self.bank()
            self.mm(pg[:, :n + 2], [(wg[:, k, :], xh[:, k, :]) for k in range(8)], [wgt, xwt], pgt)
            pv, pvt = self.bank()
            self.mm(pv[:, :n + 2], [(wv_[:, k, :], xh[:, k, :]) for k in range(8)], [wvt_, xwt], pvt)
            accs = []
            for (pp, ppt, ring, mc) in ((pg, pgt, self.accg, m), (pv, pvt, self.accv, 22 + m)):
                aa, at = ring.next()
                w0 = self.par(l, P_FCW + mc)
                w1 = self.par(l, P_FCW + 44 + mc)
                w2 = self.par(l, P_FCW + 88 + mc)
                self.act(aa[:, :n], pp[:, 1:n + 1], AF.Identity, [ppt, tC], [at], bias=DER[:, l, 0, mc:mc + 1], scale=w1)
                self.stt("dve", aa[:, :n], pp[:, 0:n], w0, aa[:, :n], ALU.mult, ALU.add, [ppt, at, tC], [at])
                self.stt("dve", aa[:, :n], pp[:, 2:n + 2], w2, aa[:, :n], ALU.mult, ALU.add, [ppt, at, tC], [at])
                if el is not None:
                    j = 1 if el == "zero" else 3
                    self.tt("pool", aa[:, 0:1], aa[:, 0:1], DER[:, l, j, mc:mc + 1], ALU.subtract, [at, tC], [at])
                if er is not None:
                    j = 2 if er == "zero" else 4
                    self.tt("pool", aa[:, n - 1:n], aa[:, n - 1:n], DER[:, l, j, mc:mc + 1], ALU.subtract, [at, tC], [at])
                accs.append((aa, at))
            la, lt = self.gl.next()
            self.act(la[:, :n], accs[0][0][:, :n], AF.Gelu_apprx_tanh, [accs[0][1]], [lt])
            self.tt("pool", ga[:, m, :n], la[:, :n], accs[1][0][:, :n], ALU.mult, [lt, accs[1][1]], [gt])
        ctx["m"] = min(22, ctx["m"] + count)

    def phase_c_window(self, kind, l, L, ctx, nxt):
        X = self.X
        tC = self.tC
        n, xc, xt, ga, gt = (ctx[k] for k in ("n", "xc", "xt", "ga", "gt"))
        self.c_iter(l, ctx, 22)
        self.cp("pool", self.xsave, X[:, :, xc + n - 1:xc + n], xt, [self.xsavet])
        nctx = None
        if nxt is not None:
            nctx = self.c_pre(kind, l, L, nxt[0], nxt[1], False)
        for mo in range(8):
            wa, wt = self.wdn.next()
            self.dma(wa.rearrange("p k j -> p (k j)"), self.d["wdnb"][l, mo], [], [wt], wt)
            pb, pbt = self.bank()
            self.mm(pb[:, :n], [(wa[:, k, :], ga[:, k, :n]) for k in range(22)], [wt, gt], pbt)
            za, zt = self.ztmp2.next()
            self.act(za[:, :n], pb[:, :n], AF.Identity, [pbt, tC], [zt], bias=self.par(l, P_BDN + mo))
            self.stt("dve", X[:, mo, xc:xc + n], X[:, mo, xc:xc + n], ALPHA, za[:, :n], ALU.mult, ALU.add,
                     xt + [zt], xt)
        if nctx is not None:
            self.c_iter(l, nctx, 6)
        self.layer_norm(l, xc, n, xt, P_G2, P_B2, self.zb2, self.zq2, self.mean2, self.rstd2, self.msq2)
        return nctx

    def build(self):
        nc = self.nc
        self.d["identin"] = nc.dram_tensor("identin", [128, 128], F32, kind="ExternalInput").ap()
        self.setup()
        self.prepass()
        self.barrier(self.c_tiles, self.ab_tiles)
        sk = sample_kind()
        for i in range(self.NS if self.stop != 'P' else 0):
            self.chunk(sk, self.d["xs"][i], self.d["ys"][i])
        for j in range(self.NP):
            self.chunk(prompt_kind(j), self.d["xp"][j], self.d["yp"][j])
        self.S.add("sp", lambda e: e.nop(nofuse=True), extra=self.out_events)
        with ExitStack() as stack:
            self.S.emit(nc, stack)
        return nc


class Ring:
    def __init__(self, items):
        self.items = items
        self.i = 0

    def next(self):
        it = self.items[self.i]
        self.i = (self.i + 1) % len(self.items)
        return it


def wlayout(w):
    K, M = w.shape
    return np.ascontiguousarray(w.reshape(K // 128, 128, M // 128, 128).transpose(2, 1, 0, 3)).reshape(
        M // 128, 128, K)


def col(v):
    return np.ascontiguousarray(v.reshape(-1, 128).T)


def bm_tile(rpb_l, i0, r0, R):
    out = np.full((128, 8, 128), NEG, np.float32)
    kc = np.arange(64)[:, None]
    qc = np.arange(64)[None, :]
    js = np.clip(qc - 8, 0, 48)
    valid = (kc >= js) & (kc < js + 16)
    dc = np.clip(kc - qc + 15, 0, 30)
    for kr in range(2):
        for qr in range(2):
            r = r0 + kr
            i = i0 + qr
            if i < 0 or i >= R:
                continue
            rs = min(max(i - 4, 0), R - 8)
            if not (rs <= r < rs + 8):
                continue
            g = rpb_l[:, r - i + 7][:, dc]
            blk = np.where(valid[None], g, np.float32(NEG))
            out[kr * 64:(kr + 1) * 64, :, qr * 64:(qr + 1) * 64] = blk.transpose(1, 0, 2)
    return out.reshape(128, 1024)


def bm_set(rpb_l, half):
    tiles = []
    big = 1000
    def interior(d):
        return bm_tile(rpb_l, 500, 500 + 2 * d, big)
    for d in range(-2, 3):
        tiles.append(interior(d))
    for t, ds in ((0, range(0, 4)), (1, range(-1, 3)), (14, range(-2, 2)), (15, range(-3, 1))):
        for d in ds:
            tiles.append(bm_tile(rpb_l, 2 * t, 2 * (t + d), 32))
    for t, ds in ((5, range(-2, 4)), (6, range(-2, 3))):
        for d in ds:
            if half == 0:
                tiles.append(bm_tile(rpb_l, 2 * (t - 5), 2 * (t - 5 + d), 128))
            else:
                tiles.append(interior(d))
    for t, ds in ((11, range(-2, 3)), (12, range(-3, 3))):
        for d in ds:
            if half == 1:
                tiles.append(bm_tile(rpb_l, 112 + 2 * (t - 5), 112 + 2 * (t - 5 + d), 128))
            else:
                tiles.append(interior(d))
    assert len(tiles) == NBM
    return np.stack(tiles)


def host_params(inp):
    pars = []
    for l in range(2):
        cols = [col(inp["b_in"][l]),
                np.ascontiguousarray(inp["sc_conv_w"][l].reshape(3, 4, 128).transpose(2, 0, 1)).reshape(128, 12),
                col(inp["sc_conv_b"][l]), col(inp["b_o"][l]), col(inp["ln1_g"][l]), col(inp["ln1_b"][l]),
                col(inp["ffn_b_up"][l]),
                np.ascontiguousarray(inp["ffn_conv_w"][l].reshape(3, 44, 128).transpose(2, 0, 1)).reshape(128, 132),
                col(inp["ffn_conv_b"][l]), col(inp["ffn_b_down"][l]), col(inp["ln2_g"][l]), col(inp["ln2_b"][l])]
        p = np.concatenate(cols, axis=1)
        assert p.shape == (128, NPAR)
        pars.append(p)
    par = np.ascontiguousarray(np.concatenate(pars, axis=1), dtype=np.float32)
    bv = np.ascontiguousarray(np.broadcast_to(
        np.stack([inp["b_in"][l][1024:1536] for l in range(2)]).reshape(1, 1024), (128, 1024)), dtype=np.float32)
    return par, bv


def host_weights(inp):
    out = {}
    for key, name in (("win", "w_in"), ("wba", "w_br_attn"), ("wbc", "w_br_conv"), ("wo", "w_o"),
                      ("wup", "ffn_w_up"), ("wdn", "ffn_w_down")):
        out[key] = np.stack([wlayout(np.asarray(inp[name][l], np.float32)) for l in range(2)])
    return out


_CACHE = {}


def get_nc(NS, NP):
    k = (NS, NP)
    if k not in _CACHE:
        _CACHE[k] = Builder(NS, NP).build()
    return _CACHE[k]


def kernel(**inputs):
    inp = {k: np.asarray(v) for k, v in inputs.items()}
    xp_full = inp["x_prompt"]
    xs_full = inp["x_sample"]
    n = 8
    nc = get_nc(4, 4)
    par, bv = host_params(inp)
    W = host_weights(inp)
    ident = np.eye(128, dtype=np.float32)
    bms = [np.stack([bm_set(inp["attn_rpb"][l], h) for l in range(2)]) for h in range(2)]
    in_maps = []
    for c in range(n):
        p, half = c // 2, c % 2
        xpc = np.zeros((4, 17 * 128, D), np.float32)
        for j in range(4):
            r0 = 64 * half + 16 * j
            lo, hi = r0 - 10, r0 + 24
            slo, shi = max(lo, 0), min(hi, 128)
            xpc[j, (slo - lo) * 64:(shi - lo) * 64] = xp_full[p, slo * 64:shi * 64]
        ftop = 0.0 if half == 0 else 1.0
        fbot = 0.0 if half == 1 else 1.0
        flg = np.ascontiguousarray(np.broadcast_to(np.array([ftop, fbot, 1 - ftop, 1 - fbot], np.float32), (128, 4)))
        m = dict(xs=np.ascontiguousarray(xs_full[4 * c:4 * c + 4]), xp=xpc, bm=bms[half], par=par, bv=bv,
                 flg=flg, identin=ident)
        m.update(W)
        in_maps.append(m)
    res = run_bass_kernel_spmd(nc, in_maps, core_ids=list(range(n)))
    y_prompt = np.empty_like(xp_full)
    y_sample = np.empty_like(xs_full)
    for c in range(n):
        r = res.results[c]
        p, half = c // 2, c % 2
        y_sample[4 * c:4 * c + 4] = r["ys"]
        for j in range(4):
            r0 = 64 * half + 16 * j
            y_prompt[p, r0 * 64:(r0 + 16) * 64] = r["yp"][j]
    return (y_prompt, y_sample)
```

```python
import numpy as np
import ml_dtypes
from contextlib import ExitStack
import concourse.bass as bass
import concourse.mybir as mybir
from concourse.bass_utils import run_bass_kernel_spmd

F32 = mybir.dt.float32
BF16 = mybir.dt.bfloat16
AF = mybir.ActivationFunctionType
ALU = mybir.AluOpType

D = 1024
NH = 8
DH = 64
DFF = 2816
ALPHA = float(4 ** 0.25)
EPS = 1e-5
NEG = -30000.0
NPAR = 324
P_BIN, P_SCW, P_SCB, P_BO, P_G1, P_B1, P_BUP, P_FCW, P_FCB, P_BDN, P_G2, P_B2 = (
    0, 40, 52, 56, 64, 72, 80, 124, 256, 300, 308, 316)
NBM = 43

ENGS = ("pe", "act", "dve", "pool", "sp")


class Tl:
    __slots__ = ("name", "w", "rs", "rd", "sem", "cnt")

    def __init__(self, name):
        self.name = name
        self.w = None
        self.rs = {}
        self.rd = []
        self.sem = None
        self.cnt = 0


class Op:
    __slots__ = ("eng", "fn", "waits", "sig", "idx", "sigval", "dsem", "line")


class Sched:
    def __init__(self):
        self.q = {e: [] for e in ENGS}
        self.dma_tiles = []

    def add(self, eng, fn, reads=(), writes=(), dma=None, extra=()):
        op = Op()
        op.eng = eng
        op.fn = fn
        op.sig = False
        op.idx = len(self.q[eng])
        op.dsem = dma
        op.sigval = 0
        import sys as _s
        f = _s._getframe(1)
        ls = []
        while f is not None and len(ls) < 4:
            ls.append(f.f_lineno)
            f = f.f_back
        op.line = ls
        deps = list(extra)
        for t in reads:
            if t.w is not None:
                deps.append(t.w)
        for t in writes:
            if t.w is not None:
                deps.append(t.w)
            deps.extend(t.rs.values())
            deps.extend(t.rd)
        waits = []
        seen = set()
        for d in deps:
            if id(d) in seen:
                continue
            seen.add(id(d))
            if isinstance(d, Op):
                if d.eng == eng:
                    if eng in ("pe", "sp"):
                        continue
                    if op.idx - d.idx > 3:
                        continue
                d.sig = True
            waits.append(d)
        op.waits = waits
        if dma is not None:
            if dma.sem is None:
                dma.sem = True
                self.dma_tiles.append(dma)
            dma.cnt += 16
            ev = (dma, dma.cnt)
        else:
            ev = op
        for t in reads:
            if isinstance(ev, Op):
                t.rs[eng] = ev
            else:
                t.rd.append(ev)
        for t in writes:
            t.w = ev
            t.rs = {}
            t.rd = []
        self.q[eng].append(op)
        return ev

    def emit(self, nc, stack):
        esem = {}
        for e in ("pe", "act", "dve", "pool"):
            esem[e] = stack.enter_context(nc.semaphore("s_" + e))
        for t in self.dma_tiles:
            t.sem = stack.enter_context(nc.semaphore("d_" + t.name))
        for e in ENGS:
            c = 0
            for op in self.q[e]:
                if op.sig:
                    c += 1
                op.sigval = c
        q = self.q

        def run(name, eng):
            waited = {}
            for op in q[name]:
                for d in op.waits:
                    if isinstance(d, Op):
                        sem, val = esem[d.eng], d.sigval
                    else:
                        sem, val = d[0].sem, d[1]
                    k = id(sem)
                    if waited.get(k, 0) < val:
                        eng.wait_ge(sem, val)
                        waited[k] = val
                ins = op.fn(eng)
                if op.dsem is not None:
                    ins.then_inc(op.dsem.sem, 16)
                elif op.sig:
                    ins.then_inc(esem[name], 1)

        with nc.Block() as block:
            @block.tensor
            def _(e):
                run("pe", e)

            @block.scalar
            def _(e):
                run("act", e)

            @block.vector
            def _(e):
                run("dve", e)

            @block.gpsimd
            def _(e):
                run("pool", e)

            @block.sync
            def _(e):
                run("sp", e)


def split_windows(t0, t1, forced=()):
    cuts = sorted(set([t0, t1] + [f for f in forced if t0 < f < t1]))
    out = []
    for a, b in zip(cuts[:-1], cuts[1:]):
        n = b - a
        k = (n + 2) // 3
        base, rem = divmod(n, k)
        s = a
        for i in range(k):
            ln = base + (1 if i < rem else 0)
            out.append((s, s + ln))
            s += ln
    return out


def sample_kind():
    q = {}
    for t in range(16):
        if t == 0:
            q[t] = [(d, ("e", 5 + i)) for i, d in enumerate(range(0, 4))]
        elif t == 1:
            q[t] = [(d, ("e", 9 + i)) for i, d in enumerate(range(-1, 3))]
        elif t == 14:
            q[t] = [(d, ("e", 13 + i)) for i, d in enumerate(range(-2, 2))]
        elif t == 15:
            q[t] = [(d, ("e", 17 + i)) for i, d in enumerate(range(-3, 1))]
        else:
            q[t] = [(d, ("i", d + 2)) for d in range(-2, 3)]
    lay = dict(A=(0, 16), B=(0, 16), C=(0, 16), q=q, forced=(),
               bl={0: "zero"}, br={16: "zero"})
    return dict(NT=16, xoff=0, L=[lay, lay], out=(0, 16))


def prompt_kind(j):
    def qmap(b0, b1, a0, a1):
        q = {}
        for t in range(b0, b1):
            if j == 0 and t == 5:
                lst = [(d, ("e", 21 + i)) for i, d in enumerate(range(-2, 4))]
            elif j == 0 and t == 6:
                lst = [(d, ("e", 27 + i)) for i, d in enumerate(range(-2, 3))]
            elif j == 3 and t == 11:
                lst = [(d, ("e", 32 + i)) for i, d in enumerate(range(-2, 3))]
            elif j == 3 and t == 12:
                lst = [(d, ("e", 37 + i)) for i, d in enumerate(range(-3, 3))]
            else:
                lst = [(d, ("i", d + 2)) for d in range(-2, 3)]
            q[t] = [(d, s) for (d, s) in lst if a0 <= t + d < a1]
        return q
    bl = {5: "ftop"} if j == 0 else {}
    br = {13: "fbot"} if j == 3 else {}
    l0 = dict(A=(0, 17), B=(2, 16), C=(2, 15), q=qmap(2, 16, 0, 17), forced=(5, 13), bl=bl, br=br)
    l1 = dict(A=(2, 15), B=(4, 14), C=(5, 13), q=qmap(4, 14, 2, 15), forced=(5, 13), bl=bl, br=br)
    return dict(NT=17, xoff=2, L=[l0, l1], out=(5, 13))


class Builder:
    def __init__(self, NS, NP, dbg=False, stop=None):
        self.NS, self.NP = NS, NP
        self.stop = stop
        self.bstop = None
        if stop is not None and len(stop) == 3:
            self.bstop = int(stop[2])
            self.stop = stop[:2]
        self.nc = nc = bass.Bass("TRN2", target_bir_lowering=False)
        self.S = Sched()
        dt = nc.dram_tensor
        self.d = d = {}
        if NS:
            d["xs"] = dt("xs", [NS, 2048, D], F32, kind="ExternalInput").ap()
            d["ys"] = dt("ys", [NS, 2048, D], F32, kind="ExternalOutput").ap()
        if NP:
            d["xp"] = dt("xp", [NP, 17 * 128, D], F32, kind="ExternalInput").ap()
            d["yp"] = dt("yp", [NP, 1024, D], F32, kind="ExternalOutput").ap()
        self.wshapes = dict(win=(40, 1024), wba=(8, 512), wbc=(8, 512), wo=(8, 1024),
                            wup=(44, 1024), wdn=(8, 2816))
        for k, (n, f) in self.wshapes.items():
            d[k] = dt(k, [2, n, 128, f], F32, kind="ExternalInput").ap()
            d[k + "b"] = dt(k + "b", [2, n, 128, f], BF16, kind="Internal").ap()
        d["bm"] = dt("bm", [2, NBM, 128, 1024], F32, kind="ExternalInput").ap()
        d["eb"] = dt("eb", [2, NBM, 128, 1024], BF16, kind="Internal").ap()
        d["par"] = dt("par", [128, 2 * NPAR], F32, kind="ExternalInput").ap()
        d["bv"] = dt("bv", [128, 1024], F32, kind="ExternalInput").ap()
        d["flg"] = dt("flg", [128, 4], F32, kind="ExternalInput").ap()
        self.off = 0
        self.arena = nc.alloc_sbuf_tensor("arena", [128, 53100], F32)
        self.cap = 53100 * 4
        self.pbank = 0
        self.nbank = 5
        self.sbi = 0
        self.out_events = []

    def sb(self, shape, dtype, name=None):
        n = 1
        for s in shape:
            n *= s
        nbytes = n * (4 if dtype == F32 else 2)
        nbytes = (nbytes + 31) // 32 * 32
        assert self.off + nbytes <= self.cap, ("SBUF overflow", name, self.off + nbytes)
        w0 = self.off // 4
        ap = self.arena[:, w0:w0 + nbytes // 4]
        if dtype != F32:
            ap = ap.bitcast(BF16)
            ap = ap[:, 0:n]
        else:
            ap = ap[:, 0:n]
        self.off += nbytes
        if len(shape) == 2:
            ap = ap.rearrange("p (a b) -> p a b", a=shape[0])
        elif len(shape) == 3:
            ap = ap.rearrange("p (a b c) -> p a b c", a=shape[0], b=shape[1])
        return ap

    def ring(self, n, shape, dtype, name):
        return Ring([(self.sb(shape, dtype, name), Tl(f"{name}{i}")) for i in range(n)])

    def bank(self):
        self.pbank = (self.pbank + 1) % self.nbank
        b = self.pbank
        return self.ps[b], self.pst[b]

    def mm(self, out, pairs, reads, wt, flags=None):
        pairs = list(pairs)

        def fn(e):
            n = len(pairs)
            ins = None
            for i, (l, r) in enumerate(pairs):
                st, sp = (i == 0, i == n - 1) if flags is None else flags
                ins = e.matmul(out, lhsT=l, rhs=r, start=st, stop=sp, skip_group_check=True)
            return ins
        self.S.add("pe", fn, reads=reads, writes=[wt])

    def dma(self, out, in_, reads, writes, sem):
        return self.S.add("sp", lambda e: e.dma_start(out=out, in_=in_), reads=reads, writes=writes, dma=sem)

    def act(self, out, in_, func, reads, writes, bias=0.0, scale=1.0):
        self.S.add("act", lambda e: e.activation(out=out, in_=in_, func=func, bias=bias, scale=scale),
                   reads=reads, writes=writes)

    def ts(self, eng, out, in0, s1, s2, op0, op1, reads, writes):
        if s2 is None:
            self.S.add(eng, lambda e: e.tensor_scalar(out, in0, s1, None, op0), reads=reads, writes=writes)
        else:
            self.S.add(eng, lambda e: e.tensor_scalar(out, in0, s1, s2, op0, op1), reads=reads, writes=writes)

    def stt(self, eng, out, in0, sc, in1, op0, op1, reads, writes):
        self.S.add(eng, lambda e: e.scalar_tensor_tensor(out, in0, sc, in1, op0, op1), reads=reads, writes=writes)

    def fma(self, eng, out, in0, sc, in1, reads, writes):
        if eng == "dve":
            self.stt("dve", out, in0, sc, in1, ALU.mult, ALU.add, reads, writes)
        else:
            pa, pt = self.ptmp.next()
            shp = list(out.shape)
            pv = pa[:, :shp[-1]]
            self.ts(eng, pv, in0, sc, None, ALU.mult, None, reads, [pt])
            self.tt(eng, out, pv, in1, ALU.add, list(reads) + [pt], writes)

    def tt(self, eng, out, in0, in1, op, reads, writes):
        self.S.add(eng, lambda e: e.tensor_tensor(out, in0, in1, op), reads=reads, writes=writes)

    def cp(self, eng, out, in_, reads, writes):
        if eng == "act":
            self.S.add(eng, lambda e: e.copy(out, in_), reads=reads, writes=writes)
        else:
            self.S.add(eng, lambda e: e.tensor_copy(out, in_), reads=reads, writes=writes)

    def par(self, l, off, n=1):
        return self.PAR[:, l * NPAR + off: l * NPAR + off + n]

    def setup(self):
        nc = self.nc
        d = self.d
        self.ps = [nc.alloc_psum_tensor(f"ps{i}", [128, 512], F32) for i in range(7)]
        self.pst = [Tl(f"ps{i}") for i in range(7)]
        self.psb = nc.alloc_psum_tensor("psb", [128, 1024], BF16)
        self.psbt = [Tl("psb0"), Tl("psb1")]
        self.psbi = 0
        self.X = self.sb([8, 16 * 128], F32, "X")
        self.Xt = [Tl(f"X{t}") for t in range(16)]
        self.PAR = self.sb([2 * NPAR], F32, "PAR")
        self.BV = self.sb([2, 512], F32, "BV")
        self.FLG = self.sb([4], F32, "FLG")
        self.DER = self.sb([2, 5, 44], F32, "DER")
        self.ident = self.sb([128], F32, "ident")
        self.identb = self.sb([128], BF16, "identb")
        self.ones = self.sb([128], BF16, "ones")
        self.epsc = self.sb([1], F32, "epsc")
        self.barbuf = self.sb([1], F32, "barbuf")
        self.tC = Tl("const")
        self.xsave = self.sb([8, 1], F32, "xsave")
        self.xsavet = Tl("xsave")
        self.wring = self.ring(6, [8, 128], BF16, "w8")
        self.xw = self.ring(2, [8, 386], BF16, "xw")
        self.ptmp = self.ring(1, [386], F32, "ptmp")
        self.mark = self.off
        self.kT = self.sb([4, 17 * 128], BF16, "kT")
        self.kTt = [Tl(f"kT{t}") for t in range(17)]
        self.vA = self.sb([17, 8, 68], BF16, "vA")
        self.vAt = [Tl(f"vA{t}") for t in range(17)]
        self.Eint = self.sb([5, 1024], BF16, "Eint")
        self.Eintt = Tl("Eint")
        self.Eedge = self.sb([6, 1024], BF16, "Eedge")
        self.Eedget = Tl("Eedge")
        self.wv = Ring([(self.Eedge.rearrange("p a b -> p (a b)")[:, 0:4096].rearrange("p (k j) -> p k j", k=8), self.Eedget)])
        self.xstage = self.ring(1, [1024], F32, "xstage")
        self.qT = self.ring(1, [4, 2, 384], BF16, "qT")
        self.Pm = self.ring(4, [512], BF16, "Pm")
        self.ya = self.ring(1, [512], BF16, "ya")
        self.rec = self.ring(2, [8], F32, "rec")
        self.yaT = self.ring(1, [4, 384], BF16, "yaT")
        self.cu = self.ring(1, [4, 386], F32, "cu")
        self.tu = self.ring(1, [386], F32, "tu")
        self.cv = self.ring(1, [384], F32, "cv")
        self.yc = self.ring(1, [4, 384], BF16, "yc")
        self.w4ring = self.ring(3, [4, 128], BF16, "w4")
        self.sg = self.ring(2, [384], F32, "sg")
        self.t12 = self.ring(2, [384], F32, "t12")
        self.mrg = self.ring(1, [8, 384], BF16, "mrg")
        self.ztmp = self.tu
        self.zb = self.ring(2, [384], BF16, "zb")
        self.zq = self.ring(2, [384], BF16, "zq")
        self.mean = Ring([self.t12.items[0]])
        self.rstd = Ring([self.t12.items[1]])
        self.msq = self.ring(1, [384], F32, "msq")
        endAB = self.off
        self.off = self.mark
        self.xwc = self.ring(2, [8, 386], BF16, "xwc")
        self.gT = self.ring(2, [22, 384], BF16, "gT")
        self.wdn = self.ring(2, [22, 128], BF16, "wdn")
        self.accg = self.ring(2, [384], F32, "accg")
        self.accv = self.ring(2, [384], F32, "accv")
        self.gl = self.ring(2, [384], F32, "gl")
        self.ostage = self.ring(2, [1024], F32, "ostage")
        self.ztmp2 = self.ring(1, [386], F32, "ztmp2")
        self.zb2 = self.ring(2, [384], BF16, "zb2")
        self.zq2 = self.ring(2, [384], BF16, "zq2")
        self.mean2 = self.ring(1, [384], F32, "mean2")
        self.rstd2 = self.ring(1, [384], F32, "rstd2")
        self.msq2 = self.ring(1, [384], F32, "msq2")
        self.stF = self.ring(4, [1024], F32, "stF")
        self.stB = self.ring(4, [1024], BF16, "stB")
        endC = self.off
        self.off = max(endAB, endC)
        print("SBUF bytes/partition: persistent", self.mark, "B", endAB - self.mark, "C", endC - self.mark, "cap", self.cap)
        self.ab_tiles = self.kTt + self.vAt + [self.Eintt, self.Eedget]
        self.c_tiles = []
        for r in (self.xstage, self.qT, self.Pm, self.ya, self.rec, self.yaT,
                  self.cu, self.tu, self.cv, self.yc, self.w4ring, self.sg, self.t12, self.mrg,
                  self.zb, self.zq, self.msq):
            self.ab_tiles += [t for _, t in r.items]
        for r in (self.xwc, self.gT, self.wdn, self.accg, self.accv, self.gl, self.ostage, self.ztmp2,
                  self.zb2, self.zq2, self.mean2, self.rstd2, self.msq2, self.stF, self.stB):
            self.c_tiles += [t for _, t in r.items]

        S = self.S
        tC = self.tC
        S.add("pool", lambda e: e.memset(self.ones, 1.0 / 1024.0), writes=[tC])
        S.add("pool", lambda e: e.memset(self.epsc, EPS), writes=[tC])
        self.dma(self.PAR, d["par"], [], [tC], tC)
        self.dma(self.BV.rearrange("p a b -> p (a b)"), d["bv"], [], [tC], tC)
        self.dma(self.FLG, d["flg"], [], [tC], tC)
        self.dma(self.ident, d["identin"], [], [tC], tC)
        self.cp("dve", self.identb, self.ident, [tC], [tC])
        for l in range(2):
            w0 = self.par(l, P_FCW, 44)
            w1 = self.par(l, P_FCW + 44, 44)
            w2 = self.par(l, P_FCW + 88, 44)
            bup = self.par(l, P_BUP, 44)
            fcb = self.par(l, P_FCB, 44)
            D_ = self.DER
            self.tt("dve", D_[:, l, 0, :], w0, w1, ALU.add, [tC], [tC])
            self.tt("dve", D_[:, l, 0, :], D_[:, l, 0, :], w2, ALU.add, [tC], [tC])
            self.tt("dve", D_[:, l, 0, :], D_[:, l, 0, :], bup, ALU.mult, [tC], [tC])
            self.tt("dve", D_[:, l, 0, :], D_[:, l, 0, :], fcb, ALU.add, [tC], [tC])
            self.tt("dve", D_[:, l, 1, :], bup, w0, ALU.mult, [tC], [tC])
            self.tt("dve", D_[:, l, 2, :], bup, w2, ALU.mult, [tC], [tC])
            self.ts("dve", D_[:, l, 3, :], D_[:, l, 1, :], self.FLG[:, 2:3], None, ALU.mult, None, [tC], [tC])
            self.ts("dve", D_[:, l, 4, :], D_[:, l, 2, :], self.FLG[:, 3:4], None, ALU.mult, None, [tC], [tC])

    def barrier(self, old_tiles, new_tiles):
        S = self.S
        S.add("pool", lambda e: e.memset(self.barbuf, 0.0), writes=list(old_tiles))
        last = old_tiles[0].w
        for t in new_tiles:
            t.w = last
            t.rs = {}
            t.rd = []

    def prepass(self):
        d = self.d
        jobs = []
        for k, (n, f) in self.wshapes.items():
            for l in range(2):
                for m in range(n):
                    for c0 in range(0, f, 1024):
                        c1 = min(c0 + 1024, f)
                        jobs.append((d[k][l, m][:, c0:c1], d[k + "b"][l, m][:, c0:c1], c1 - c0, "cast"))
        for l in range(2):
            for m in range(NBM):
                jobs.append((d["bm"][l, m], d["eb"][l, m], 1024, "exp"))
        evs = []
        ci = 0
        LOOK = 3
        staged = {}

        def issue_in(i):
            src, dst, f, kind = jobs[i]
            fa, ft = self.stF.next()
            self.dma(fa[:, :f], src, [], [ft], ft)
            staged[i] = (fa, ft)
        for i in range(min(LOOK, len(jobs))):
            issue_in(i)
        for i, (src, dst, f, kind) in enumerate(jobs):
            if i + LOOK < len(jobs):
                issue_in(i + LOOK)
            fa, ft = staged.pop(i)
            ba, bt = self.stB.next()
            if kind == "exp":
                self.act(ba[:, :f], fa[:, :f], AF.Identity, [ft], [bt], scale=8.0)
            else:
                eng = ("dve", "act")[ci % 2]
                ci += 1
                self.cp(eng, ba[:, :f], fa[:, :f], [ft], [bt])
            evs.append(self.dma(dst, ba[:, :f], [bt], [], bt))
        self.S.add("sp", lambda e: e.nop(nofuse=True), extra=evs)

    def load_w8(self, key, l, mc):
        ap, t = self.wring.next()
        self.dma(ap.rearrange("p k j -> p (k j)"), self.d[key + "b"][l, mc], [], [t], t)
        return ap, t

    def load_w4(self, key, l, mc):
        ap, t = self.w4ring.next()
        self.dma(ap.rearrange("p k j -> p (k j)"), self.d[key + "b"][l, mc], [], [t], t)
        return ap, t

    def layer_norm(self, l, c0, n, xt, goff, boff, zb, zq, mean, rstd, msq):
        X = self.X
        pm, pmt = self.bank()
        pq, pqt = self.bank()
        for c in range(8):
            za, zt = zb.next()
            qa, qt = zq.next()
            self.cp("act", za[:, :n], X[:, c, c0:c0 + n], xt, [zt])
            self.act(qa[:, :n], X[:, c, c0:c0 + n], AF.Square, xt, [qt])
            self.mm(pm[:, :n], [(self.ones, za[:, :n])], [zt, self.tC], pmt, flags=(c == 0, c == 7))
            self.mm(pq[:, :n], [(self.ones, qa[:, :n])], [qt, self.tC], pqt, flags=(c == 0, c == 7))
        ma, mt = mean.next()
        ra, rt = rstd.next()
        sa, st = msq.next()
        self.cp("act", ma[:, :n], pm[:, :n], [pmt], [mt])
        self.tt("dve", sa[:, :n], ma[:, :n], ma[:, :n], ALU.mult, [mt], [st])
        self.tt("dve", sa[:, :n], pq[:, :n], sa[:, :n], ALU.subtract, [pqt, st], [st])
        self.ts("dve", sa[:, :n], sa[:, :n], EPS, None, ALU.add, None, [st], [st])
        self.act(ra[:, :n], sa[:, :n], AF.Sqrt, [st], [rt])
        ma2, m2t = self.ptmp.next()
        self.S.add("dve", lambda e: e.reciprocal(ra[:, :n], ra[:, :n]), reads=[rt], writes=[rt])
        self.tt("dve", ma2[:, :n], ra[:, :n], ra[:, :n], ALU.mult, [rt], [m2t])
        self.tt("dve", ma2[:, :n], ma2[:, :n], sa[:, :n], ALU.mult, [m2t, st], [m2t])
        self.ts("dve", ma2[:, :n], ma2[:, :n], -0.5, 1.5, ALU.mult, ALU.add, [m2t], [m2t])
        self.tt("dve", ra[:, :n], ra[:, :n], ma2[:, :n], ALU.mult, [rt, m2t], [rt])
        xv = X[:, :, c0:c0 + n]
        self.tt("dve", xv, xv, ma[:, :n].unsqueeze(1).to_broadcast([128, 8, n]), ALU.subtract, xt + [mt], xt)
        self.tt("dve", xv, xv, ra[:, :n].unsqueeze(1).to_broadcast([128, 8, n]), ALU.mult, xt + [rt], xt)
        for c in range(8):
            eng = "dve"
            if eng == "act":
                self.act(X[:, c, c0:c0 + n], X[:, c, c0:c0 + n], AF.Identity, xt + [self.tC], xt,
                         bias=self.par(l, boff + c), scale=self.par(l, goff + c))
            else:
                self.ts("dve", X[:, c, c0:c0 + n], X[:, c, c0:c0 + n], self.par(l, goff + c),
                        self.par(l, boff + c), ALU.mult, ALU.add, xt + [self.tC], xt)

    def chunk(self, kind, xin, yout):
        for l in range(2):
            self.chunk_layer(kind, l, xin, yout)
            if self.stop is not None and self.stop[0] == str(l):
                break

    def xtiles(self, kind, t0, t1):
        xo = kind["xoff"]
        return [self.Xt[t - xo] for t in range(t0, t1) if 0 <= t - xo < 16]

    def kv_from_xb(self, l, xb, xbt, tiles, wvap, wvt):
        n = len(tiles) * 128
        t0 = tiles[0]
        for m in range(4):
            wa, wt = self.load_w8("win", l, 4 + m)
            pb, pbt = self.bank()
            self.mm(pb[:, :n], [(wa[:, k, :], xb[:, k, 0:n]) for k in range(8)], [wt, xbt], pbt)
            self.act(self.kT[:, m, t0 * 128:t0 * 128 + n], pb[:, :n], AF.Identity, [pbt, self.tC],
                     [self.kTt[t] for t in tiles], bias=self.par(l, P_BIN + 4 + m))
        for i, t in enumerate(tiles):
            pb, pbt = self.bank()
            self.mm(pb[:, :], [(xb[:, k, i * 128:(i + 1) * 128], wvap[:, k, :]) for k in range(8)],
                    [wvt, xbt], pbt)
            self.tt("dve", self.vA[:, t, :, 0:64], pb[:, :].rearrange("p (h d) -> p h d", h=8),
                    self.BV[:, l, :].rearrange("p (h d) -> p h d", h=8), ALU.add,
                    [pbt, self.tC], [self.vAt[t]])

    def chunk_layer(self, kind, l, xin, yout):
        L = kind["L"][l]
        xo = kind["xoff"]
        X = self.X
        d = self.d
        a0, a1 = L["A"]
        b0, b1 = L["B"]
        c0_, c1_ = L["C"]
        wvap, wvt = self.wv.next()
        for m_ in range(4):
            self.dma(wvap[:, :, m_ * 128:(m_ + 1) * 128],
                     d["winb"][l, 8 + m_].rearrange("p (k j) -> p k j", k=8), [], [wvt], wvt)
        self.S.add("pool", lambda e: e.memset(self.vA[:, :, :, 64:65], 1.0), writes=self.vAt)
        qa0, qt0 = self.qT.items[0]
        self.S.add("pool", lambda e: e.memset(qa0[0:64, :, 1, :], 0.0), writes=[qt0])
        self.S.add("pool", lambda e: e.memset(qa0[64:128, :, 0, :], 0.0), writes=[qt0])
        self.dma(self.Eint, d["eb"][l, 0:5].rearrange("m p f -> p m f"), [], [self.Eintt], self.Eintt)
        t = a0
        while t < a1:
            tiles = list(range(t, min(t + 3, a1)))
            t += 3
            n = len(tiles) * 128
            xb, xbt = self.xw.next()
            if l == 0:
                for i, tt_ in enumerate(tiles):
                    sa, st = self.xstage.next()
                    self.dma(sa, xin[tt_ * 128:(tt_ + 1) * 128, :], [], [st], st)
                    for hb in range(2):
                        pb, pbt = self.bank()

                        def fn(e, pb=pb, sa=sa, hb=hb):
                            ins = None
                            for c in range(4):
                                ins = e.transpose(out=pb[:, c * 128:(c + 1) * 128],
                                                  in_=sa[:, (hb * 4 + c) * 128:(hb * 4 + c + 1) * 128],
                                                  identity=self.ident)
                            return ins
                        self.S.add("pe", fn, reads=[st, self.tC], writes=[pbt])
                        pv = pb[:, :].rearrange("p (c j) -> p c j", c=4)
                        if 0 <= tt_ - xo < 16:
                            xc = (tt_ - xo) * 128
                            self.cp("dve", X[:, hb * 4:hb * 4 + 4, xc:xc + 128], pv, [pbt], [self.Xt[tt_ - xo]])
                            self.cp("act", xb[:, hb * 4:hb * 4 + 4, i * 128:(i + 1) * 128],
                                    X[:, hb * 4:hb * 4 + 4, xc:xc + 128], [self.Xt[tt_ - xo]], [xbt])
                        else:
                            self.cp("act", xb[:, hb * 4:hb * 4 + 4, i * 128:(i + 1) * 128], pv, [pbt], [xbt])
            else:
                xc = (tiles[0] - xo) * 128
                self.cp("act", xb[:, :, 0:n], X[:, :, xc:xc + n], self.xtiles(kind, tiles[0], tiles[-1] + 1), [xbt])
            self.kv_from_xb(l, xb, xbt, tiles, wvap, wvt)
        if self.stop == f"{l}A":
            return
        wins = split_windows(b0, b1, L["forced"])
        ctx = self.b_pre(kind, l, L, wins[0][0], wins[0][1], True)
        self.b_q(l, ctx)
        for wi, (t0, t1) in enumerate(wins):
            ctx = self.phase_b_window(kind, l, L, ctx, wins[wi + 1] if wi + 1 < len(wins) else None)
        self.barrier(self.ab_tiles, self.c_tiles)
        if self.stop == f"{l}B":
            self.write_out(kind, yout)
            self.barrier(self.c_tiles, self.ab_tiles)
            return
        wins = split_windows(c0_, c1_, L["forced"])
        ctx = self.c_pre(kind, l, L, wins[0][0], wins[0][1], True)
        for wi, (t0, t1) in enumerate(wins):
            ctx = self.phase_c_window(kind, l, L, ctx, wins[wi + 1] if wi + 1 < len(wins) else None)
        if l == 1 or self.stop == f"{l}C":
            self.write_out(kind, yout)
        self.barrier(self.c_tiles, self.ab_tiles)

    def write_out(self, kind, yout):
        X = self.X
        xo = kind["xoff"]
        o0, o1 = kind["out"]
        for t in range(o0, o1):
            xc = (t - xo) * 128
            oa, ot = self.ostage.next()
            for hb in range(2):
                pb, pbt = self.bank()

                def fn(e, pb=pb, hb=hb, xc=xc):
                    ins = None
                    for c in range(4):
                        ins = e.transpose(out=pb[:, c * 128:(c + 1) * 128],
                                          in_=X[:, hb * 4 + c, xc:xc + 128], identity=self.ident)
                    return ins
                self.S.add("pe", fn, reads=[self.Xt[t - xo], self.tC], writes=[pbt])
                self.cp("act" if hb == 0 else "dve", oa[:, hb * 512:(hb + 1) * 512], pb[:, :], [pbt], [ot])
            ev = self.dma(yout[(t - o0) * 128:(t - o0 + 1) * 128, :], oa, [ot], [], ot)
            self.out_events.append(ev)

    def build_window(self, xwa, xwt, kind, t0, t1, first, L, leftx):
        X = self.X
        xo = kind["xoff"]
        n = (t1 - t0) * 128
        xc = (t0 - xo) * 128
        xt = self.xtiles(kind, t0, t1)
        right_ok = (t1 - xo) < 16 and t1 < kind["NT"]
        if right_ok:
            self.cp("act", xwa[:, :, 1:n + 2], X[:, :, xc:xc + n + 1], xt + self.xtiles(kind, t1, t1 + 1), [xwt])
        else:
            self.cp("act", xwa[:, :, 1:n + 1], X[:, :, xc:xc + n], xt, [xwt])
            self.S.add("pool", lambda e: e.memset(xwa[:, :, n + 1:n + 2], 0.0), writes=[xwt])
        if first and leftx and xc > 0:
            self.cp("pool", xwa[:, :, 0:1], X[:, :, xc - 1:xc], self.xtiles(kind, t0 - 1, t0), [xwt])
        elif first:
            self.S.add("pool", lambda e: e.memset(xwa[:, :, 0:1], 0.0), writes=[xwt])
        else:
            self.cp("pool", xwa[:, :, 0:1], self.xsave, [self.xsavet], [xwt])
        return n, xc, xt

    def edge_mode(self, L, t0, t1):
        return L["bl"].get(t0), L["br"].get(t1)

    def b_pre(self, kind, l, L, t0, t1, first):
        tC = self.tC
        xwa, xwt = self.xw.next()
        n, xc, xt = self.build_window(xwa, xwt, kind, t0, t1, first, L, False)
        return dict(xwa=xwa, xwt=xwt, n=n, xc=xc, xt=xt, t0=t0, t1=t1)

    def b_q(self, l, ctx):
        tC = self.tC
        xwa, xwt, n = ctx["xwa"], ctx["xwt"], ctx["n"]
        xm = xwa[:, :, 1:n + 1]
        qa, qt = self.qT.next()
        ctx["qa"], ctx["qt"] = qa, qt
        for m in range(4):
            wa, wt = self.load_w8("win", l, m)
            pb, pbt = self.bank()
            self.mm(pb[:, :n], [(wa[:, k, :], xm[:, k, :]) for k in range(8)], [wt, xwt], pbt)
            for pr in (0, 64):
                self.act(qa[pr:pr + 64, m, pr // 64, :n], pb[pr:pr + 64, :n], AF.Identity, [pbt, tC], [qt],
                         bias=self.PAR[pr:pr + 64, l * NPAR + P_BIN + m:l * NPAR + P_BIN + m + 1])

    def phase_b_window(self, kind, l, L, ctx, nwin):
        X = self.X
        d = self.d
        S = self.S
        tC = self.tC
        xwa, xwt, n, xc, xt, t0, t1 = (ctx[k] for k in ("xwa", "xwt", "n", "xc", "xt", "t0", "t1"))
        qa, qt = ctx["qa"], ctx["qt"]
        el, er = self.edge_mode(L, t0, t1)
        xm = xwa[:, :, 1:n + 1]
        for m in range(0):
            wa, wt = self.load_w8("win", l, m)
            pb, pbt = self.bank()
            self.mm(pb[:, :n], [(wa[:, k, :], xm[:, k, :]) for k in range(8)], [wt, xwt], pbt)
            for pr in (0, 64):
                self.act(qa[pr:pr + 64, m, pr // 64, :n], pb[pr:pr + 64, :n], AF.Identity, [pbt, tC], [qt],
                         bias=self.PAR[pr:pr + 64, l * NPAR + P_BIN + m:l * NPAR + P_BIN + m + 1])
        yTa, yTt = self.yaT.next()
        cua, cut = self.cu.next()
        yca, yct = self.yc.next()
        xh = xwa[:, :, 0:n + 2]
        fillers = []

        def f_ugc(m):
            wa, wt = self.load_w8("win", l, 12 + m)
            pu, put = self.bank()
            self.mm(pu[:, :n + 2], [(wa[:, k, :], xh[:, k, :]) for k in range(8)], [wt, xwt], put)
            wa2, wt2 = self.load_w8("win", l, 20 + m)
            pg, pgt = self.bank()
            self.mm(pg[:, :n + 2], [(wa2[:, k, :], xh[:, k, :]) for k in range(8)], [wt2, xwt], pgt)
            ta, tt_ = self.tu.next()
            self.act(ta[:, :n + 2], pu[:, :n + 2], AF.Identity, [put, tC], [tt_], bias=self.par(l, P_BIN + 12 + m))
            self.stt("dve", cua[:, m, :n + 2], pg[:, :n + 2], self.par(l, P_BIN + 20 + m), ta[:, :n + 2],
                     ALU.add, ALU.mult, [pgt, tt_, tC], [cut])

        def f_edge():
            for mode, col in ((el, 0), (er, n + 1)):
                if mode == "zero":
                    S.add("pool", lambda e, col=col: e.memset(cua[:, :, col:col + 1], 0.0), writes=[cut])
                elif mode in ("ftop", "fbot"):
                    f = self.FLG[:, 0:1] if mode == "ftop" else self.FLG[:, 1:2]
                    for m in range(4):
                        self.ts("dve", cua[:, m, col:col + 1], cua[:, m, col:col + 1], f, None, ALU.mult, None,
                                [cut, tC], [cut])

        def f_conv(m):
            if m == 0:
                f_edge()
            va, vt = self.cv.next()
            w0 = self.par(l, P_SCW + m)
            w1 = self.par(l, P_SCW + 4 + m)
            w2 = self.par(l, P_SCW + 8 + m)
            self.act(va[:, :n], cua[:, m, 1:n + 1], AF.Identity, [cut, tC], [vt], bias=self.par(l, P_SCB + m), scale=w1)
            self.fma("dve", va[:, :n], cua[:, m, 0:n], w0, va[:, :n], [cut, tC, vt], [vt])
            self.fma("dve", va[:, :n], cua[:, m, 2:n + 2], w2, va[:, :n], [cut, tC, vt], [vt])
            wa, wt = self.load_w8("win", l, 16 + m)
            pb, pbt = self.bank()
            self.mm(pb[:, :n], [(wa[:, k, :], xm[:, k, :]) for k in range(8)], [wt, xwt], pbt)
            self.stt("dve", yca[:, m, :n], pb[:, :n], self.par(l, P_BIN + 16 + m), va[:, :n], ALU.add, ALU.mult,
                     [pbt, vt, tC], [yct])
        for m in range(4):
            fillers.append(lambda m=m: f_ugc(m))
        for m in range(4):
            fillers.append(lambda m=m: f_conv(m))
        if ctx.get("ln") is not None:
            fillers.insert(1, ctx["ln"])

        items = []
        pre_edge = None
        for t in range(t0, t1):
            dl = L["q"][t]
            esrc = [s_ for (_, s_) in dl if s_[0] == "e"]
            if esrc and pre_edge is None:
                pre_edge = t
                self.dma(self.Eedge[:, 0:len(esrc), :],
                         d["eb"][l, esrc[0][1]:esrc[0][1] + len(esrc)].rearrange("m p f -> p m f"),
                         [], [self.Eedget], self.Eedget)
            for j, (dd, src) in enumerate(dl):
                for hb in range(2):
                    items.append((t, j, dd, src, len(dl), esrc, hb))
        po = [(self.ps[5], self.pst[5]), (self.ps[6], self.pst[6])]
        self.nbank = 3
        self.pbank = 0

        def emit_scores(it):
            t, j, dd, src, nj, esrc, hb = it
            kc = (t + dd) * 128
            qc = (t - t0) * 128
            self.sbi ^= 1
            pbank, pbt = self.ps[3 + self.sbi], self.pst[3 + self.sbi]

            if src[0] == "i":
                ea, et = self.Eint[:, src[1], hb * 512:(hb + 1) * 512], self.Eintt
            else:
                ea, et = self.Eedge[:, src[1] - esrc[0][1], hb * 512:(hb + 1) * 512], self.Eedget
                if j == 0 and hb == 0 and t != pre_edge:
                    i0 = esrc[0][1]
                    ne = len(esrc)
                    self.dma(self.Eedge[:, 0:ne, :], d["eb"][l, i0:i0 + ne].rearrange("m p f -> p m f"),
                             [], [self.Eedget], self.Eedget)

            def fn(e, hb=hb, kc=kc, qc=qc, pbank=pbank, ea=ea):
                ins = e.matmul(pbank[:, :], lhsT=self.identb, rhs=ea, start=True, stop=False, skip_group_check=True)
                for hh in range(4):
                    h = hb * 4 + hh
                    ins = e.matmul(pbank[:, hh * 128:(hh + 1) * 128],
                                   lhsT=self.kT[:, h // 2, kc:kc + 128],
                                   rhs=qa[:, h // 2, h % 2, qc:qc + 128],
                                   start=False, stop=(hh == 3), skip_group_check=True)
                return ins
            S.add("pe", fn, reads=[self.kTt[t + dd], qt, et, tC], writes=[pbt])
            return pbank, pbt

        def finish_tile(t):
            qc = (t - t0) * 128
            ra, rt = self.rec.next()
            ya, yt = self.ya.next()
            for hb in range(2):
                pv = po[hb][0][:, 0:260].rearrange("p (h d) -> p h d", h=4)
                S.add("dve", lambda e, ra=ra, pv=pv, hb=hb: e.reciprocal(ra[:, hb * 4:hb * 4 + 4].unsqueeze(2), pv[:, :, 64:65]),
                      reads=[po[hb][1]], writes=[rt])
                self.tt("dve", ya[:, hb * 256:(hb + 1) * 256].rearrange("p (h d) -> p h d", h=4), pv[:, :, 0:64],
                        ra[:, hb * 4:hb * 4 + 4].unsqueeze(2).to_broadcast([128, 4, 64]), ALU.mult,
                        [po[hb][1], rt], [yt])
            pbb = self.psb[:, 0:512]

            def fn(e, pbb=pbb, ya=ya):
                ins = None
                for c in range(4):
                    ins = e.transpose(out=pbb[:, c * 128:(c + 1) * 128], in_=ya[:, c * 128:(c + 1) * 128],
                                      identity=self.identb)
                return ins
            S.add("pe", fn, reads=[yt, tC], writes=[self.psbt[0]])
            self.cp("act", yTa[:, :, qc:qc + 128], pbb.rearrange("p (c j) -> p c j", c=4), [self.psbt[0]], [yTt])

        pending = emit_scores(items[0])
        for idx, it in enumerate(items):
            t, j, dd, src, nj, esrc, hb = it
            kt = t + dd
            nxt = emit_scores(items[idx + 1]) if idx + 1 < len(items) else None
            ma, mt = self.Pm.next()
            self.act(ma, pending[0][:, :], AF.Exp, [pending[1]], [mt], scale=0.125)

            def fn(e, hb=hb, kt=kt, ma=ma, j=j, nj=nj, pbank=po[hb][0]):
                ins = None
                for hh in range(4):
                    h = hb * 4 + hh
                    ins = e.matmul(pbank[:, hh * 65:(hh + 1) * 65], lhsT=ma[:, hh * 128:(hh + 1) * 128],
                                   rhs=self.vA[:, kt, h, 0:65], start=(j == 0 and hh == 0), stop=(j == nj - 1),
                                   skip_group_check=True)
                return ins
            S.add("pe", fn, reads=[mt, self.vAt[kt]], writes=[po[hb][1]])
            if fillers and hb == 1:
                fillers.pop(0)()
            if j == nj - 1 and hb == 1:
                finish_tile(t)
            pending = nxt
        self.nbank = 5
        while fillers:
            fillers.pop(0)()
        mga, mgt = self.mrg.next()
        for mo in range(8):
            wa, wt = self.load_w4("wba", l, mo)
            pa_, pat = self.bank()
            self.mm(pa_[:, :n], [(wa[:, k, :], yTa[:, k, :n]) for k in range(4)], [wt, yTt], pat)
            wc, wct = self.load_w4("wbc", l, mo)
            pc_, pct = self.bank()
            self.mm(pc_[:, :n], [(wc[:, k, :], yca[:, k, :n]) for k in range(4)], [wct, yct], pct)
            wg, wgt = self.load_w8("win", l, 24 + mo)
            pga, pgat = self.bank()
            self.mm(pga[:, :n], [(wg[:, k, :], xm[:, k, :]) for k in range(8)], [wgt, xwt], pgat)
            wg2, wgt2 = self.load_w8("win", l, 32 + mo)
            pgc, pgct = self.bank()
            self.mm(pgc[:, :n], [(wg2[:, k, :], xm[:, k, :]) for k in range(8)], [wgt2, xwt], pgct)
            s1, s1t = self.sg.next()
            s2, s2t = self.sg.next()
            self.act(s1[:, :n], pga[:, :n], AF.Sigmoid, [pgat, tC], [s1t], bias=self.par(l, P_BIN + 24 + mo))
            self.act(s2[:, :n], pgc[:, :n], AF.Sigmoid, [pgct, tC], [s2t], bias=self.par(l, P_BIN + 32 + mo))
            u1, u1t = self.t12.next()
            u2, u2t = self.t12.next()
            self.tt("dve", u1[:, :n], s1[:, :n], pa_[:, :n], ALU.mult, [s1t, pat], [u1t])
            self.tt("dve", u2[:, :n], s2[:, :n], pc_[:, :n], ALU.mult, [s2t, pct], [u2t])
            self.tt("pool", mga[:, mo, :n], u1[:, :n], u2[:, :n], ALU.add, [u1t, u2t], [mgt])
        self.cp("pool", self.xsave, X[:, :, xc + n - 1:xc + n], xt, [self.xsavet])
        nctx = None
        if nwin is not None:
            nctx = self.b_pre(kind, l, L, nwin[0], nwin[1], False)
        for mo in range(8):
            wa, wt = self.load_w8("wo", l, mo)
            pb, pbt = self.bank()
            self.mm(pb[:, :n], [(wa[:, k, :], mga[:, k, :n]) for k in range(8)], [wt, mgt], pbt)
            za, zt = self.ztmp.next()
            self.act(za[:, :n], pb[:, :n], AF.Identity, [pbt, tC], [zt], bias=self.par(l, P_BO + mo))
            self.stt("dve", X[:, mo, xc:xc + n], X[:, mo, xc:xc + n], ALPHA, za[:, :n], ALU.mult, ALU.add,
                     xt + [zt], xt)
        ln = lambda: self.layer_norm(l, xc, n, xt, P_G1, P_B1, self.zb, self.zq, self.mean, self.rstd, self.msq)
        if nctx is not None:
            self.b_q(l, nctx)
            nctx["ln"] = ln
        else:
            ln()
        return nctx

    def c_pre(self, kind, l, L, t0, t1, first):
        tC = self.tC
        xwa, xwt = self.xwc.next()
        n, xc, xt = self.build_window(xwa, xwt, kind, t0, t1, first, L, t0 > L["B"][0])
        el, er = self.edge_mode(L, t0, t1)
        for mode, col in ((el, 0), (er, n + 1)):
            if mode in ("ftop", "fbot"):
                f = self.FLG[:, 0:1] if mode == "ftop" else self.FLG[:, 1:2]
                self.ts("dve", xwa[:, :, col:col + 1], xwa[:, :, col:col + 1], f, None, ALU.mult, None,
                        [xwt, tC], [xwt])
        ga, gt = self.gT.next()
        return dict(xwa=xwa, xwt=xwt, n=n, xc=xc, xt=xt, el=el, er=er, ga=ga, gt=gt, m=0)

    def c_iter(self, l, ctx, count):
        tC = self.tC
        DER = self.DER
        xwa, xwt, n, el, er, ga, gt = (ctx[k] for k in ("xwa", "xwt", "n", "el", "er", "ga", "gt"))
        xh = xwa[:, :, 0:n + 2]
        for m in range(ctx["m"], min(22, ctx["m"] + count)):
            wg, wgt = self.load_w8("wup", l, m)
            wv_, wvt_ = self.load_w8("wup", l, 22 + m)
            pg, pgt = self.bank()
            self.mm(pg[:, :n + 2], [(wg[:, k, :], xh[:, k, :]) for k in range(8)], [wgt, xwt], pgt)
            pv, pvt = self.bank()
            self.mm(pv[:, :n + 2], [(wv_[:, k, :], xh[:, k, :]) for k in range(8)], [wvt_, xwt], pvt)
            accs = []
            for (pp, ppt, ring, mc) in ((pg, pgt, self.accg, m), (pv, pvt, self.accv, 22 + m)):
                aa, at = ring.next()
                w0 = self.par(l, P_FCW + mc)
                w1 = self.par(l, P_FCW + 44 + mc)
                w2 = self.par(l, P_FCW + 88 + mc)
                self.act(aa[:, :n], pp[:, 1:n + 1], AF.Identity, [ppt, tC], [at], bias=DER[:, l, 0, mc:mc + 1], scale=w1)
                self.stt("dve", aa[:, :n], pp[:, 0:n], w0, aa[:, :n], ALU.mult, ALU.add, [ppt, at, tC], [at])
                self.stt("dve", aa[:, :n], pp[:, 2:n + 2], w2, aa[:, :n], ALU.mult, ALU.add, [ppt, at, tC], [at])
                if el is not None:
                    j = 1 if el == "zero" else 3
                    self.tt("pool", aa[:, 0:1], aa[:, 0:1], DER[:, l, j, mc:mc + 1], ALU.subtract, [at, tC], [at])
                if er is not None:
                    j = 2 if er == "zero" else 4
                    self.tt("pool", aa[:, n - 1:n], aa[:, n - 1:n], DER[:, l, j, mc:mc + 1], ALU.subtract, [at, tC], [at])
                accs.append((aa, at))
            la, lt = self.gl.next()
            self.act(la[:, :n], accs[0][0][:, :n], AF.Gelu_apprx_tanh, [accs[0][1]], [lt])
            self.tt("pool", ga[:, m, :n], la[:, :n], accs[1][0][:, :n], ALU.mult, [lt, accs[1][1]], [gt])
        ctx["m"] = min(22, ctx["m"] + count)

    def phase_c_window(self, kind, l, L, ctx, nxt):
        X = self.X
        tC = self.tC
        n, xc, xt, ga, gt = (ctx[k] for k in ("n", "xc", "xt", "ga", "gt"))
        self.c_iter(l, ctx, 22)
        self.cp("pool", self.xsave, X[:, :, xc + n - 1:xc + n], xt, [self.xsavet])
        nctx = None
        if nxt is not None:
            nctx = self.c_pre(kind, l, L, nxt[0], nxt[1], False)
        for mo in range(8):
            wa, wt = self.wdn.next()
            self.dma(wa.rearrange("p k j -> p (k j)"), self.d["wdnb"][l, mo], [], [wt], wt)
            pb, pbt = self.bank()
            self.mm(pb[:, :n], [(wa[:, k, :], ga[:, k, :n]) for k in range(22)], [wt, gt], pbt)
            za, zt = self.ztmp2.next()
            self.act(za[:, :n], pb[:, :n], AF.Identity, [pbt, tC], [zt], bias=self.par(l, P_BDN + mo))
            self.stt("dve", X[:, mo, xc:xc + n], X[:, mo, xc:xc + n], ALPHA, za[:, :n], ALU.mult, ALU.add,
                     xt + [zt], xt)
        if nctx is not None:
            self.c_iter(l, nctx, 6)
        self.layer_norm(l, xc, n, xt, P_G2, P_B2, self.zb2, self.zq2, self.mean2, self.rstd2, self.msq2)
        return nctx

    def build(self):
        nc = self.nc
        self.d["identin"] = nc.dram_tensor("identin", [128, 128], F32, kind="ExternalInput").ap()
        self.setup()
        self.prepass()
        self.barrier(self.c_tiles, self.ab_tiles)
        sk = sample_kind()
        for i in range(self.NS if self.stop != 'P' else 0):
            self.chunk(sk, self.d["xs"][i], self.d["ys"][i])
        for j in range(self.NP):
            self.chunk(prompt_kind(j), self.d["xp"][j], self.d["yp"][j])
        self.S.add("sp", lambda e: e.nop(nofuse=True), extra=self.out_events)
        with ExitStack() as stack:
            self.S.emit(nc, stack)
        return nc


class Ring:
    def __init__(self, items):
        self.items = items
        self.i = 0

    def next(self):
        it = self.items[self.i]
        self.i = (self.i + 1) % len(self.items)
        return it


def wlayout(w):
    K, M = w.shape
    return np.ascontiguousarray(w.reshape(K // 128, 128, M // 128, 128).transpose(2, 1, 0, 3)).reshape(
        M // 128, 128, K)


def col(v):
    return np.ascontiguousarray(v.reshape(-1, 128).T)


def bm_tile(rpb_l, i0, r0, R):
    out = np.full((128, 8, 128), NEG, np.float32)
    kc = np.arange(64)[:, None]
    qc = np.arange(64)[None, :]
    js = np.clip(qc - 8, 0, 48)
    valid = (kc >= js) & (kc < js + 16)
    dc = np.clip(kc - qc + 15, 0, 30)
    for kr in range(2):
        for qr in range(2):
            r = r0 + kr
            i = i0 + qr
            if i < 0 or i >= R:
                continue
            rs = min(max(i - 4, 0), R - 8)
            if not (rs <= r < rs + 8):
                continue
            g = rpb_l[:, r - i + 7][:, dc]
            blk = np.where(valid[None], g, np.float32(NEG))
            out[kr * 64:(kr + 1) * 64, :, qr * 64:(qr + 1) * 64] = blk.transpose(1, 0, 2)
    return out.reshape(128, 1024)


def bm_set(rpb_l, half):
    tiles = []
    big = 1000
    def interior(d):
        return bm_tile(rpb_l, 500, 500 + 2 * d, big)
    for d in range(-2, 3):
        tiles.append(interior(d))
    for t, ds in ((0, range(0, 4)), (1, range(-1, 3)), (14, range(-2, 2)), (15, range(-3, 1))):
        for d in ds:
            tiles.append(bm_tile(rpb_l, 2 * t, 2 * (t + d), 32))
    for t, ds in ((5, range(-2, 4)), (6, range(-2, 3))):
        for d in ds:
            if half == 0:
                tiles.append(bm_tile(rpb_l, 2 * (t - 5), 2 * (t - 5 + d), 128))
            else:
                tiles.append(interior(d))
    for t, ds in ((11, range(-2, 3)), (12, range(-3, 3))):
        for d in ds:
            if half == 1:
                tiles.append(bm_tile(rpb_l, 112 + 2 * (t - 5), 112 + 2 * (t - 5 + d), 128))
            else:
                tiles.append(interior(d))
    assert len(tiles) == NBM
    return np.stack(tiles)


def host_params(inp):
    pars = []
    for l in range(2):
        cols = [col(inp["b_in"][l]),
                np.ascontiguousarray(inp["sc_conv_w"][l].reshape(3, 4, 128).transpose(2, 0, 1)).reshape(128, 12),
                col(inp["sc_conv_b"][l]), col(inp["b_o"][l]), col(inp["ln1_g"][l]), col(inp["ln1_b"][l]),
                col(inp["ffn_b_up"][l]),
                np.ascontiguousarray(inp["ffn_conv_w"][l].reshape(3, 44, 128).transpose(2, 0, 1)).reshape(128, 132),
                col(inp["ffn_conv_b"][l]), col(inp["ffn_b_down"][l]), col(inp["ln2_g"][l]), col(inp["ln2_b"][l])]
        p = np.concatenate(cols, axis=1)
        assert p.shape == (128, NPAR)
        pars.append(p)
    par = np.ascontiguousarray(np.concatenate(pars, axis=1), dtype=np.float32)
    bv = np.ascontiguousarray(np.broadcast_to(
        np.stack([inp["b_in"][l][1024:1536] for l in range(2)]).reshape(1, 1024), (128, 1024)), dtype=np.float32)
    return par, bv


def host_weights(inp):
    out = {}
    for key, name in (("win", "w_in"), ("wba", "w_br_attn"), ("wbc", "w_br_conv"), ("wo", "w_o"),
                      ("wup", "ffn_w_up"), ("wdn", "ffn_w_down")):
        out[key] = np.stack([wlayout(np.asarray(inp[name][l], np.float32)) for l in range(2)])
    return out


_CACHE = {}


def get_nc(NS, NP):
    k = (NS, NP)
    if k not in _CACHE:
        _CACHE[k] = Builder(NS, NP).build()
    return _CACHE[k]


def kernel(**inputs):
    inp = {k: np.asarray(v) for k, v in inputs.items()}
    xp_full = inp["x_prompt"]
    xs_full = inp["x_sample"]
    n = 8
    nc = get_nc(4, 4)
    par, bv = host_params(inp)
    W = host_weights(inp)
    ident = np.eye(128, dtype=np.float32)
    bms = [np.stack([bm_set(inp["attn_rpb"][l], h) for l in range(2)]) for h in range(2)]
    in_maps = []
    for c in range(n):
        p, half = c // 2, c % 2
        xpc = np.zeros((4, 17 * 128, D), np.float32)
        for j in range(4):
            r0 = 64 * half + 16 * j
            lo, hi = r0 - 10, r0 + 24
            slo, shi = max(lo, 0), min(hi, 128)
            xpc[j, (slo - lo) * 64:(shi - lo) * 64] = xp_full[p, slo * 64:shi * 64]
        ftop = 0.0 if half == 0 else 1.0
        fbot = 0.0 if half == 1 else 1.0
        flg = np.ascontiguousarray(np.broadcast_to(np.array([ftop, fbot, 1 - ftop, 1 - fbot], np.float32), (128, 4)))
        m = dict(xs=np.ascontiguousarray(xs_full[4 * c:4 * c + 4]), xp=xpc, bm=bms[half], par=par, bv=bv,
                 flg=flg, identin=ident)
        m.update(W)
        in_maps.append(m)
    res = run_bass_kernel_spmd(nc, in_maps, core_ids=list(range(n)))
    y_prompt = np.empty_like(xp_full)
    y_sample = np.empty_like(xs_full)
    for c in range(n):
        r = res.results[c]
        p, half = c // 2, c % 2
        y_sample[4 * c:4 * c + 4] = r["ys"]
        for j in range(4):
            r0 = 64 * half + 16 * j
            y_prompt[p, r0 * 64:(r0 + 16) * 64] = r["yp"][j]
    return (y_prompt, y_sample)
```

```python
import numpy as np
import ml_dtypes
from contextlib import ExitStack
import concourse.bass as bass
import concourse.mybir as mybir
from concourse.bass_utils import run_bass_kernel_spmd

F32 = mybir.dt.float32
BF16 = mybir.dt.bfloat16
AF = mybir.ActivationFunctionType
ALU = mybir.AluOpType

D = 1024
NH = 8
DH = 64
DFF = 2816
ALPHA = float(4 ** 0.25)
EPS = 1e-5
NEG = -30000.0
NPAR = 324
P_BIN, P_SCW, P_SCB, P_BO, P_G1, P_B1, P_BUP, P_FCW, P_FCB, P_BDN, P_G2, P_B2 = (
    0, 40, 52, 56, 64, 72, 80, 124, 256, 300, 308, 316)
NBM = 43

ENGS = ("pe", "act", "dve", "pool", "sp")


class Tl:
    __slots__ = ("name", "w", "rs", "rd", "sem", "cnt")

    def __init__(self, name):
        self.name = name
        self.w = None
        self.rs = {}
        self.rd = []
        self.sem = None
        self.cnt = 0


class Op:
    __slots__ = ("eng", "fn", "waits", "sig", "idx", "sigval", "dsem", "line")


class Sched:
    def __init__(self):
        self.q = {e: [] for e in ENGS}
        self.dma_tiles = []

    def add(self, eng, fn, reads=(), writes=(), dma=None, extra=()):
        op = Op()
        op.eng = eng
        op.fn = fn
        op.sig = False
        op.idx = len(self.q[eng])
        op.dsem = dma
        op.sigval = 0
        import sys as _s
        f = _s._getframe(1)
        ls = []
        while f is not None and len(ls) < 4:
            ls.append(f.f_lineno)
            f = f.f_back
        op.line = ls
        deps = list(extra)
        for t in reads:
            if t.w is not None:
                deps.append(t.w)
        for t in writes:
            if t.w is not None:
                deps.append(t.w)
            deps.extend(t.rs.values())
            deps.extend(t.rd)
        waits = []
        seen = set()
        for d in deps:
            if id(d) in seen:
                continue
            seen.add(id(d))
            if isinstance(d, Op):
                if d.eng == eng:
                    if eng in ("pe", "sp"):
                        continue
                    if op.idx - d.idx > 3:
                        continue
                d.sig = True
            waits.append(d)
        op.waits = waits
        if dma is not None:
            if dma.sem is None:
                dma.sem = True
                self.dma_tiles.append(dma)
            dma.cnt += 16
            ev = (dma, dma.cnt)
        else:
            ev = op
        for t in reads:
            if isinstance(ev, Op):
                t.rs[eng] = ev
            else:
                t.rd.append(ev)
        for t in writes:
            t.w = ev
            t.rs = {}
            t.rd = []
        self.q[eng].append(op)
        return ev

    def emit(self, nc, stack):
        esem = {}
        for e in ("pe", "act", "dve", "pool"):
            esem[e] = stack.enter_context(nc.semaphore("s_" + e))
        for t in self.dma_tiles:
            t.sem = stack.enter_context(nc.semaphore("d_" + t.name))
        for e in ENGS:
            c = 0
            for op in self.q[e]:
                if op.sig:
                    c += 1
                op.sigval = c
        q = self.q

        def run(name, eng):
            waited = {}
            for op in q[name]:
                for d in op.waits:
                    if isinstance(d, Op):
                        sem, val = esem[d.eng], d.sigval
                    else:
                        sem, val = d[0].sem, d[1]
                    k = id(sem)
                    if waited.get(k, 0) < val:
                        eng.wait_ge(sem, val)
                        waited[k] = val
                ins = op.fn(eng)
                if op.dsem is not None:
                    ins.then_inc(op.dsem.sem, 16)
                elif op.sig:
                    ins.then_inc(esem[name], 1)

        with nc.Block() as block:
            @block.tensor
            def _(e):
                run("pe", e)

            @block.scalar
            def _(e):
                run("act", e)

            @block.vector
            def _(e):
                run("dve", e)

            @block.gpsimd
            def _(e):
                run("pool", e)

            @block.sync
            def _(e):
                run("sp", e)


def split_windows(t0, t1, forced=()):
    cuts = sorted(set([t0, t1] + [f for f in forced if t0 < f < t1]))
    out = []
    for a, b in zip(cuts[:-1], cuts[1:]):
        n = b - a
        k = (n + 2) // 3
        base, rem = divmod(n, k)
        s = a
        for i in range(k):
            ln = base + (1 if i < rem else 0)
            out.append((s, s + ln))
            s += ln
    return out


def sample_kind():
    q = {}
    for t in range(16):
        if t == 0:
            q[t] = [(d, ("e", 5 + i)) for i, d in enumerate(range(0, 4))]
        elif t == 1:
            q[t] = [(d, ("e", 9 + i)) for i, d in enumerate(range(-1, 3))]
        elif t == 14:
            q[t] = [(d, ("e", 13 + i)) for i, d in enumerate(range(-2, 2))]
        elif t == 15:
            q[t] = [(d, ("e", 17 + i)) for i, d in enumerate(range(-3, 1))]
        else:
            q[t] = [(d, ("i", d + 2)) for d in range(-2, 3)]
    lay = dict(A=(0, 16), B=(0, 16), C=(0, 16), q=q, forced=(),
               bl={0: "zero"}, br={16: "zero"})
    return dict(NT=16, xoff=0, L=[lay, lay], out=(0, 16))


def prompt_kind(j):
    def qmap(b0, b1, a0, a1):
        q = {}
        for t in range(b0, b1):
            if j == 0 and t == 5:
                lst = [(d, ("e", 21 + i)) for i, d in enumerate(range(-2, 4))]
            elif j == 0 and t == 6:
                lst = [(d, ("e", 27 + i)) for i, d in enumerate(range(-2, 3))]
            elif j == 3 and t == 11:
                lst = [(d, ("e", 32 + i)) for i, d in enumerate(range(-2, 3))]
            elif j == 3 and t == 12:
                lst = [(d, ("e", 37 + i)) for i, d in enumerate(range(-3, 3))]
            else:
                lst = [(d, ("i", d + 2)) for d in range(-2, 3)]
            q[t] = [(d, s) for (d, s) in lst if a0 <= t + d < a1]
        return q
    bl = {5: "ftop"} if j == 0 else {}
    br = {13: "fbot"} if j == 3 else {}
    l0 = dict(A=(0, 17), B=(2, 16), C=(2, 15), q=qmap(2, 16, 0, 17), forced=(5, 13), bl=bl, br=br)
    l1 = dict(A=(2, 15), B=(4, 14), C=(5, 13), q=qmap(4, 14, 2, 15), forced=(5, 13), bl=bl, br=br)
    return dict(NT=17, xoff=2, L=[l0, l1], out=(5, 13))


class Builder:
    def __init__(self, NS, NP, dbg=False, stop=None):
        self.NS, self.NP = NS, NP
        self.stop = stop
        self.bstop = None
        if stop is not None and len(stop) == 3:
            self.bstop = int(stop[2])
            self.stop = stop[:2]
        self.nc = nc = bass.Bass("TRN2", target_bir_lowering=False)
        self.S = Sched()
        dt = nc.dram_tensor
        self.d = d = {}
        if NS:
            d["xs"] = dt("xs", [NS, 2048, D], F32, kind="ExternalInput").ap()
            d["ys"] = dt("ys", [NS, 2048, D], F32, kind="ExternalOutput").ap()
        if NP:
            d["xp"] = dt("xp", [NP, 17 * 128, D], F32, kind="ExternalInput").ap()
            d["yp"] = dt("yp", [NP, 1024, D], F32, kind="ExternalOutput").ap()
        self.wshapes = dict(win=(40, 1024), wba=(8, 512), wbc=(8, 512), wo=(8, 1024),
                            wup=(44, 1024), wdn=(8, 2816))
        for k, (n, f) in self.wshapes.items():
            d[k] = dt(k, [2, n, 128, f], F32, kind="ExternalInput").ap()
            d[k + "b"] = dt(k + "b", [2, n, 128, f], BF16, kind="Internal").ap()
        d["bm"] = dt("bm", [2, NBM, 128, 1024], F32, kind="ExternalInput").ap()
        d["eb"] = dt("eb", [2, NBM, 128, 1024], BF16, kind="Internal").ap()
        d["par"] = dt("par", [128, 2 * NPAR], F32, kind="ExternalInput").ap()
        d["bv"] = dt("bv", [128, 1024], F32, kind="ExternalInput").ap()
        d["flg"] = dt("flg", [128, 4], F32, kind="ExternalInput").ap()
        self.off = 0
        self.arena = nc.alloc_sbuf_tensor("arena", [128, 53100], F32)
        self.cap = 53100 * 4
        self.pbank = 0
        self.nbank = 5
        self.sbi = 0
        self.out_events = []

    def sb(self, shape, dtype, name=None):
        n = 1
        for s in shape:
            n *= s
        nbytes = n * (4 if dtype == F32 else 2)
        nbytes = (nbytes + 31) // 32 * 32
        assert self.off + nbytes <= self.cap, ("SBUF overflow", name, self.off + nbytes)
        w0 = self.off // 4
        ap = self.arena[:, w0:w0 + nbytes // 4]
        if dtype != F32:
            ap = ap.bitcast(BF16)
            ap = ap[:, 0:n]
        else:
            ap = ap[:, 0:n]
        self.off += nbytes
        if len(shape) == 2:
            ap = ap.rearrange("p (a b) -> p a b", a=shape[0])
        elif len(shape) == 3:
            ap = ap.rearrange("p (a b c) -> p a b c", a=shape[0], b=shape[1])
        return ap

    def ring(self, n, shape, dtype, name):
        return Ring([(self.sb(shape, dtype, name), Tl(f"{name}{i}")) for i in range(n)])

    def bank(self):
        self.pbank = (self.pbank + 1) % self.nbank
        b = self.pbank
        return self.ps[b], self.pst[b]

    def mm(self, out, pairs, reads, wt, flags=None):
        pairs = list(pairs)

        def fn(e):
            n = len(pairs)
            ins = None
            for i, (l, r) in enumerate(pairs):
                st, sp = (i == 0, i == n - 1) if flags is None else flags
                ins = e.matmul(out, lhsT=l, rhs=r, start=st, stop=sp, skip_group_check=True)
            return ins
        self.S.add("pe", fn, reads=reads, writes=[wt])

    def dma(self, out, in_, reads, writes, sem):
        return self.S.add("sp", lambda e: e.dma_start(out=out, in_=in_), reads=reads, writes=writes, dma=sem)

    def act(self, out, in_, func, reads, writes, bias=0.0, scale=1.0):
        self.S.add("act", lambda e: e.activation(out=out, in_=in_, func=func, bias=bias, scale=scale),
                   reads=reads, writes=writes)

    def ts(self, eng, out, in0, s1, s2, op0, op1, reads, writes):
        if s2 is None:
            self.S.add(eng, lambda e: e.tensor_scalar(out, in0, s1, None, op0), reads=reads, writes=writes)
        else:
            self.S.add(eng, lambda e: e.tensor_scalar(out, in0, s1, s2, op0, op1), reads=reads, writes=writes)

    def stt(self, eng, out, in0, sc, in1, op0, op1, reads, writes):
        self.S.add(eng, lambda e: e.scalar_tensor_tensor(out, in0, sc, in1, op0, op1), reads=reads, writes=writes)

    def fma(self, eng, out, in0, sc, in1, reads, writes):
        if eng == "dve":
            self.stt("dve", out, in0, sc, in1, ALU.mult, ALU.add, reads, writes)
        else:
            pa, pt = self.ptmp.next()
            shp = list(out.shape)
            pv = pa[:, :shp[-1]]
            self.ts(eng, pv, in0, sc, None, ALU.mult, None, reads, [pt])
            self.tt(eng, out, pv, in1, ALU.add, list(reads) + [pt], writes)

    def tt(self, eng, out, in0, in1, op, reads, writes):
        self.S.add(eng, lambda e: e.tensor_tensor(out, in0, in1, op), reads=reads, writes=writes)

    def cp(self, eng, out, in_, reads, writes):
        if eng == "act":
            self.S.add(eng, lambda e: e.copy(out, in_), reads=reads, writes=writes)
        else:
            self.S.add(eng, lambda e: e.tensor_copy(out, in_), reads=reads, writes=writes)

    def par(self, l, off, n=1):
        return self.PAR[:, l * NPAR + off: l * NPAR + off + n]

    def setup(self):
        nc = self.nc
        d = self.d
        self.ps = [nc.alloc_psum_tensor(f"ps{i}", [128, 512], F32) for i in range(7)]
        self.pst = [Tl(f"ps{i}") for i in range(7)]
        self.psb = nc.alloc_psum_tensor("psb", [128, 1024], BF16)
        self.psbt = [Tl("psb0"), Tl("psb1")]
        self.psbi = 0
        self.X = self.sb([8, 16 * 128], F32, "X")
        self.Xt = [Tl(f"X{t}") for t in range(16)]
        self.PAR = self.sb([2 * NPAR], F32, "PAR")
        self.BV = self.sb([2, 512], F32, "BV")
        self.FLG = self.sb([4], F32, "FLG")
        self.DER = self.sb([2, 5, 44], F32, "DER")
        self.ident = self.sb([128], F32, "ident")
        self.identb = self.sb([128], BF16, "identb")
        self.ones = self.sb([128], BF16, "ones")
        self.epsc = self.sb([1], F32, "epsc")
        self.barbuf = self.sb([1], F32, "barbuf")
        self.tC = Tl("const")
        self.xsave = self.sb([8, 1], F32, "xsave")
        self.xsavet = Tl("xsave")
        self.wring = self.ring(6, [8, 128], BF16, "w8")
        self.xw = self.ring(2, [8, 386], BF16, "xw")
        self.ptmp = self.ring(1, [386], F32, "ptmp")
        self.mark = self.off
        self.kT = self.sb([4, 17 * 128], BF16, "kT")
        self.kTt = [Tl(f"kT{t}") for t in range(17)]
        self.vA = self.sb([17, 8, 68], BF16, "vA")
        self.vAt = [Tl(f"vA{t}") for t in range(17)]
        self.Eint = self.sb([5, 1024], BF16, "Eint")
        self.Eintt = Tl("Eint")
        self.Eedge = self.sb([6, 1024], BF16, "Eedge")
        self.Eedget = Tl("Eedge")
        self.wv = Ring([(self.Eedge.rearrange("p a b -> p (a b)")[:, 0:4096].rearrange("p (k j) -> p k j", k=8), self.Eedget)])
        self.xstage = self.ring(1, [1024], F32, "xstage")
        self.qT = self.ring(1, [4, 2, 384], BF16, "qT")
        self.Pm = self.ring(4, [512], BF16, "Pm")
        self.ya = self.ring(1, [512], BF16, "ya")
        self.rec = self.ring(2, [8], F32, "rec")
        self.yaT = self.ring(1, [4, 384], BF16, "yaT")
        self.cu = self.ring(1, [4, 386], F32, "cu")
        self.tu = self.ring(1, [386], F32, "tu")
        self.cv = self.ring(1, [384], F32, "cv")
        self.yc = self.ring(1, [4, 384], BF16, "yc")
        self.w4ring = self.ring(3, [4, 128], BF16, "w4")
        self.sg = self.ring(2, [384], F32, "sg")
        self.t12 = self.ring(2, [384], F32, "t12")
        self.mrg = self.ring(1, [8, 384], BF16, "mrg")
        self.ztmp = self.tu
        self.zb = self.ring(2, [384], BF16, "zb")
        self.zq = self.ring(2, [384], BF16, "zq")
        self.mean = Ring([self.t12.items[0]])
        self.rstd = Ring([self.t12.items[1]])
        self.msq = self.ring(1, [384], F32, "msq")
        endAB = self.off
        self.off = self.mark
        self.xwc = self.ring(2, [8, 386], BF16, "xwc")
        self.gT = self.ring(2, [22, 384], BF16, "gT")
        self.wdn = self.ring(2, [22, 128], BF16, "wdn")
        self.accg = self.ring(2, [384], F32, "accg")
        self.accv = self.ring(2, [384], F32, "accv")
        self.gl = self.ring(2, [384], F32, "gl")
        self.ostage = self.ring(2, [1024], F32, "ostage")
        self.ztmp2 = self.ring(1, [386], F32, "ztmp2")
        self.zb2 = self.ring(2, [384], BF16, "zb2")
        self.zq2 = self.ring(2, [384], BF16, "zq2")
        self.mean2 = self.ring(1, [384], F32, "mean2")
        self.rstd2 = self.ring(1, [384], F32, "rstd2")
        self.msq2 = self.ring(1, [384], F32, "msq2")
        self.stF = self.ring(4, [1024], F32, "stF")
        self.stB = self.ring(4, [1024], BF16, "stB")
        endC = self.off
        self.off = max(endAB, endC)
        print("SBUF bytes/partition: persistent", self.mark, "B", endAB - self.mark, "C", endC - self.mark, "cap", self.cap)
        self.ab_tiles = self.kTt + self.vAt + [self.Eintt, self.Eedget]
        self.c_tiles = []
        for r in (self.xstage, self.qT, self.Pm, self.ya, self.rec, self.yaT,
                  self.cu, self.tu, self.cv, self.yc, self.w4ring, self.sg, self.t12, self.mrg,
                  self.zb, self.zq, self.msq):
            self.ab_tiles += [t for _, t in r.items]
        for r in (self.xwc, self.gT, self.wdn, self.accg, self.accv, self.gl, self.ostage, self.ztmp2,
                  self.zb2, self.zq2, self.mean2, self.rstd2, self.msq2, self.stF, self.stB):
            self.c_tiles += [t for _, t in r.items]

        S = self.S
        tC = self.tC
        S.add("pool", lambda e: e.memset(self.ones, 1.0 / 1024.0), writes=[tC])
        S.add("pool", lambda e: e.memset(self.epsc, EPS), writes=[tC])
        self.dma(self.PAR, d["par"], [], [tC], tC)
        self.dma(self.BV.rearrange("p a b -> p (a b)"), d["bv"], [], [tC], tC)
        self.dma(self.FLG, d["flg"], [], [tC], tC)
        self.dma(self.ident, d["identin"], [], [tC], tC)
        self.cp("dve", self.identb, self.ident, [tC], [tC])
        for l in range(2):
            w0 = self.par(l, P_FCW, 44)
            w1 = self.par(l, P_FCW + 44, 44)
            w2 = self.par(l, P_FCW + 88, 44)
            bup = self.par(l, P_BUP, 44)
            fcb = self.par(l, P_FCB, 44)
            D_ = self.DER
            self.tt("dve", D_[:, l, 0, :], w0, w1, ALU.add, [tC], [tC])
            self.tt("dve", D_[:, l, 0, :], D_[:, l, 0, :], w2, ALU.add, [tC], [tC])
            self.tt("dve", D_[:, l, 0, :], D_[:, l, 0, :], bup, ALU.mult, [tC], [tC])
            self.tt("dve", D_[:, l, 0, :], D_[:, l, 0, :], fcb, ALU.add, [tC], [tC])
            self.tt("dve", D_[:, l, 1, :], bup, w0, ALU.mult, [tC], [tC])
            self.tt("dve", D_[:, l, 2, :], bup, w2, ALU.mult, [tC], [tC])
            self.ts("dve", D_[:, l, 3, :], D_[:, l, 1, :], self.FLG[:, 2:3], None, ALU.mult, None, [tC], [tC])
            self.ts("dve", D_[:, l, 4, :], D_[:, l, 2, :], self.FLG[:, 3:4], None, ALU.mult, None, [tC], [tC])

    def barrier(self, old_tiles, new_tiles):
        S = self.S
        S.add("pool", lambda e: e.memset(self.barbuf, 0.0), writes=list(old_tiles))
        last = old_tiles[0].w
        for t in new_tiles:
            t.w = last
            t.rs = {}
            t.rd = []

    def prepass(self):
        d = self.d
        jobs = []
        for k, (n, f) in self.wshapes.items():
            for l in range(2):
                for m in range(n):
                    for c0 in range(0, f, 1024):
                        c1 = min(c0 + 1024, f)
                        jobs.append((d[k][l, m][:, c0:c1], d[k + "b"][l, m][:, c0:c1], c1 - c0, "cast"))
        for l in range(2):
            for m in range(NBM):
                jobs.append((d["bm"][l, m], d["eb"][l, m], 1024, "exp"))
        evs = []
        ci = 0
        LOOK = 3
        staged = {}

        def issue_in(i):
            src, dst, f, kind = jobs[i]
            fa, ft = self.stF.next()
            self.dma(fa[:, :f], src, [], [ft], ft)
            staged[i] = (fa, ft)
        for i in range(min(LOOK, len(jobs))):
            issue_in(i)
        for i, (src, dst, f, kind) in enumerate(jobs):
            if i + LOOK < len(jobs):
                issue_in(i + LOOK)
            fa, ft = staged.pop(i)
            ba, bt = self.stB.next()
            if kind == "exp":
                self.act(ba[:, :f], fa[:, :f], AF.Identity, [ft], [bt], scale=8.0)
            else:
                eng = ("dve", "act")[ci % 2]
                ci += 1
                self.cp(eng, ba[:, :f], fa[:, :f], [ft], [bt])
            evs.append(self.dma(dst, ba[:, :f], [bt], [], bt))
        self.S.add("sp", lambda e: e.nop(nofuse=True), extra=evs)

    def load_w8(self, key, l, mc):
        ap, t = self.wring.next()
        self.dma(ap.rearrange("p k j -> p (k j)"), self.d[key + "b"][l, mc], [], [t], t)
        return ap, t

    def load_w4(self, key, l, mc):
        ap, t = self.w4ring.next()
        self.dma(ap.rearrange("p k j -> p (k j)"), self.d[key + "b"][l, mc], [], [t], t)
        return ap, t

    def layer_norm(self, l, c0, n, xt, goff, boff, zb, zq, mean, rstd, msq, split=False):
        X = self.X
        pm, pmt = self.bank()
        pq, pqt = self.bank()
        for c in range(8):
            za, zt = zb.next()
            qa, qt = zq.next()
            self.cp("act", za[:, :n], X[:, c, c0:c0 + n], xt, [zt])
            self.act(qa[:, :n], X[:, c, c0:c0 + n], AF.Square, xt, [qt])
            self.mm(pm[:, :n], [(self.ones, za[:, :n])], [zt, self.tC], pmt, flags=(c == 0, c == 7))
            self.mm(pq[:, :n], [(self.ones, qa[:, :n])], [qt, self.tC], pqt, flags=(c == 0, c == 7))
        ma, mt = mean.next()
        ra, rt = rstd.next()
        sa, st = msq.next()
        self.cp("act", ma[:, :n], pm[:, :n], [pmt], [mt])
        self.tt("dve", sa[:, :n], ma[:, :n], ma[:, :n], ALU.mult, [mt], [st])
        self.tt("dve", sa[:, :n], pq[:, :n], sa[:, :n], ALU.subtract, [pqt, st], [st])
        self.ts("dve", sa[:, :n], sa[:, :n], EPS, None, ALU.add, None, [st], [st])
        self.act(ra[:, :n], sa[:, :n], AF.Sqrt, [st], [rt])
        ma2, m2t = self.ptmp.next()
        self.S.add("dve", lambda e: e.reciprocal(ra[:, :n], ra[:, :n]), reads=[rt], writes=[rt])
        self.tt("dve", ma2[:, :n], ra[:, :n], ra[:, :n], ALU.mult, [rt], [m2t])
        self.tt("dve", ma2[:, :n], ma2[:, :n], sa[:, :n], ALU.mult, [m2t, st], [m2t])
        self.ts("dve", ma2[:, :n], ma2[:, :n], -0.5, 1.5, ALU.mult, ALU.add, [m2t], [m2t])
        self.tt("dve", ra[:, :n], ra[:, :n], ma2[:, :n], ALU.mult, [rt, m2t], [rt])
        if split:
            pieces = []
            for c in range(8):
                def piece(c=c):
                    xc_ = X[:, c, c0:c0 + n]
                    self.tt("dve", xc_, xc_, ma[:, :n], ALU.subtract, xt + [mt], xt)
                    self.tt("dve", xc_, xc_, ra[:, :n], ALU.mult, xt + [rt], xt)
                    self.ts("dve", xc_, xc_, self.par(l, goff + c), self.par(l, boff + c), ALU.mult, ALU.add,
                            xt + [self.tC], xt)
                pieces.append(piece)
            return pieces
        xv = X[:, :, c0:c0 + n]
        self.tt("dve", xv, xv, ma[:, :n].unsqueeze(1).to_broadcast([128, 8, n]), ALU.subtract, xt + [mt], xt)
        self.tt("dve", xv, xv, ra[:, :n].unsqueeze(1).to_broadcast([128, 8, n]), ALU.mult, xt + [rt], xt)
        for c in range(8):
            eng = "dve"
            if eng == "act":
                self.act(X[:, c, c0:c0 + n], X[:, c, c0:c0 + n], AF.Identity, xt + [self.tC], xt,
                         bias=self.par(l, boff + c), scale=self.par(l, goff + c))
            else:
                self.ts("dve", X[:, c, c0:c0 + n], X[:, c, c0:c0 + n], self.par(l, goff + c),
                        self.par(l, boff + c), ALU.mult, ALU.add, xt + [self.tC], xt)

    def chunk(self, kind, xin, yout):
        for l in range(2):
            self.chunk_layer(kind, l, xin, yout)
            if self.stop is not None and self.stop[0] == str(l):
                break

    def xtiles(self, kind, t0, t1):
        xo = kind["xoff"]
        return [self.Xt[t - xo] for t in range(t0, t1) if 0 <= t - xo < 16]

    def kv_from_xb(self, l, xb, xbt, tiles, wvap, wvt):
        n = len(tiles) * 128
        t0 = tiles[0]
        for m in range(4):
            wa, wt = self.load_w8("win", l, 4 + m)
            pb, pbt = self.bank()
            self.mm(pb[:, :n], [(wa[:, k, :], xb[:, k, 0:n]) for k in range(8)], [wt, xbt], pbt)
            self.act(self.kT[:, m, t0 * 128:t0 * 128 + n], pb[:, :n], AF.Identity, [pbt, self.tC],
                     [self.kTt[t] for t in tiles], bias=self.par(l, P_BIN + 4 + m))
        for i, t in enumerate(tiles):
            pb, pbt = self.bank()
            self.mm(pb[:, :], [(xb[:, k, i * 128:(i + 1) * 128], wvap[:, k, :]) for k in range(8)],
                    [wvt, xbt], pbt)
            self.tt("dve", self.vA[:, t, :, 0:64], pb[:, :].rearrange("p (h d) -> p h d", h=8),
                    self.BV[:, l, :].rearrange("p (h d) -> p h d", h=8), ALU.add,
                    [pbt, self.tC], [self.vAt[t]])

    def chunk_layer(self, kind, l, xin, yout):
        L = kind["L"][l]
        xo = kind["xoff"]
        X = self.X
        d = self.d
        a0, a1 = L["A"]
        b0, b1 = L["B"]
        c0_, c1_ = L["C"]
        wvap, wvt = self.wv.next()
        for m_ in range(4):
            self.dma(wvap[:, :, m_ * 128:(m_ + 1) * 128],
                     d["winb"][l, 8 + m_].rearrange("p (k j) -> p k j", k=8), [], [wvt], wvt)
        self.S.add("pool", lambda e: e.memset(self.vA[:, :, :, 64:65], 1.0), writes=self.vAt)
        qa0, qt0 = self.qT.items[0]
        self.S.add("pool", lambda e: e.memset(qa0[0:64, :, 1, :], 0.0), writes=[qt0])
        self.S.add("pool", lambda e: e.memset(qa0[64:128, :, 0, :], 0.0), writes=[qt0])
        self.dma(self.Eint, d["eb"][l, 0:5].rearrange("m p f -> p m f"), [], [self.Eintt], self.Eintt)
        t = a0
        while t < a1:
            tiles = list(range(t, min(t + 3, a1)))
            t += 3
            n = len(tiles) * 128
            xb, xbt = self.xw.next()
            if l == 0:
                for i, tt_ in enumerate(tiles):
                    sa, st = self.xstage.next()
                    self.dma(sa, xin[tt_ * 128:(tt_ + 1) * 128, :], [], [st], st)
                    for hb in range(2):
                        pb, pbt = self.bank()

                        def fn(e, pb=pb, sa=sa, hb=hb):
                            ins = None
                            for c in range(4):
                                ins = e.transpose(out=pb[:, c * 128:(c + 1) * 128],
                                                  in_=sa[:, (hb * 4 + c) * 128:(hb * 4 + c + 1) * 128],
                                                  identity=self.ident)
                            return ins
                        self.S.add("pe", fn, reads=[st, self.tC], writes=[pbt])
                        pv = pb[:, :].rearrange("p (c j) -> p c j", c=4)
                        if 0 <= tt_ - xo < 16:
                            xc = (tt_ - xo) * 128
                            self.cp("dve", X[:, hb * 4:hb * 4 + 4, xc:xc + 128], pv, [pbt], [self.Xt[tt_ - xo]])
                            self.cp("act", xb[:, hb * 4:hb * 4 + 4, i * 128:(i + 1) * 128],
                                    X[:, hb * 4:hb * 4 + 4, xc:xc + 128], [self.Xt[tt_ - xo]], [xbt])
                        else:
                            self.cp("act", xb[:, hb * 4:hb * 4 + 4, i * 128:(i + 1) * 128], pv, [pbt], [xbt])
            else:
                xc = (tiles[0] - xo) * 128
                self.cp("act", xb[:, :, 0:n], X[:, :, xc:xc + n], self.xtiles(kind, tiles[0], tiles[-1] + 1), [xbt])
            self.kv_from_xb(l, xb, xbt, tiles, wvap, wvt)
        if self.stop == f"{l}A":
            return
        wins = split_windows(b0, b1, L["forced"])
        ctx = self.b_pre(kind, l, L, wins[0][0], wins[0][1], True)
        self.b_q(l, ctx)
        for wi, (t0, t1) in enumerate(wins):
            ctx = self.phase_b_window(kind, l, L, ctx, wins[wi + 1] if wi + 1 < len(wins) else None)
        self.barrier(self.ab_tiles, self.c_tiles)
        if self.stop == f"{l}B":
            self.write_out(kind, yout)
            self.barrier(self.c_tiles, self.ab_tiles)
            return
        wins = split_windows(c0_, c1_, L["forced"])
        ctx = self.c_pre(kind, l, L, wins[0][0], wins[0][1], True)
        for wi, (t0, t1) in enumerate(wins):
            ctx = self.phase_c_window(kind, l, L, ctx, wins[wi + 1] if wi + 1 < len(wins) else None)
        if l == 1 or self.stop == f"{l}C":
            self.write_out(kind, yout)
        self.barrier(self.c_tiles, self.ab_tiles)

    def write_out(self, kind, yout):
        X = self.X
        xo = kind["xoff"]
        o0, o1 = kind["out"]
        for t in range(o0, o1):
            xc = (t - xo) * 128
            oa, ot = self.ostage.next()
            for hb in range(2):
                pb, pbt = self.bank()

                def fn(e, pb=pb, hb=hb, xc=xc):
                    ins = None
                    for c in range(4):
                        ins = e.transpose(out=pb[:, c * 128:(c + 1) * 128],
                                          in_=X[:, hb * 4 + c, xc:xc + 128], identity=self.ident)
                    return ins
                self.S.add("pe", fn, reads=[self.Xt[t - xo], self.tC], writes=[pbt])
                self.cp("act" if hb == 0 else "dve", oa[:, hb * 512:(hb + 1) * 512], pb[:, :], [pbt], [ot])
            ev = self.dma(yout[(t - o0) * 128:(t - o0 + 1) * 128, :], oa, [ot], [], ot)
            self.out_events.append(ev)

    def build_window(self, xwa, xwt, kind, t0, t1, first, L, leftx):
        X = self.X
        xo = kind["xoff"]
        n = (t1 - t0) * 128
        xc = (t0 - xo) * 128
        xt = self.xtiles(kind, t0, t1)
        right_ok = (t1 - xo) < 16 and t1 < kind["NT"]
        if right_ok:
            self.cp("act", xwa[:, :, 1:n + 2], X[:, :, xc:xc + n + 1], xt + self.xtiles(kind, t1, t1 + 1), [xwt])
        else:
            self.cp("act", xwa[:, :, 1:n + 1], X[:, :, xc:xc + n], xt, [xwt])
            self.S.add("pool", lambda e: e.memset(xwa[:, :, n + 1:n + 2], 0.0), writes=[xwt])
        if first and leftx and xc > 0:
            self.cp("pool", xwa[:, :, 0:1], X[:, :, xc - 1:xc], self.xtiles(kind, t0 - 1, t0), [xwt])
        elif first:
            self.S.add("pool", lambda e: e.memset(xwa[:, :, 0:1], 0.0), writes=[xwt])
        else:
            self.cp("pool", xwa[:, :, 0:1], self.xsave, [self.xsavet], [xwt])
        return n, xc, xt

    def edge_mode(self, L, t0, t1):
        return L["bl"].get(t0), L["br"].get(t1)

    def b_pre(self, kind, l, L, t0, t1, first):
        tC = self.tC
        xwa, xwt = self.xw.next()
        n, xc, xt = self.build_window(xwa, xwt, kind, t0, t1, first, L, False)
        return dict(xwa=xwa, xwt=xwt, n=n, xc=xc, xt=xt, t0=t0, t1=t1)

    def b_q(self, l, ctx):
        tC = self.tC
        xwa, xwt, n = ctx["xwa"], ctx["xwt"], ctx["n"]
        xm = xwa[:, :, 1:n + 1]
        qa, qt = self.qT.next()
        ctx["qa"], ctx["qt"] = qa, qt
        for m in range(4):
            wa, wt = self.load_w8("win", l, m)
            pb, pbt = self.bank()
            self.mm(pb[:, :n], [(wa[:, k, :], xm[:, k, :]) for k in range(8)], [wt, xwt], pbt)
            for pr in (0, 64):
                self.act(qa[pr:pr + 64, m, pr // 64, :n], pb[pr:pr + 64, :n], AF.Identity, [pbt, tC], [qt],
                         bias=self.PAR[pr:pr + 64, l * NPAR + P_BIN + m:l * NPAR + P_BIN + m + 1])

    def phase_b_window(self, kind, l, L, ctx, nwin):
        X = self.X
        d = self.d
        S = self.S
        tC = self.tC
        xwa, xwt, n, xc, xt, t0, t1 = (ctx[k] for k in ("xwa", "xwt", "n", "xc", "xt", "t0", "t1"))
        qa, qt = ctx["qa"], ctx["qt"]
        el, er = self.edge_mode(L, t0, t1)
        xm = xwa[:, :, 1:n + 1]
        for m in range(0):
            wa, wt = self.load_w8("win", l, m)
            pb, pbt = self.bank()
            self.mm(pb[:, :n], [(wa[:, k, :], xm[:, k, :]) for k in range(8)], [wt, xwt], pbt)
            for pr in (0, 64):
                self.act(qa[pr:pr + 64, m, pr // 64, :n], pb[pr:pr + 64, :n], AF.Identity, [pbt, tC], [qt],
                         bias=self.PAR[pr:pr + 64, l * NPAR + P_BIN + m:l * NPAR + P_BIN + m + 1])
        yTa, yTt = self.yaT.next()
        cua, cut = self.cu.next()
        yca, yct = self.yc.next()
        xh = xwa[:, :, 0:n + 2]
        fillers = []

        def f_ugc(m):
            wa, wt = self.load_w8("win", l, 12 + m)
            pu, put = self.bank()
            self.mm(pu[:, :n + 2], [(wa[:, k, :], xh[:, k, :]) for k in range(8)], [wt, xwt], put)
            wa2, wt2 = self.load_w8("win", l, 20 + m)
            pg, pgt = self.bank()
            self.mm(pg[:, :n + 2], [(wa2[:, k, :], xh[:, k, :]) for k in range(8)], [wt2, xwt], pgt)
            ta, tt_ = self.tu.next()
            self.act(ta[:, :n + 2], pu[:, :n + 2], AF.Identity, [put, tC], [tt_], bias=self.par(l, P_BIN + 12 + m))
            self.stt("dve", cua[:, m, :n + 2], pg[:, :n + 2], self.par(l, P_BIN + 20 + m), ta[:, :n + 2],
                     ALU.add, ALU.mult, [pgt, tt_, tC], [cut])

        def f_edge():
            for mode, col in ((el, 0), (er, n + 1)):
                if mode == "zero":
                    S.add("pool", lambda e, col=col: e.memset(cua[:, :, col:col + 1], 0.0), writes=[cut])
                elif mode in ("ftop", "fbot"):
                    f = self.FLG[:, 0:1] if mode == "ftop" else self.FLG[:, 1:2]
                    for m in range(4):
                        self.ts("dve", cua[:, m, col:col + 1], cua[:, m, col:col + 1], f, None, ALU.mult, None,
                                [cut, tC], [cut])

        def f_conv(m):
            if m == 0:
                f_edge()
            va, vt = self.cv.next()
            w0 = self.par(l, P_SCW + m)
            w1 = self.par(l, P_SCW + 4 + m)
            w2 = self.par(l, P_SCW + 8 + m)
            self.act(va[:, :n], cua[:, m, 1:n + 1], AF.Identity, [cut, tC], [vt], bias=self.par(l, P_SCB + m), scale=w1)
            self.fma("dve", va[:, :n], cua[:, m, 0:n], w0, va[:, :n], [cut, tC, vt], [vt])
            self.fma("dve", va[:, :n], cua[:, m, 2:n + 2], w2, va[:, :n], [cut, tC, vt], [vt])
            wa, wt = self.load_w8("win", l, 16 + m)
            pb, pbt = self.bank()
            self.mm(pb[:, :n], [(wa[:, k, :], xm[:, k, :]) for k in range(8)], [wt, xwt], pbt)
            self.stt("dve", yca[:, m, :n], pb[:, :n], self.par(l, P_BIN + 16 + m), va[:, :n], ALU.add, ALU.mult,
                     [pbt, vt, tC], [yct])
        for m in range(4):
            fillers.append(lambda m=m: f_ugc(m))
        for m in range(4):
            fillers.append(lambda m=m: f_conv(m))
        if ctx.get("ln") is not None:
            fillers.insert(1, ctx["ln"])

        items = []
        pre_edge = None
        for t in range(t0, t1):
            dl = L["q"][t]
            esrc = [s_ for (_, s_) in dl if s_[0] == "e"]
            if esrc and pre_edge is None:
                pre_edge = t
                self.dma(self.Eedge[:, 0:len(esrc), :],
                         d["eb"][l, esrc[0][1]:esrc[0][1] + len(esrc)].rearrange("m p f -> p m f"),
                         [], [self.Eedget], self.Eedget)
            for j, (dd, src) in enumerate(dl):
                for hb in range(2):
                    items.append((t, j, dd, src, len(dl), esrc, hb))
        po = [(self.ps[5], self.pst[5]), (self.ps[6], self.pst[6])]
        self.nbank = 3
        self.pbank = 0

        def emit_scores(it):
            t, j, dd, src, nj, esrc, hb = it
            kc = (t + dd) * 128
            qc = (t - t0) * 128
            self.sbi ^= 1
            pbank, pbt = self.ps[3 + self.sbi], self.pst[3 + self.sbi]

            if src[0] == "i":
                ea, et = self.Eint[:, src[1], hb * 512:(hb + 1) * 512], self.Eintt
            else:
                ea, et = self.Eedge[:, src[1] - esrc[0][1], hb * 512:(hb + 1) * 512], self.Eedget
                if j == 0 and hb == 0 and t != pre_edge:
                    i0 = esrc[0][1]
                    ne = len(esrc)
                    self.dma(self.Eedge[:, 0:ne, :], d["eb"][l, i0:i0 + ne].rearrange("m p f -> p m f"),
                             [], [self.Eedget], self.Eedget)

            def fn(e, hb=hb, kc=kc, qc=qc, pbank=pbank, ea=ea):
                ins = e.matmul(pbank[:, :], lhsT=self.identb, rhs=ea, start=True, stop=False, skip_group_check=True)
                for hh in range(4):
                    h = hb * 4 + hh
                    ins = e.matmul(pbank[:, hh * 128:(hh + 1) * 128],
                                   lhsT=self.kT[:, h // 2, kc:kc + 128],
                                   rhs=qa[:, h // 2, h % 2, qc:qc + 128],
                                   start=False, stop=(hh == 3), skip_group_check=True)
                return ins
            S.add("pe", fn, reads=[self.kTt[t + dd], qt, et, tC], writes=[pbt])
            return pbank, pbt

        def finish_tile(t):
            qc = (t - t0) * 128
            ra, rt = self.rec.next()
            ya, yt = self.ya.next()
            for hb in range(2):
                pv = po[hb][0][:, 0:260].rearrange("p (h d) -> p h d", h=4)
                S.add("dve", lambda e, ra=ra, pv=pv, hb=hb: e.reciprocal(ra[:, hb * 4:hb * 4 + 4].unsqueeze(2), pv[:, :, 64:65]),
                      reads=[po[hb][1]], writes=[rt])
                self.tt("dve", ya[:, hb * 256:(hb + 1) * 256].rearrange("p (h d) -> p h d", h=4), pv[:, :, 0:64],
                        ra[:, hb * 4:hb * 4 + 4].unsqueeze(2).to_broadcast([128, 4, 64]), ALU.mult,
                        [po[hb][1], rt], [yt])
            pbb = self.psb[:, 0:512]

            def fn(e, pbb=pbb, ya=ya):
                ins = None
                for c in range(4):
                    ins = e.transpose(out=pbb[:, c * 128:(c + 1) * 128], in_=ya[:, c * 128:(c + 1) * 128],
                                      identity=self.identb)
                return ins
            S.add("pe", fn, reads=[yt, tC], writes=[self.psbt[0]])
            self.cp("act", yTa[:, :, qc:qc + 128], pbb.rearrange("p (c j) -> p c j", c=4), [self.psbt[0]], [yTt])

        pending = emit_scores(items[0])
        for idx, it in enumerate(items):
            t, j, dd, src, nj, esrc, hb = it
            kt = t + dd
            nxt = emit_scores(items[idx + 1]) if idx + 1 < len(items) else None
            ma, mt = self.Pm.next()
            self.act(ma, pending[0][:, :], AF.Exp, [pending[1]], [mt], scale=0.125)

            def fn(e, hb=hb, kt=kt, ma=ma, j=j, nj=nj, pbank=po[hb][0]):
                ins = None
                for hh in range(4):
                    h = hb * 4 + hh
                    ins = e.matmul(pbank[:, hh * 65:(hh + 1) * 65], lhsT=ma[:, hh * 128:(hh + 1) * 128],
                                   rhs=self.vA[:, kt, h, 0:65], start=(j == 0 and hh == 0), stop=(j == nj - 1),
                                   skip_group_check=True)
                return ins
            S.add("pe", fn, reads=[mt, self.vAt[kt]], writes=[po[hb][1]])
            if fillers and hb == 1:
                fillers.pop(0)()
            if j == nj - 1 and hb == 1:
                finish_tile(t)
            pending = nxt
        self.nbank = 5
        while fillers:
            fillers.pop(0)()
        mga, mgt = self.mrg.next()
        for mo in range(8):
            wa, wt = self.load_w4("wba", l, mo)
            pa_, pat = self.bank()
            self.mm(pa_[:, :n], [(wa[:, k, :], yTa[:, k, :n]) for k in range(4)], [wt, yTt], pat)
            wc, wct = self.load_w4("wbc", l, mo)
            pc_, pct = self.bank()
            self.mm(pc_[:, :n], [(wc[:, k, :], yca[:, k, :n]) for k in range(4)], [wct, yct], pct)
            wg, wgt = self.load_w8("win", l, 24 + mo)
            pga, pgat = self.bank()
            self.mm(pga[:, :n], [(wg[:, k, :], xm[:, k, :]) for k in range(8)], [wgt, xwt], pgat)
            wg2, wgt2 = self.load_w8("win", l, 32 + mo)
            pgc, pgct = self.bank()
            self.mm(pgc[:, :n], [(wg2[:, k, :], xm[:, k, :]) for k in range(8)], [wgt2, xwt], pgct)
            s1, s1t = self.sg.next()
            s2, s2t = self.sg.next()
            self.act(s1[:, :n], pga[:, :n], AF.Sigmoid, [pgat, tC], [s1t], bias=self.par(l, P_BIN + 24 + mo))
            self.act(s2[:, :n], pgc[:, :n], AF.Sigmoid, [pgct, tC], [s2t], bias=self.par(l, P_BIN + 32 + mo))
            u1, u1t = self.t12.next()
            u2, u2t = self.t12.next()
            self.tt("dve", u1[:, :n], s1[:, :n], pa_[:, :n], ALU.mult, [s1t, pat], [u1t])
            self.tt("dve", u2[:, :n], s2[:, :n], pc_[:, :n], ALU.mult, [s2t, pct], [u2t])
            self.tt("pool", mga[:, mo, :n], u1[:, :n], u2[:, :n], ALU.add, [u1t, u2t], [mgt])
        self.cp("pool", self.xsave, X[:, :, xc + n - 1:xc + n], xt, [self.xsavet])
        nctx = None
        if nwin is not None:
            nctx = self.b_pre(kind, l, L, nwin[0], nwin[1], False)
        for mo in range(8):
            wa, wt = self.load_w8("wo", l, mo)
            pb, pbt = self.bank()
            self.mm(pb[:, :n], [(wa[:, k, :], mga[:, k, :n]) for k in range(8)], [wt, mgt], pbt)
            za, zt = self.ztmp.next()
            self.act(za[:, :n], pb[:, :n], AF.Identity, [pbt, tC], [zt], bias=self.par(l, P_BO + mo))
            self.stt("dve", X[:, mo, xc:xc + n], X[:, mo, xc:xc + n], ALPHA, za[:, :n], ALU.mult, ALU.add,
                     xt + [zt], xt)
        ln = lambda: self.layer_norm(l, xc, n, xt, P_G1, P_B1, self.zb, self.zq, self.mean, self.rstd, self.msq)
        if nctx is not None:
            self.b_q(l, nctx)
            nctx["ln"] = ln
        else:
            ln()
        return nctx

    def c_pre(self, kind, l, L, t0, t1, first):
        tC = self.tC
        xwa, xwt = self.xwc.next()
        n, xc, xt = self.build_window(xwa, xwt, kind, t0, t1, first, L, t0 > L["B"][0])
        el, er = self.edge_mode(L, t0, t1)
        for mode, col in ((el, 0), (er, n + 1)):
            if mode in ("ftop", "fbot"):
                f = self.FLG[:, 0:1] if mode == "ftop" else self.FLG[:, 1:2]
                self.ts("dve", xwa[:, :, col:col + 1], xwa[:, :, col:col + 1], f, None, ALU.mult, None,
                        [xwt, tC], [xwt])
        ga, gt = self.gT.next()
        return dict(xwa=xwa, xwt=xwt, n=n, xc=xc, xt=xt, el=el, er=er, ga=ga, gt=gt, m=0)

    def c_iter(self, l, ctx, count):
        tC = self.tC
        DER = self.DER
        xwa, xwt, n, el, er, ga, gt = (ctx[k] for k in ("xwa", "xwt", "n", "el", "er", "ga", "gt"))
        xh = xwa[:, :, 0:n + 2]
        for m in range(ctx["m"], min(22, ctx["m"] + count)):
            wg, wgt = self.load_w8("wup", l, m)
            wv_, wvt_ = self.load_w8("wup", l, 22 + m)
            pg, pgt = self.bank()
            self.mm(pg[:, :n + 2], [(wg[:, k, :], xh[:, k, :]) for k in range(8)], [wgt, xwt], pgt)
            pv, pvt = self.bank()
            self.mm(pv[:, :n + 2], [(wv_[:, k, :], xh[:, k, :]) for k in range(8)], [wvt_, xwt], pvt)
            accs = []
            for (pp, ppt, ring, mc) in ((pg, pgt, self.accg, m), (pv, pvt, self.accv, 22 + m)):
                aa, at = ring.next()
                w0 = self.par(l, P_FCW + mc)
                w1 = self.par(l, P_FCW + 44 + mc)
                w2 = self.par(l, P_FCW + 88 + mc)
                self.act(aa[:, :n], pp[:, 1:n + 1], AF.Identity, [ppt, tC], [at], bias=DER[:, l, 0, mc:mc + 1], scale=w1)
                self.stt("dve", aa[:, :n], pp[:, 0:n], w0, aa[:, :n], ALU.mult, ALU.add, [ppt, at, tC], [at])
                self.stt("dve", aa[:, :n], pp[:, 2:n + 2], w2, aa[:, :n], ALU.mult, ALU.add, [ppt, at, tC], [at])
                if el is not None:
                    j = 1 if el == "zero" else 3
                    self.tt("pool", aa[:, 0:1], aa[:, 0:1], DER[:, l, j, mc:mc + 1], ALU.subtract, [at, tC], [at])
                if er is not None:
                    j = 2 if er == "zero" else 4
                    self.tt("pool", aa[:, n - 1:n], aa[:, n - 1:n], DER[:, l, j, mc:mc + 1], ALU.subtract, [at, tC], [at])
                accs.append((aa, at))
            la, lt = self.gl.next()
            self.act(la[:, :n], accs[0][0][:, :n], AF.Gelu_apprx_tanh, [accs[0][1]], [lt])
            self.tt("pool", ga[:, m, :n], la[:, :n], accs[1][0][:, :n], ALU.mult, [lt, accs[1][1]], [gt])
        ctx["m"] = min(22, ctx["m"] + count)

    def phase_c_window(self, kind, l, L, ctx, nxt):
        X = self.X
        tC = self.tC
        n, xc, xt, ga, gt = (ctx[k] for k in ("n", "xc", "xt", "ga", "gt"))
        self.c_iter(l, ctx, 22)
        self.cp("pool", self.xsave, X[:, :, xc + n - 1:xc + n], xt, [self.xsavet])
        nctx = None
        if nxt is not None:
            nctx = self.c_pre(kind, l, L, nxt[0], nxt[1], False)
        for mo in range(8):
            wa, wt = self.wdn.next()
            self.dma(wa.rearrange("p k j -> p (k j)"), self.d["wdnb"][l, mo], [], [wt], wt)
            pb, pbt = self.bank()
            self.mm(pb[:, :n], [(wa[:, k, :], ga[:, k, :n]) for k in range(22)], [wt, gt], pbt)
            za, zt = self.ztmp2.next()
            self.act(za[:, :n], pb[:, :n], AF.Identity, [pbt, tC], [zt], bias=self.par(l, P_BDN + mo))
            self.stt("dve", X[:, mo, xc:xc + n], X[:, mo, xc:xc + n], ALPHA, za[:, :n], ALU.mult, ALU.add,
                     xt + [zt], xt)
        if nctx is not None:
            self.c_iter(l, nctx, 3)
            pieces = self.layer_norm(l, xc, n, xt, P_G2, P_B2, self.zb2, self.zq2, self.mean2, self.rstd2,
                                     self.msq2, split=True)
            for p in pieces:
                self.c_iter(l, nctx, 1)
                p()
        else:
            self.layer_norm(l, xc, n, xt, P_G2, P_B2, self.zb2, self.zq2, self.mean2, self.rstd2, self.msq2)
        return nctx

    def build(self):
        nc = self.nc
        self.d["identin"] = nc.dram_tensor("identin", [128, 128], F32, kind="ExternalInput").ap()
        self.setup()
        self.prepass()
        self.barrier(self.c_tiles, self.ab_tiles)
        sk = sample_kind()
        for i in range(self.NS if self.stop != 'P' else 0):
            self.chunk(sk, self.d["xs"][i], self.d["ys"][i])
        for j in range(self.NP):
            self.chunk(prompt_kind(j), self.d["xp"][j], self.d["yp"][j])
        self.S.add("sp", lambda e: e.nop(nofuse=True), extra=self.out_events)
        with ExitStack() as stack:
            self.S.emit(nc, stack)
        return nc


class Ring:
    def __init__(self, items):
        self.items = items
        self.i = 0

    def next(self):
        it = self.items[self.i]
        self.i = (self.i + 1) % len(self.items)
        return it


def wlayout(w):
    K, M = w.shape
    return np.ascontiguousarray(w.reshape(K // 128, 128, M // 128, 128).transpose(2, 1, 0, 3)).reshape(
        M // 128, 128, K)


def col(v):
    return np.ascontiguousarray(v.reshape(-1, 128).T)


def bm_tile(rpb_l, i0, r0, R):
    out = np.full((128, 8, 128), NEG, np.float32)
    kc = np.arange(64)[:, None]
    qc = np.arange(64)[None, :]
    js = np.clip(qc - 8, 0, 48)
    valid = (kc >= js) & (kc < js + 16)
    dc = np.clip(kc - qc + 15, 0, 30)
    for kr in range(2):
        for qr in range(2):
            r = r0 + kr
            i = i0 + qr
            if i < 0 or i >= R:
                continue
            rs = min(max(i - 4, 0), R - 8)
            if not (rs <= r < rs + 8):
                continue
            g = rpb_l[:, r - i + 7][:, dc]
            blk = np.where(valid[None], g, np.float32(NEG))
            out[kr * 64:(kr + 1) * 64, :, qr * 64:(qr + 1) * 64] = blk.transpose(1, 0, 2)
    return out.reshape(128, 1024)


def bm_set(rpb_l, half):
    tiles = []
    big = 1000
    def interior(d):
        return bm_tile(rpb_l, 500, 500 + 2 * d, big)
    for d in range(-2, 3):
        tiles.append(interior(d))
    for t, ds in ((0, range(0, 4)), (1, range(-1, 3)), (14, range(-2, 2)), (15, range(-3, 1))):
        for d in ds:
            tiles.append(bm_tile(rpb_l, 2 * t, 2 * (t + d), 32))
    for t, ds in ((5, range(-2, 4)), (6, range(-2, 3))):
        for d in ds:
            if half == 0:
                tiles.append(bm_tile(rpb_l, 2 * (t - 5), 2 * (t - 5 + d), 128))
            else:
                tiles.append(interior(d))
    for t, ds in ((11, range(-2, 3)), (12, range(-3, 3))):
        for d in ds:
            if half == 1:
                tiles.append(bm_tile(rpb_l, 112 + 2 * (t - 5), 112 + 2 * (t - 5 + d), 128))
            else:
                tiles.append(interior(d))
    assert len(tiles) == NBM
    return np.stack(tiles)


def host_params(inp):
    pars = []
    for l in range(2):
        cols = [col(inp["b_in"][l]),
                np.ascontiguousarray(inp["sc_conv_w"][l].reshape(3, 4, 128).transpose(2, 0, 1)).reshape(128, 12),
                col(inp["sc_conv_b"][l]), col(inp["b_o"][l]), col(inp["ln1_g"][l]), col(inp["ln1_b"][l]),
                col(inp["ffn_b_up"][l]),
                np.ascontiguousarray(inp["ffn_conv_w"][l].reshape(3, 44, 128).transpose(2, 0, 1)).reshape(128, 132),
                col(inp["ffn_conv_b"][l]), col(inp["ffn_b_down"][l]), col(inp["ln2_g"][l]), col(inp["ln2_b"][l])]
        p = np.concatenate(cols, axis=1)
        assert p.shape == (128, NPAR)
        pars.append(p)
    par = np.ascontiguousarray(np.concatenate(pars, axis=1), dtype=np.float32)
    bv = np.ascontiguousarray(np.broadcast_to(
        np.stack([inp["b_in"][l][1024:1536] for l in range(2)]).reshape(1, 1024), (128, 1024)), dtype=np.float32)
    return par, bv


def host_weights(inp):
    out = {}
    for key, name in (("win", "w_in"), ("wba", "w_br_attn"), ("wbc", "w_br_conv"), ("wo", "w_o"),
                      ("wup", "ffn_w_up"), ("wdn", "ffn_w_down")):
        out[key] = np.stack([wlayout(np.asarray(inp[name][l], np.float32)) for l in range(2)])
    return out


_CACHE = {}


def get_nc(NS, NP):
    k = (NS, NP)
    if k not in _CACHE:
        _CACHE[k] = Builder(NS, NP).build()
    return _CACHE[k]


def kernel(**inputs):
    inp = {k: np.asarray(v) for k, v in inputs.items()}
    xp_full = inp["x_prompt"]
    xs_full = inp["x_sample"]
    n = 8
    nc = get_nc(4, 4)
    par, bv = host_params(inp)
    W = host_weights(inp)
    ident = np.eye(128, dtype=np.float32)
    bms = [np.stack([bm_set(inp["attn_rpb"][l], h) for l in range(2)]) for h in range(2)]
    in_maps = []
    for c in range(n):
        p, half = c // 2, c % 2
        xpc = np.zeros((4, 17 * 128, D), np.float32)
        for j in range(4):
            r0 = 64 * half + 16 * j
            lo, hi = r0 - 10, r0 + 24
            slo, shi = max(lo, 0), min(hi, 128)
            xpc[j, (slo - lo) * 64:(shi - lo) * 64] = xp_full[p, slo * 64:shi * 64]
        ftop = 0.0 if half == 0 else 1.0
        fbot = 0.0 if half == 1 else 1.0
        flg = np.ascontiguousarray(np.broadcast_to(np.array([ftop, fbot, 1 - ftop, 1 - fbot], np.float32), (128, 4)))
        m = dict(xs=np.ascontiguousarray(xs_full[4 * c:4 * c + 4]), xp=xpc, bm=bms[half], par=par, bv=bv,
                 flg=flg, identin=ident)
        m.update(W)
        in_maps.append(m)
    res = run_bass_kernel_spmd(nc, in_maps, core_ids=list(range(n)))
    y_prompt = np.empty_like(xp_full)
    y_sample = np.empty_like(xs_full)
    for c in range(n):
        r = res.results[c]
        p, half = c // 2, c % 2
        y_sample[4 * c:4 * c + 4] = r["ys"]
        for j in range(4):
            r0 = 64 * half + 16 * j
            y_prompt[p, r0 * 64:(r0 + 16) * 64] = r["yp"][j]
    return (y_prompt, y_sample)
```
